# Optimizing a Trainium2 kernel written in Bass

```python
import jax, jax.numpy as jnp
from jax import lax
import numpy as np

D_MODEL = 1024
BATCH = 8
SEQ = 4096
DEPTH = 1

CHUNK = 64
D_MIX = D_MODEL
HEAD_DIM = 64
SGU_HEADS = 8
SGU_WIDTH = SGU_HEADS * HEAD_DIM
SGU_BLOCK = 128
FOX_HEADS = 8
FOX_WIDTH = FOX_HEADS * HEAD_DIM
Q_BLOCK = 128
D_FF = 2816
EPS = 1e-6
N_OUT_GROUPS = SGU_HEADS + FOX_HEADS
SPLITS = (SGU_WIDTH, 2 * SGU_WIDTH, 2 * SGU_WIDTH + FOX_WIDTH,
          2 * SGU_WIDTH + 2 * FOX_WIDTH, 2 * SGU_WIDTH + 3 * FOX_WIDTH)
IN_COLS = 2 * SGU_WIDTH + 3 * FOX_WIDTH + FOX_HEADS

kernel_name = "hybrid_sgu_fox_macaron_block"


def rmsnorm(x, g):
    xf = x.astype(jnp.float32)
    y = xf * lax.rsqrt(jnp.mean(xf * xf, axis=-1, keepdims=True) + EPS)
    return (y * g.astype(jnp.float32)).astype(x.dtype)


def layernorm(x, g, b):
    xf = x.astype(jnp.float32)
    mu = jnp.mean(xf, axis=-1, keepdims=True)
    xc = xf - mu
    y = xc * lax.rsqrt(jnp.mean(xc * xc, axis=-1, keepdims=True) + EPS)
    return (y * g.astype(jnp.float32) + b.astype(jnp.float32)).astype(x.dtype)


def swiglu(h, w1, w3, w2):
    return (jax.nn.silu(h @ w1) * (h @ w3)) @ w2


def spatial_gating(u, v, ln_g, ln_b, w_s, b_s):
    B, S, _ = v.shape
    u = jax.nn.gelu(u)
    v = layernorm(jax.nn.gelu(v), ln_g, ln_b)
    nb = S // SGU_BLOCK
    vb = v.reshape(B, nb, SGU_BLOCK, SGU_HEADS, HEAD_DIM)
    chunk_id = jnp.arange(SGU_BLOCK) // CHUNK
    mask = chunk_id[:, None] >= chunk_id[None, :]
    ws = jnp.where(mask[None], w_s, jnp.zeros_like(w_s))
    mixed = jnp.einsum('hts,bnshc->bnthc', ws.astype(v.dtype), vb)
    mixed = mixed + b_s.T.astype(v.dtype)[None, None, :, :, None]
    return u * mixed.reshape(B, S, SGU_WIDTH)


def forgetting_attention(q, k, v, f_logit, b_f):
    B, S, _ = q.shape
    to_heads = lambda t: t.reshape(B, S, FOX_HEADS, HEAD_DIM).transpose(0, 2, 1, 3)
    q, k, v = to_heads(q), to_heads(k), to_heads(v)
    log_f = jax.nn.log_sigmoid(f_logit.astype(jnp.float32) + b_f.astype(jnp.float32))
    F = jnp.cumsum(log_f, axis=1).transpose(0, 2, 1)
    nq = S // Q_BLOCK
    qb = q.reshape(B, FOX_HEADS, nq, Q_BLOCK, HEAD_DIM).transpose(2, 0, 1, 3, 4)
    Fq = F.reshape(B, FOX_HEADS, nq, Q_BLOCK).transpose(2, 0, 1, 3)
    key_pos = jnp.arange(S)
    scale = HEAD_DIM ** -0.5

    def block(args):
        qi, Fi, i = args
        logits = jnp.einsum('bhqd,bhkd->bhqk', qi, k).astype(jnp.float32) * scale
        logits = logits + Fi[..., None] - F[:, :, None, :]
        qpos = i * Q_BLOCK + jnp.arange(Q_BLOCK)
        allowed = key_pos[None, :] <= qpos[:, None]
        logits = jnp.where(allowed, logits, -jnp.inf)
        p = jax.nn.softmax(logits, axis=-1)
        return jnp.einsum('bhqk,bhkd->bhqd', p.astype(v.dtype), v)

    out = lax.map(block, (qb, Fq, jnp.arange(nq)))
    return out.transpose(1, 0, 3, 2, 4).reshape(B, S, FOX_WIDTH)


def hybrid_mixer(h, w_in, b_f, ln_g, ln_b, w_s, b_s, out_g, w_out):
    B, S, _ = h.shape
    z = h @ w_in
    u, v_s, q, k, v_a, f_logit = jnp.split(z, list(SPLITS), axis=-1)
    y_a = spatial_gating(u, v_s, ln_g, ln_b, w_s, b_s)
    y_b = forgetting_attention(q, k, v_a, f_logit, b_f)
    y = jnp.concatenate([y_a, y_b], axis=-1).reshape(B, S, N_OUT_GROUPS, HEAD_DIM)
    y = rmsnorm(y, jnp.ones((HEAD_DIM,), jnp.float32)).reshape(B, S, D_MIX) * out_g.astype(h.dtype)
    return y @ w_out


def setup_inputs(seed: int = 0) -> dict:
    key = jax.random.key(seed)
    ks = jax.random.split(key, 24)
    n = lambda k, shape, s: jax.random.normal(k, shape, jnp.float32) * s
    L = DEPTH
    return {
        "x": jax.random.normal(ks[0], (BATCH, SEQ, D_MODEL), jnp.float32),
        "ffn1_norm_g": 1.0 + n(ks[1], (L, D_MODEL), 0.02),
        "ffn1_w1": n(ks[2], (L, D_MODEL, D_FF), D_MODEL ** -0.5),
        "ffn1_w3": n(ks[3], (L, D_MODEL, D_FF), D_MODEL ** -0.5),
        "ffn1_w2": n(ks[4], (L, D_FF, D_MODEL), D_FF ** -0.5),
        "mix_norm_g": 1.0 + n(ks[5], (L, D_MODEL), 0.02),
        "w_in": n(ks[6], (L, D_MODEL, IN_COLS), D_MODEL ** -0.5),
        "fox_f_bias": 2.0 + n(ks[7], (L, FOX_HEADS), 0.1),
        "sgu_ln_g": 1.0 + n(ks[8], (L, SGU_WIDTH), 0.02),
        "sgu_ln_b": n(ks[9], (L, SGU_WIDTH), 0.02),
        "sgu_w_s": n(ks[10], (L, SGU_HEADS, SGU_BLOCK, SGU_BLOCK), SGU_BLOCK ** -0.5),
        "sgu_b_s": 1.0 + n(ks[11], (L, SGU_HEADS, SGU_BLOCK), 0.1),
        "mix_out_g": 1.0 + n(ks[12], (L, D_MIX), 0.02),
        "w_out": n(ks[13], (L, D_MIX, D_MODEL), D_MIX ** -0.5),
        "ffn2_norm_g": 1.0 + n(ks[14], (L, D_MODEL), 0.02),
        "ffn2_w1": n(ks[15], (L, D_MODEL, D_FF), D_MODEL ** -0.5),
        "ffn2_w3": n(ks[16], (L, D_MODEL, D_FF), D_MODEL ** -0.5),
        "ffn2_w2": n(ks[17], (L, D_FF, D_MODEL), D_FF ** -0.5),
        "final_norm_g": 1.0 + n(ks[18], (D_MODEL,), 0.02),
    }


def reference(x, ffn1_norm_g, ffn1_w1, ffn1_w3, ffn1_w2, mix_norm_g, w_in, fox_f_bias,
              sgu_ln_g, sgu_ln_b, sgu_w_s, sgu_b_s, mix_out_g, w_out,
              ffn2_norm_g, ffn2_w1, ffn2_w3, ffn2_w2, final_norm_g):
    for l in range(DEPTH):
        x = x + 0.5 * swiglu(rmsnorm(x, ffn1_norm_g[l]), ffn1_w1[l], ffn1_w3[l], ffn1_w2[l])
        x = x + hybrid_mixer(rmsnorm(x, mix_norm_g[l]), w_in[l], fox_f_bias[l], sgu_ln_g[l],
                             sgu_ln_b[l], sgu_w_s[l], sgu_b_s[l], mix_out_g[l], w_out[l])
        x = x + 0.5 * swiglu(rmsnorm(x, ffn2_norm_g[l]), ffn2_w1[l], ffn2_w3[l], ffn2_w2[l])
    return rmsnorm(x, final_norm_g)
```

```python
import numpy as np
from contextlib import ExitStack
import concourse.bass as bass
import concourse.mybir as mybir
from concourse.bass_utils import run_bass_kernel_spmd

F32 = mybir.dt.float32
BF16 = mybir.dt.bfloat16
AF = mybir.ActivationFunctionType
ALU = mybir.AluOpType
AX = mybir.AxisListType

D = 1024
DFF = 2816
NFF = 22
EPS = 1e-6
INC = 2568
GC1 = 0.044715
GC2 = 1.5957691216057308
ENGS = ("pe", "act", "dve", "pool", "sp")


class Buf:
    def __init__(self):
        self.w = {}
        self.r = {}
        self.tw = 0.0
        self.tr = 0.0
        self.weng = None


def _merge(d, tok):
    k, v = tok
    if d.get(k, 0) < v:
        d[k] = v


class Prog:
    def __init__(self, nc, es):
        self.nc = nc
        self.sem = {e: es.enter_context(nc.semaphore("c_" + e)) for e in ENGS}
        self.cnt = {e: 0 for e in ENGS}
        self.seen = {e: {} for e in ENGS}
        self.dsem = {}
        self.es = es
        self.q = {e: [] for e in ENGS}

    def _semobj(self, key):
        return self.sem[key] if key in self.sem else self.dsem[key][0]

    def _deps(self, eng, rd, wr, extra):
        need = {}
        for b in rd:
            for k, v in b.w.items():
                _merge(need, (k, v))
        for b in wr:
            for k, v in b.w.items():
                _merge(need, (k, v))
            for k, v in b.r.items():
                _merge(need, (k, v))
        for t in extra:
            if t is not None:
                _merge(need, t)
        waits = []
        for k, v in need.items():
            if k == eng and eng == "pe":
                continue
            if self.seen[eng].get(k, 0) >= v:
                continue
            self.seen[eng][k] = v
            waits.append((k, v))
        return waits

    def _commit(self, tok, rd, wr):
        for b in rd:
            _merge(b.r, tok)
        for b in wr:
            b.w = {tok[0]: tok[1]}
            b.r = {}

    rec = None

    def record(self, fn, *a):
        self.rec = []
        fn(*a)
        L, self.rec = self.rec, None
        return L

    COST = {"pe": 0.22, "act": 0.7, "dve": 0.62, "pool": 1.0, "sp": 0.05}
    HOP = 0.3

    def play(self, lists):
        idx = [0] * len(lists)
        tf = self.__dict__.setdefault("tfree", {})
        while True:
            best = None
            for li, L in enumerate(lists):
                if idx[li] >= len(L):
                    continue
                kind, a, kw = L[idx[li]]
                eng = a[0]
                t = tf.get(eng, 0.0)
                for b in kw["rd"]:
                    t = max(t, b.tw + (self.HOP if b.weng != eng else 0.05))
                for b in kw["wr"]:
                    t = max(t, b.tw + (self.HOP if b.weng != eng else 0.05), b.tr + self.HOP)
                tb = kw.get("tbl")
                if tb is not None and tb != self.__dict__.get("cur_tbl"):
                    t += 1.3
                if best is None or t < best[0] - 1e-9:
                    best = (t, li)
            if best is None:
                break
            t, li = best
            kind, a, kw = lists[li][idx[li]]
            idx[li] += 1
            eng = a[0]
            cost = kw.pop("cost", None)
            tb = kw.pop("tbl", None)
            if tb is not None:
                self.cur_tbl = tb
            if kind == "op":
                fns = a[1]
                nf = len(fns) if isinstance(fns, (list, tuple)) else 1
                dur = cost if cost is not None else (self.COST[eng] * nf + (0.06 if eng == "pe" else 0.0))
                end = t + dur
                tf[eng] = end
                self.op(*a, **kw)
            else:
                tf[eng] = t + 0.05
                end = t + 2.5
                self.dma(*a, **kw)
            for b in kw["rd"]:
                b.tr = max(b.tr, end)
            for b in kw["wr"]:
                b.tw = end
                b.tr = 0.0
                b.weng = eng if kind == "op" else "dma"

    def op(self, eng, fns, rd=(), wr=(), extra=(), cost=None):
        if self.rec is not None:
            self.rec.append(("op", (eng, fns), dict(rd=list(rd), wr=list(wr), extra=list(extra), cost=cost,
                                                    tbl=getattr(self, "_tbl", None))))
            return None
        if not isinstance(fns, (list, tuple)):
            fns = [fns]
        waits = self._deps(eng, rd, wr, extra)
        self.cnt[eng] += 1
        tok = (eng, self.cnt[eng])
        self.q[eng].append((waits, list(fns), ("eng", eng)))
        self._commit(tok, rd, wr)
        return tok

    def dma(self, eng, semname, out, in_, rd=(), wr=(), extra=(), part=False):
        if self.rec is not None:
            self.rec.append(("dma", (eng, semname, out, in_), dict(rd=list(rd), wr=list(wr), extra=list(extra), part=part)))
            return None
        if semname not in self.dsem:
            self.dsem[semname] = [self.es.enter_context(self.nc.semaphore("d_" + semname)), 0]
        if part:
            ex = list(extra)
            for b in wr:
                ex.extend(b.r.items())
                ex.extend((k, v) for k, v in b.w.items() if k != semname)
            waits = self._deps(eng, rd, (), ex)
        else:
            waits = self._deps(eng, rd, wr, extra)
        s = self.dsem[semname]
        s[1] += 16
        tok = (semname, s[1])
        self.q[eng].append((waits, [lambda e: e.dma_start(out=out, in_=in_)], ("dma", semname)))
        if part:
            for b in rd:
                _merge(b.r, tok)
            for b in wr:
                _merge(b.w, tok)
                b.r = {}
        else:
            self._commit(tok, rd, wr)
        return tok

    TBL = {AF.Sigmoid: "sig", AF.Sqrt: "sqrt", AF.Exp: "exp", AF.Ln: "exp", AF.Gelu_apprx_tanh: "gelu", AF.Silu: "silu"}

    def act(self, out, in_, func, rd=(), wr=(), extra=(), cost=None, **kw):
        self._tbl = self.TBL.get(func)
        r = self.op("act", lambda e: e.activation(out=out, in_=in_, func=func, **kw), rd, wr, extra, cost)
        self._tbl = None
        return r

    def tt(self, eng, out, in0, in1, op, rd=(), wr=(), extra=(), cost=None):
        return self.op(eng, lambda e: e.tensor_tensor(out=out, in0=in0, in1=in1, op=op), rd, wr, extra, cost)

    def ts(self, eng, out, in0, s1, s2, op0, op1=None, rd=(), wr=(), extra=(), cost=None):
        if op1 is None:
            return self.op(eng, lambda e: e.tensor_scalar(out=out, in0=in0, scalar1=s1, scalar2=None, op0=op0), rd, wr, extra, cost)
        return self.op(eng, lambda e: e.tensor_scalar(out=out, in0=in0, scalar1=s1, scalar2=s2, op0=op0, op1=op1), rd, wr, extra, cost)

    def stt(self, out, in0, scalar, in1, op0, op1, rd=(), wr=(), extra=(), cost=None, **kw):
        return self.op("dve", lambda e: e.scalar_tensor_tensor(out=out, in0=in0, scalar=scalar, in1=in1, op0=op0, op1=op1, **kw), rd, wr, extra, cost)

    def copy(self, eng, out, in_, rd=(), wr=(), extra=()):
        return self.op(eng, lambda e: e.tensor_copy(out=out, in_=in_), rd, wr, extra)

    def recip(self, out, in_, rd=(), wr=(), extra=(), cost=0.2):
        return self.op("dve", lambda e: e.reciprocal(out=out, in_=in_), rd, wr, extra, cost)

    def memset(self, eng, ap, val, rd=(), wr=(), extra=()):
        return self.op(eng, lambda e: e.memset(ap, val), rd, wr, extra)

    def mms(self, specs, rd=(), wr=(), extra=()):
        fns = []
        for (o, l, r, st, sp) in specs:
            fns.append((lambda o, l, r, st, sp: (lambda e: e.matmul(o, lhsT=l, rhs=r, start=st, stop=sp)))(o, l, r, st, sp))
        return self.op("pe", fns, rd, wr, extra)

    def trs(self, specs, ident, rd=(), wr=(), extra=()):
        fns = []
        for (o, i) in specs:
            fns.append((lambda o, i: (lambda e: e.transpose(out=o, in_=i, identity=ident)))(o, i))
        return self.op("pe", fns, rd, wr, extra)

    def run_phase(self, name):
        fin = [(k, s[1]) for k, s in self.dsem.items() if s[1] > 0]
        self.op("sp", lambda e: e.nop(), extra=fin)
        q = self.q
        self.q = {e: [] for e in ENGS}

        def replay(eng, e):
            for waits, fns, inc in q[eng]:
                for (k, v) in waits:
                    e.wait_ge(self._semobj(k), v)
                ins = None
                for f in fns:
                    ins = f(e)
                if inc[0] == "eng":
                    ins.then_inc(self.sem[eng], 1)
                else:
                    ins.then_inc(self.dsem[inc[1]][0], 16)

        with self.nc.Block() as block:
            @block.tensor
            def _(e):
                replay("pe", e)

            @block.scalar
            def _(e):
                replay("act", e)

            @block.vector
            def _(e):
                replay("dve", e)

            @block.gpsimd
            def _(e):
                replay("pool", e)

            @block.sync
            def _(e):
                replay("sp", e)


class NormT:
    def __init__(self, P, sb, ps, src, gd, identd, NSUB, nh):
        self.P = P
        self.src = src
        self.xin = [sb("xin%d" % i, [128, D], F32) for i in range(2)]
        self.hbf = [sb("hbf%d" % i, [128, D], BF16) for i in range(2)]
        self.ssq = sb("ssq", [128, NSUB], F32)
        self.std = sb("std", [128, NSUB], F32)
        self.rstd = sb("rstd", [128, NSUB], F32)
        self.hT = [sb("hT%d" % i, [128, 8, 512], BF16) for i in range(nh)]
        self.tp = [ps("tp%d" % i, [128, 8, 128], BF16) for i in range(2)]
        self.ident = sb("ident", [128, 128], BF16)
        self.gfm = sb("gfm", [128, 8], F32)
        self.b_xin = [Buf(), Buf()]
        self.b_hbf = [Buf(), Buf()]
        self.b_tp = [Buf(), Buf()]
        self.b_hT = [Buf() for _ in range(nh)]
        self.b_stc = [Buf(), Buf()]
        self.b_c = Buf()
        P.dma("sp", "cst", out=self.gfm[:, :], in_=gd, wr=[self.b_c], part=True)
        P.dma("pool", "cstp", out=self.ident[:, :], in_=identd, wr=[self.b_c], part=True)

    def norm(self, g):
        P = self.P
        sl = g % 2
        xin, hbf = self.xin[sl], self.hbf[sl]
        P.dma("sp", "xin%d" % sl, out=xin[:, :], in_=self.src[g * 128:(g + 1) * 128, :], wr=[self.b_xin[sl]])
        P.act(hbf[:, :], xin[:, :], AF.Square, rd=[self.b_xin[sl]], wr=[self.b_hbf[sl]], accum_out=self.ssq[:, g:g + 1])
        bst = [self.b_stc[sl]]
        P.act(self.std[:, g:g + 1], self.ssq[:, g:g + 1], AF.Sqrt, rd=[self.b_hbf[sl], self.b_c], wr=bst, scale=1.0 / D, bias=self.eps[:, 0:1], cost=0.3)
        P.recip(self.rstd[:, g:g + 1], self.std[:, g:g + 1], wr=bst)
        P.ts("dve", hbf[:, :], xin[:, :], self.rstd[:, g:g + 1], None, ALU.mult, rd=[self.b_xin[sl]] + bst, wr=[self.b_hbf[sl]])

    def transp(self, g, hi):
        P = self.P
        sl = g % 2
        s = g % 4
        hbf, tp = self.hbf[sl], self.tp[sl]
        P.trs([(tp[:, k, :], hbf[:, k * 128:(k + 1) * 128]) for k in range(8)], self.ident[:, :],
              rd=[self.b_hbf[sl], self.b_c], wr=[self.b_tp[sl]])
        P.tt("dve", self.hT[hi][:, :, s * 128:(s + 1) * 128], tp[:, :, :],
             self.gfm[:, :].unsqueeze(2).to_broadcast([128, 8, 128]), ALU.mult,
             rd=[self.b_tp[sl], self.b_c], wr=[self.b_hT[hi]])


def load_cast(P, sem, dst, srcd, nk, ncol, wr):
    npiece = (ncol + 2047) // 2048
    while ncol % npiece:
        npiece += 1
    w = ncol // npiece
    for k in range(nk):
        for p in range(npiece):
            P.dma("pool", sem, out=dst[:, k, p * w:(p + 1) * w], in_=srcd[:, k, p * w:(p + 1) * w], wr=wr, part=True)


def ffn_phase(P, S, src, dst, gd, w1d, w3d, w2d, identd, epsd, fgd, ph):
    nc = P.nc
    NT = S // 512
    NSUB = S // 128
    final = fgd is not None
    with ExitStack() as es:
        def sb(name, shape, dt):
            return es.enter_context(nc.sbuf_tensor("%s_%s" % (ph, name), shape, dt))

        def ps(name, shape, dt):
            return es.enter_context(nc.psum_tensor("%s_%s" % (ph, name), shape, dt))

        W1 = sb("W1", [128, 8, DFF], BF16)
        W3 = sb("W3", [128, 8, DFF], BF16)
        W2 = sb("W2", [128, NFF, D], BF16)
        N = NormT(P, sb, ps, src, gd, identd, NSUB, 1)
        N.eps = sb("eps", [128, 1], F32)
        P.dma("sp", "cst", out=N.eps[:, :], in_=epsd, wr=[N.b_c], part=True)
        sil = [sb("sil%d" % i, [128, 512], F32) for i in range(2)]
        aT = sb("aT", [128, NFF, 512], BF16)
        NX = 3 if final else 2
        xres = [sb("xres%d" % i, [128, D], F32) for i in range(NX)]
        if final:
            fg = sb("fg", [128, D], F32)
            junk2 = sb("junk2", [128, D], BF16)
            ssq2 = sb("ssq2", [128, NSUB], F32)
            std2 = sb("std2", [128, NSUB], F32)
            rstd2 = sb("rstd2", [128, NSUB], F32)
            P.dma("sp", "cst", out=fg[:, :], in_=fgd, wr=[N.b_c], part=True)
        pa = [ps("pa%d" % i, [128, 512], F32) for i in range(2)]
        pb = [ps("pb%d" % i, [128, 512], F32) for i in range(2)]
        py = [ps("py%d" % i, [128, 512], F32) for i in range(2)]
        b_w2 = Buf()
        b_pa, b_pb, b_py = [Buf(), Buf()], [Buf(), Buf()], [Buf(), Buf()]
        b_sil = [Buf(), Buf()]
        b_aT = [Buf() for _ in range(NFF)]
        b_xres = [Buf() for _ in range(NX)]
        b_j2 = Buf()

        HW_ = DFF // 2
        b_wg = [Buf(), Buf()]
        for gi in range(2):
            for Wt, wd in ((W1, w1d), (W3, w3d)):
                for k in range(8):
                    P.dma("pool", "w13%d" % gi, out=Wt[:, k, gi * HW_:(gi + 1) * HW_], in_=wd[:, k, gi * HW_:(gi + 1) * HW_],
                          wr=[b_wg[gi]], part=True)
        load_cast(P, "w2", W2, w2d, NFF, D, [b_w2])

        def stage1(i):
            hT = N.hT[0]
            for f in range(NFF):
                sl = f % 2
                bw = b_wg[0] if f < NFF // 2 else b_wg[1]
                P.mms([(pa[sl][:, :], W1[:, k, f * 128:(f + 1) * 128], hT[:, k, :], k == 0, k == 7) for k in range(8)],
                      rd=[N.b_hT[0], bw], wr=[b_pa[sl]])
                P.mms([(pb[sl][:, :], W3[:, k, f * 128:(f + 1) * 128], hT[:, k, :], k == 0, k == 7) for k in range(8)],
                      rd=[N.b_hT[0], bw], wr=[b_pb[sl]])
                P.act(sil[sl][:, :], pa[sl][:, :], AF.Silu, rd=[b_pa[sl]], wr=[b_sil[sl]])
                P.tt("dve", aT[:, f, :], sil[sl][:, :], pb[sl][:, :], ALU.mult, rd=[b_sil[sl], b_pb[sl]], wr=[b_aT[f]])

        def fin_store(g):
            sl = g % NX
            xr = xres[sl]
            if final:
                P.act(junk2[:, :], xr[:, :], AF.Square, rd=[b_xres[sl]], wr=[b_j2], accum_out=ssq2[:, g:g + 1])
                t1 = P.act(std2[:, g:g + 1], ssq2[:, g:g + 1], AF.Sqrt, rd=[b_j2, N.b_c], scale=1.0 / D, bias=N.eps[:, 0:1])
                t2 = P.recip(rstd2[:, g:g + 1], std2[:, g:g + 1], extra=[t1])
                P.stt(xr[:, :], xr[:, :], rstd2[:, g:g + 1], fg[:, :], ALU.mult, ALU.mult, rd=[N.b_c], wr=[b_xres[sl]], extra=[t2])
            P.dma("sp", "st%d" % sl, out=dst[g * 128:(g + 1) * 128, :], in_=xr[:, :], rd=[b_xres[sl]])

        def stage2(i, s):
            g = 4 * i + s
            sl = g % NX
            xr = xres[sl]
            P.dma("sp", "xr%d" % sl, out=xr[:, :], in_=src[g * 128:(g + 1) * 128, :], wr=[b_xres[sl]])
            for h in range(2):
                P.mms([(py[h][:, :], aT[:, f, s * 128:(s + 1) * 128], W2[:, f, h * 512:(h + 1) * 512], f == 0, f == NFF - 1)
                       for f in range(NFF)], rd=b_aT + [b_w2], wr=[b_py[h]])
                P.stt(xr[:, h * 512:(h + 1) * 512], py[h][:, :], 0.5, xr[:, h * 512:(h + 1) * 512], ALU.mult, ALU.add,
                      rd=[b_py[h]], wr=[b_xres[sl]])
            if final:
                if g > 0:
                    fin_store(g - 1)
            else:
                fin_store(g)

        for s in range(4):
            N.norm(s)
            N.transp(s, 0)
        for i in range(NT):
            stage1(i)
            for s in range(4):
                if i + 1 < NT:
                    N.norm(4 * (i + 1) + s)
                stage2(i, s)
                if i + 1 < NT:
                    N.transp(4 * (i + 1) + s, 0)
        if final:
            fin_store(4 * NT - 1)
        P.run_phase(ph)


def b0_phase(P, dr, Wo, wsT, B2c, b_setup, Win):
    nc = P.nc
    with ExitStack() as es:
        def sb(name, shape, dt):
            return es.enter_context(nc.sbuf_tensor("B0_" + name, shape, dt))
        wost = sb("wost", [128, 8, D], F32)
        ogfm = sb("ogfm", [128, 8], F32)
        wsf = sb("wsf", [128, 8, 128], F32)
        smk = sb("smk", [128, 128], F32)
        lnb = sb("lnb", [128, 8, 64], F32)
        bsT = sb("bsT", [128, 8], F32)
        rs = sb("rs", [128, 8], F32)
        onesb = sb("onesb", [128, 1], BF16)
        prs = es.enter_context(nc.psum_tensor("B0_prs", [128, 8], F32))
        b_in, b_o, b_ws, b_prs, b_rs = Buf(), Buf(), Buf(), Buf(), Buf()
        load_cast(P, "w13", Win, dr["win"], 8, INC, [Buf()])
        P.dma("sp", "c0", out=wost[:, :, :], in_=dr["wo"], wr=[b_in], part=True)
        for nm, t, ap in (("og", ogfm, None), ("ws", wsf, None), ("smask", smk, None), ("lnb", lnb, None), ("bsT", bsT, None)):
            if nm == "lnb":
                P.dma("sp", "c0", out=t[:, :, :], in_=dr[nm].rearrange("p (h c) -> p h c", c=64), wr=[b_in], part=True)
            elif nm == "ws":
                P.dma("sp", "c0", out=t[:, :, :], in_=dr[nm], wr=[b_in], part=True)
            else:
                P.dma("sp", "c0", out=t[:, :], in_=dr[nm], wr=[b_in], part=True)
        for kc in range(8):
            P.ts("dve", Wo[:, kc, :], wost[:, kc, :], ogfm[:, kc:kc + 1], None, ALU.mult,
                 rd=[b_in], wr=[b_setup])
        P.tt("dve", wsT[:, :, :], wsf[:, :, :], smk[:, :].unsqueeze(1).to_broadcast([128, 8, 128]), ALU.mult,
             rd=[b_in], wr=[b_ws])
        P.memset("dve", onesb[:, :], 1.0, wr=[b_o])
        P.mms([(prs[:, h:h + 1], wsT[:, h, :], onesb[:, 0:1], True, True) for h in range(8)], rd=[b_ws, b_o], wr=[b_prs])
        P.copy("dve", rs[:, :], prs[:, :], rd=[b_prs], wr=[b_rs])
        P.tt("dve", B2c[:, :, :], lnb[:, :, :], rs[:, :].unsqueeze(2).to_broadcast([128, 8, 64]), ALU.mult,
             rd=[b_in, b_rs], wr=[b_setup])
        P.tt("dve", B2c[:, :, :], B2c[:, :, :], bsT[:, :].unsqueeze(2).to_broadcast([128, 8, 64]), ALU.add,
             rd=[b_in], wr=[b_setup])
        P.run_phase("B0")


def b1_phase(P, S, src, qTd, kTd, vd, ysTd, dr, wsT, B2c, b_setup, Win):
    nc = P.nc
    NCH = S // 512
    NSUB = S // 128
    with ExitStack() as es:
        def sb(name, shape, dt):
            return es.enter_context(nc.sbuf_tensor("B1_" + name, shape, dt))

        def ps(name, shape, dt):
            return es.enter_context(nc.psum_tensor("B1_" + name, shape, dt))

        N = NormT(P, sb, ps, src, dr["mg"], dr["ident"], NSUB, 2)
        N.eps = sb("eps", [128, 1], F32)
        P.dma("sp", "cst", out=N.eps[:, :], in_=dr["epsc"], wr=[N.b_c], part=True)
        lng = sb("lng", [128, 512], F32)
        P.dma("sp", "cst", out=lng[:, :], in_=dr["lng"], wr=[N.b_c], part=True)
        fb = sb("fb", [8, 1], F32)
        nfb = sb("nfb", [8, 1], F32)
        one8 = sb("one8", [8, 1], F32)
        P.dma("sp", "cst", out=fb[:, :], in_=dr["fb"], wr=[N.b_c], part=True)
        P.ts("dve", nfb[:, :], fb[:, :], -1.0, None, ALU.mult, rd=[N.b_c], wr=[N.b_c])
        P.memset("dve", one8[:, :], 1.0, wr=[N.b_c])
        qst = sb("qst", [128, 4, 512], BF16)
        kst = sb("kst", [128, 4, 512], BF16)
        vst = sb("vst", [128, 4, 4, 192], BF16)
        yst = sb("yst", [128, 4, 512], BF16)
        augq = sb("augq", [8, 6, 512], BF16)
        augk = sb("augk", [8, 6, 512], BF16)
        ef = sb("ef", [8, 512], F32)
        lf = sb("lf", [8, 512], F32)
        Fc = [sb("Fc%d" % i, [8, 512], F32) for i in range(2)]
        tmpU = [sb("tmpU%d" % i, [128, 512], F32) for i in range(3)]
        tmpV = [sb("tmpV%d" % i, [128, 512], F32) for i in range(3)]
        gu = [sb("gu%d" % i, [128, 512], F32) for i in range(3)]
        gv = [sb("gv%d" % i, [128, 512], F32) for i in range(3)]
        nbf = [sb("nbf%d" % i, [128, 8, 64], BF16) for i in range(3)]
        yn = [sb("yn%d" % i, [128, 512], BF16) for i in range(2)]
        stn = ("vsum", "vssq", "mu", "m2", "var", "sdv", "rstdv", "nmr")
        st = {n: sb(n, [128, NSUB], F32) for n in stn}
        gss = sb("gss", [128, NSUB, 8], F32)
        gsd = sb("gsd", [128, NSUB, 8], F32)
        gr = sb("gr", [128, NSUB, 8], F32)
        pA = [ps("pA%d" % i, [128, 512], F32) for i in range(2)]
        pB = [ps("pB%d" % i, [128, 512], F32) for i in range(2)]
        pM = ps("pM", [128, 8, 64], F32)
        pT = ps("pT", [128, 4, 128], BF16)
        b_win = Buf()
        b_pA, b_pB = [Buf(), Buf()], [Buf(), Buf()]
        b_pM, b_pT = Buf(), Buf()
        b_qst, b_kst, b_vst, b_yst, b_augq, b_augk = Buf(), Buf(), Buf(), Buf(), Buf(), Buf()
        b_ef, b_lf = Buf(), Buf()
        b_Fc = [Buf(), Buf()]
        b_tmpU, b_tmpV, b_gu, b_gv = ([Buf() for _ in range(3)] for _ in range(4))
        b_nbf, b_yn, b_stat = [Buf() for _ in range(3)], [Buf(), Buf()], [Buf() for _ in range(3)]

        P.memset("dve", vst[:, :, :, :], 1.0, wr=[b_vst])
        P.memset("dve", augq[:, :, :], 1.0, wr=[b_augq])
        P.memset("dve", augk[:, :, :], 1.0, wr=[b_augk])
        P.memset("dve", Fc[1][:, :], 0.0, wr=[b_Fc[1]])

        def v3(t):
            return t[:, :].rearrange("p (h c) -> p h c", c=64)

        def proj_qkf(c):
            cc = c % 2
            hT = N.hT[cc]
            c0, c1 = c * 512, (c + 1) * 512
            for j in range(8):
                pbj = pB[j % 2]
                P.mms([(pbj[:, :], Win[:, k, 1024 + j * 128:1024 + (j + 1) * 128], hT[:, k, :], k == 0, k == 7) for k in range(8)],
                      rd=[N.b_hT[cc], b_win], wr=[b_pB[j % 2]])
                if j < 4:
                    P.act(qst[:, j, :], pbj[:, :], AF.Copy, rd=[b_pB[j % 2]], wr=[b_qst], scale=0.125)
                else:
                    P.copy("dve", kst[:, j - 4, :], pbj[:, :], rd=[b_pB[j % 2]], wr=[b_kst])
            for h in range(8):
                r0 = (h % 2) * 64
                P.dma("sp", "qd", out=qTd[h, 0:64, c0:c1], in_=qst[r0:r0 + 64, h // 2, :], rd=[b_qst])
                P.dma("sp", "kd", out=kTd[h, 0:64, c0:c1], in_=kst[r0:r0 + 64, h // 2, :], rd=[b_kst])
            P.mms([(pB[0][0:8, :], Win[:, k, 2560:2568], hT[:, k, :], k == 0, k == 7) for k in range(8)],
                  rd=[N.b_hT[cc], b_win], wr=[b_pB[0]])
            P.act(ef[:, :], pB[0][0:8, :], AF.Exp, rd=[b_pB[0], N.b_c], wr=[b_ef], scale=-1.0, bias=nfb[:, 0:1])
            P.act(lf[:, :], ef[:, :], AF.Ln, rd=[b_ef], wr=[b_lf], bias=one8[:, 0:1])
            fcc, fpp = Fc[cc], Fc[1 - cc]
            P.op("dve", lambda e: e.tensor_tensor_scan(out=fcc[:, :], data0=one8[:, 0:1].to_broadcast([8, 512]), data1=lf[:, :],
                                                       initial=fpp[:, 511:512], op0=ALU.mult, op1=ALU.subtract),
                 rd=[b_lf, b_Fc[1 - cc], N.b_c], wr=[b_Fc[cc]])
            P.copy("dve", augq[:, 0, :], fcc[:, :], rd=[b_Fc[cc]], wr=[b_augq])
            P.tt("dve", ef[:, :], fcc[:, :], augq[:, 0, :], ALU.subtract, rd=[b_Fc[cc], b_augq], wr=[b_ef])
            P.copy("dve", augq[:, 1, :], ef[:, :], rd=[b_ef], wr=[b_augq])
            P.tt("dve", lf[:, :], ef[:, :], augq[:, 1, :], ALU.subtract, rd=[b_ef, b_augq], wr=[b_lf])
            P.copy("dve", augq[:, 2, :], lf[:, :], rd=[b_lf], wr=[b_augq])
            P.ts("dve", augk[:, 3:6, :], augq[:, 0:3, :], -1.0, None, ALU.mult, rd=[b_augq], wr=[b_augk])
            P.dma("sp", "qa", out=qTd[:, 64:70, c0:c1], in_=augq[:, :, :], rd=[b_augq])
            P.dma("sp", "ka", out=kTd[:, 64:70, c0:c1], in_=augk[:, :, :], rd=[b_augk])

        def proj_v(c):
            cc = c % 2
            hT = N.hT[cc]
            for s in range(4):
                pa_ = pA[s % 2]
                P.mms([(pa_[:, :], hT[:, k, s * 128:(s + 1) * 128], Win[:, k, 2048:2560], k == 0, k == 7) for k in range(8)],
                      rd=[N.b_hT[cc], b_win], wr=[b_pA[s % 2]])
                pv4 = pa_[:, :].rearrange("p (j hh d) -> p j hh d", hh=2, d=64)
                P.copy("dve", vst[:, s, :, 0:64], pv4[:, :, 0, :], rd=[b_pA[s % 2]], wr=[b_vst])
                P.act(vst[:, s, :, 128:192], pv4[:, :, 1, :], AF.Copy, rd=[b_pA[s % 2]], wr=[b_vst])
            P.dma("sp", "vd", out=vd[:, c * 4:(c + 1) * 4, :],
                  in_=vst[:, :, :, :].rearrange("p s j f -> p s (j f)"), rd=[b_vst])

        def gelu(Xp, b_X, tmp, b_tmp, res, b_res, accum):
            if accum is None:
                P.act(res[:, :], Xp[:, :], AF.Gelu_apprx_tanh, rd=[b_X], wr=[b_res])
            else:
                P.act(res[:, :], Xp[:, :], AF.Gelu_apprx_tanh, rd=[b_X], wr=[b_res, accum[1]], accum_out=accum[0])

        def sgu1(c, s):
            cc = c % 2
            hT = N.hT[cc]
            g = 4 * c + s
            w = g % 3
            gc = slice(g, g + 1)
            P.mms([(pA[0][:, :], hT[:, k, s * 128:(s + 1) * 128], Win[:, k, 0:512], k == 0, k == 7) for k in range(8)],
                  rd=[N.b_hT[cc], b_win], wr=[b_pA[0]])
            P.mms([(pA[1][:, :], hT[:, k, s * 128:(s + 1) * 128], Win[:, k, 512:1024], k == 0, k == 7) for k in range(8)],
                  rd=[N.b_hT[cc], b_win], wr=[b_pA[1]])
            gelu(pA[0], b_pA[0], tmpU[w], b_tmpU[w], gu[w], b_gu[w], None)
            gelu(pA[1], b_pA[1], tmpV[w], b_tmpV[w], gv[w], b_gv[w], (st["vsum"][:, gc], b_stat[w]))

        def sgu1b(c, s):
            g = 4 * c + s
            w = g % 3
            gc = slice(g, g + 1)
            bs = [b_stat[w]]
            P.act(tmpV[w][:, :], gv[w][:, :], AF.Square, rd=[b_gv[w]], wr=[b_tmpV[w]] + bs, accum_out=st["vssq"][:, gc])
            P.ts("dve", st["mu"][:, gc], st["vsum"][:, gc], 1.0 / 512, None, ALU.mult, wr=bs, cost=0.25)
            P.tt("dve", st["m2"][:, gc], st["mu"][:, gc], st["mu"][:, gc], ALU.mult, wr=bs, cost=0.25)
            P.stt(st["var"][:, gc], st["vssq"][:, gc], 1.0 / 512, st["m2"][:, gc], ALU.mult, ALU.subtract, wr=bs, cost=0.25)
            P.act(st["sdv"][:, gc], st["var"][:, gc], AF.Sqrt, rd=[N.b_c], wr=bs, bias=N.eps[:, 0:1], cost=0.25)
            P.recip(st["rstdv"][:, gc], st["sdv"][:, gc], wr=bs)
            P.stt(st["nmr"][:, gc], st["mu"][:, gc], -1.0, st["rstdv"][:, gc], ALU.mult, ALU.mult, wr=bs, cost=0.25)
            P.act(nbf[w][:, :, :], v3(gv[w]), AF.Identity, rd=[b_gv[w]] + bs, wr=[b_nbf[w]],
                  scale=st["rstdv"][:, gc], bias=st["nmr"][:, gc])

        def sgu2(c, s):
            g = 4 * c + s
            w = g % 3
            y = g % 2
            bs = [b_stat[w]]
            P.mms([(pM[:, h, :], wsT[:, h, :], nbf[w][:, h, :], True, True) for h in range(8)],
                  rd=[b_nbf[w], b_setup], wr=[b_pM])
            P.tt("dve", v3(tmpU[w]), pM[:, :, :], v3(lng), ALU.mult, rd=[b_pM, N.b_c], wr=[b_tmpU[w]])
            P.tt("dve", v3(tmpU[w]), v3(tmpU[w]), B2c[:, :, :], ALU.add, rd=[b_setup], wr=[b_tmpU[w]])
            P.tt("dve", tmpU[w][:, :], tmpU[w][:, :], gu[w][:, :], ALU.mult, rd=[b_gu[w]], wr=[b_tmpU[w]])
            P.tt("dve", tmpV[w][:, :], tmpU[w][:, :], tmpU[w][:, :], ALU.mult, rd=[b_tmpU[w]], wr=[b_tmpV[w]])
            tv3 = v3(tmpV[w])
            P.op("dve", lambda e: e.tensor_reduce(out=gss[:, g, :], in_=tv3, axis=AX.X, op=ALU.add), rd=[b_tmpV[w]], wr=bs)
            P.act(gsd[:, g, :], gss[:, g, :], AF.Sqrt, rd=[N.b_c], wr=bs, scale=1.0 / 64, bias=N.eps[:, 0:1], cost=0.25)
            P.recip(gr[:, g, :], gsd[:, g, :], wr=bs)
            P.tt("dve", v3(yn[y]), v3(tmpU[w]), gr[:, g, :].unsqueeze(2).to_broadcast([128, 8, 64]), ALU.mult,
                 rd=[b_tmpU[w]] + bs, wr=[b_yn[y]])

        def sgu3(c, s):
            y = (4 * c + s) % 2
            P.trs([(pT[:, kc, :], yn[y][:, kc * 128:(kc + 1) * 128]) for kc in range(4)], N.ident[:, :],
                  rd=[b_yn[y], N.b_c], wr=[b_pT])
            P.act(yst[:, :, s * 128:(s + 1) * 128], pT[:, :, :], AF.Copy, rd=[b_pT], wr=[b_yst])
            if s == 3:
                P.dma("sp", "yd", out=ysTd.rearrange("(kc p) n -> p kc n", p=128)[:, :, c * 512:(c + 1) * 512],
                      in_=yst[:, :, :], rd=[b_yst])

        for s in range(4):
            N.norm(s)
            N.transp(s, 0)
        NG = 4 * NCH

        def stream_a(t):
            c, s_ = divmod(t, 4)
            if s_ == 0:
                proj_qkf(c)
                proj_v(c)
            if c + 1 < NCH:
                N.norm(4 * (c + 1) + s_)
            sgu1(c, s_)
            if c + 1 < NCH:
                N.transp(4 * (c + 1) + s_, (c + 1) % 2)

        for t in range(NG + 3):
            lists = []
            if t < NG:
                lists.append(P.record(stream_a, t))
            for d_, fn_ in ((1, sgu1b), (2, sgu2), (3, sgu3)):
                if 0 <= t - d_ < NG:
                    lists.append(P.record(fn_, (t - d_) // 4, (t - d_) % 4))
            P.play(lists)
        P.run_phase("B1")


def b2_phase(P, S, x1, x2, qTd, kTd, vd, ysTd, dr, Wo, b_setup):
    nc = P.nc
    NCH = S // 512
    NSUB = S // 128
    with ExitStack() as es:
        def sb(name, shape, dt):
            return es.enter_context(nc.sbuf_tensor("B2_" + name, shape, dt))

        def ps(name, shape, dt):
            return es.enter_context(nc.psum_tensor("B2_" + name, shape, dt))

        Kst = sb("Kst", [128, 8, S], BF16)
        Vst = sb("Vst", [128, NSUB, 768], BF16)
        qc = [sb("qc%d" % i, [128, 8, 512], BF16) for i in range(2)]
        ysg = [sb("ysg%d" % i, [128, 4, 512], BF16) for i in range(2)]
        yfox = [sb("yfox%d" % i, [128, 4, 512], BF16) for i in range(2)]
        PT = [sb("PT%d" % i, [128, 512], BF16) for i in range(4)]
        Ocp = [sb("Ocp%d" % i, [128, 512], BF16) for i in range(2)]
        X = [sb("X%d" % i, [128, 512], BF16) for i in range(2)]
        lnn = sb("lnn", [128, 512], F32)
        rr = sb("rr", [128, 512], F32)
        xres = [sb("xres%d" % i, [128, D], F32) for i in range(2)]
        tri = sb("tri", [128, 128], BF16)
        idb = sb("idb", [128, 128], BF16)
        b2m = [sb("b2m%d" % i, [128, 128], BF16) for i in range(2)]
        pS = [ps("pS%d" % i, [128, 512], F32) for i in range(4)]
        pO = [ps("pO%d" % i, [128, 512], F32) for i in range(2)]
        pN = ps("pN", [128, 512], F32)
        pY = [ps("pY0", [128, 512], F32)] * 2
        b_cst = Buf()
        b_qc, b_ysg, b_yfox = [Buf(), Buf()], [Buf(), Buf()], [Buf(), Buf()]
        b_PT = [Buf() for _ in range(4)]
        b_Ocp, b_X = [Buf(), Buf()], [Buf(), Buf()]
        b_lnn, b_rr = Buf(), Buf()
        b_xres = [Buf(), Buf()]
        b_pS = [Buf() for _ in range(4)]
        b_pO = [Buf(), Buf()]
        b_pN = Buf()
        b_pY = [Buf()] * 2

        P.dma("pool", "c2", out=tri[:, :], in_=dr["nmask"], wr=[b_cst], part=True)
        P.dma("pool", "c2", out=idb[:, :], in_=dr["ident"], wr=[b_cst], part=True)
        P.dma("pool", "c2", out=b2m[0][:, :], in_=dr["b2a"], wr=[b_cst], part=True)
        P.dma("pool", "c2", out=b2m[1][:, :], in_=dr["b2b"], wr=[b_cst], part=True)
        b_Kh = [Buf() for _ in range(8)]
        b_Kl = [Buf() for _ in range(8)]
        nv = max(1, NSUB // 8)
        b_Vp = [Buf() for _ in range(0, NSUB, nv)]
        for i in range(2):
            P.memset("dve", qc[i][64:128, :, :].bitcast(F32), 0.0, wr=[b_qc[i]])
        for h in range(8):
            P.memset("dve", Kst[64:128, h, :].bitcast(F32), 0.0, wr=[b_Kh[h]])
        def vlhsT(h, j):
            o = (h // 2) * 192 + (h % 2) * 64
            return Vst[:, j, o:o + 128]

        def load_chunk(c):
            cc = c % 2
            P.dma("sp", "qc%d" % cc, out=qc[cc][0:70, :, :], in_=qTd[:, :, c * 512:(c + 1) * 512].rearrange("h r n -> r h n"),
                  wr=[b_qc[cc]])
            P.dma("sp", "ys%d" % cc, out=ysg[cc][:, :, :],
                  in_=ysTd.rearrange("(kc p) n -> p kc n", p=128)[:, :, c * 512:(c + 1) * 512], wr=[b_ysg[cc]])

        load_chunk(0)
        for h in range(8):
            P.dma("sp", "kl%d" % h, out=Kst[0:64, h, :], in_=kTd[h, 0:64, :], wr=[b_Kl[h]])
            if h < len(b_Vp):
                i = h * nv
                P.dma("sp", "vv%d" % h, out=Vst[:, i:i + nv, :], in_=vd[:, i:i + nv, :], wr=[b_Vp[h]])
        for h in range(8):
            P.dma("sp", "kh%d" % h, out=Kst[64:70, h, :], in_=kTd[h, 64:70, :], wr=[b_Kh[h]])
        for h in range(8, len(b_Vp)):
            i = h * nv
            P.dma("sp", "vv%d" % h, out=Vst[:, i:i + nv, :], in_=vd[:, i:i + nv, :], wr=[b_Vp[h]])

        def epi_a(u):
            P.copy("dve", Ocp[u % 2][:, :], pO[u % 2][:, :], rd=[b_pO[u % 2]], wr=[b_Ocp[u % 2]])
            P.tt("dve", X[u % 2][:, :], Ocp[u % 2][:, :], Ocp[u % 2][:, :], ALU.mult, rd=[b_Ocp[u % 2]], wr=[b_X[u % 2]])

        def epi_b(u, c, h):
            cc = c % 2
            R = slice(0, 64) if h % 2 == 0 else slice(64, 128)
            P.mms([(pN[:, :], b2m[h % 2][:, :], X[u % 2][:, :], True, True)], rd=[b_X[u % 2], b_cst], wr=[b_pN])
            P.act(lnn[R, :], pN[R, :], AF.Ln, rd=[b_pN], wr=[b_lnn])
            P.act(rr[R, :], lnn[R, :], AF.Exp, rd=[b_lnn], wr=[b_rr], scale=-0.5)
            P.tt("dve", yfox[cc][R, h // 2, :], Ocp[u % 2][R, :], rr[R, :], ALU.mult, rd=[b_Ocp[u % 2], b_rr], wr=[b_yfox[cc]])

        def w_out_parts(c):
            cc = c % 2
            parts = []
            for s in range(4):
                g = 4 * c + s
                sl = g % 2
                xr = xres[sl]
                for hf in range(2):
                    for q in range(4):
                        def part(s=s, g=g, sl=sl, xr=xr, hf=hf, q=q):
                            if hf == 0 and q == 0:
                                P.dma("sp", "xr%d" % sl, out=xr[:, :], in_=x1[g * 128:(g + 1) * 128, :], wr=[b_xres[sl]])
                            P.mms([(pY[hf][:, :], (ysg[cc] if kc < 4 else yfox[cc])[:, kc % 4, s * 128:(s + 1) * 128],
                                    Wo[:, kc, hf * 512:(hf + 1) * 512], kc == 0, kc == 7) for kc in (2 * q, 2 * q + 1)],
                                  rd=[b_ysg[cc], b_yfox[cc], b_setup], wr=[b_pY[hf]])
                            if q == 3:
                                P.tt("dve", xr[:, hf * 512:(hf + 1) * 512], pY[hf][:, :], xr[:, hf * 512:(hf + 1) * 512],
                                     ALU.add, rd=[b_pY[hf]], wr=[b_xres[sl]])
                                if hf == 1:
                                    P.dma("sp", "st%d" % sl, out=x2[g * 128:(g + 1) * 128, :], in_=xr[:, :], rd=[b_xres[sl]])
                        parts.append(part)
            return parts

        wq = []

        def unit(c, h, u, sl, pend=(), nw=0):
            cc = c % 2
            njt = 4 * c + 4

            def s_mm(j):
                q0 = max(0, j - 4 * c) * 128
                k_ = 2 * sl + j % 2
                mm = [(pS[k_][:, q0:512], Kst[:, h, j * 128:(j + 1) * 128], qc[cc][:, h, q0:512], True, j < 4 * c)]
                if j >= 4 * c:
                    mm.append((pS[k_][:, q0:q0 + 128], idb[:, :], tri[:, :], False, True))
                P.mms(mm, rd=[b_Kl[h], b_Kh[h], b_qc[cc], b_cst], wr=[b_pS[k_]])

            def e_pv(j):
                r = j - 4 * c
                q0 = max(0, r) * 128
                k_ = 2 * sl + j % 2
                P.act(PT[k_][:, q0:512], pS[k_][:, q0:512], AF.Exp, rd=[b_pS[k_]], wr=[b_PT[k_]])
                P.mms([(pO[u % 2][:, q0:512], vlhsT(h, j), PT[k_][:, q0:512], j == 0, j == njt - 1)],
                      rd=[b_PT[k_], b_Vp[j // nv]], wr=[b_pO[u % 2]])

            for j in range(min(2, njt)):
                s_mm(j)
            for j in range(njt):
                e_pv(j)
                if j + 2 < njt:
                    s_mm(j + 2)
                if j == min(2, njt - 1):
                    for f in pend:
                        f()
                if j >= 1 and nw > 0 and wq:
                    wq.pop(0)()
                    nw -= 1
            while nw > 0 and wq:
                wq.pop(0)()
                nw -= 1

        u = 0
        pending = []
        for c in range(NCH):
            for p in range(4):
                nw = 0 if p == 0 else (11 if p < 3 else 99)

                lists = [P.record(unit, c, 2 * p, u, 0, pending, nw), P.record(unit, c, 2 * p + 1, u + 1, 1)]
                P.play(lists)
                epi_a(u)
                epi_a(u + 1)
                pending = [(lambda u_=u, c_=c, h_=2 * p: epi_b(u_, c_, h_)),
                           (lambda u_=u + 1, c_=c, h_=2 * p + 1: epi_b(u_, c_, h_))]
                if p == 0 and c > 0:
                    wq.extend(w_out_parts(c - 1))
                if p == 3 and c + 1 < NCH:
                    load_chunk(c + 1)
                u += 2
        for f in pending:
            f()
        for f in w_out_parts(NCH - 1):
            f()
        P.run_phase("B2")


def build_nc(S=4096, upto="C"):
    nc = bass.Bass("TRN2", target_bir_lowering=False)
    NSUB = S // 128

    def din(name, shape):
        return nc.dram_tensor(name, shape, F32, kind="ExternalInput").ap()

    def dint(name, shape, dt):
        return nc.dram_tensor(name, shape, dt, kind="Internal").ap()

    x = din("x", [S, D])
    f1g = din("f1g", [128, 8])
    f1w1 = din("f1w1", [128, 8, DFF])
    f1w3 = din("f1w3", [128, 8, DFF])
    f1w2 = din("f1w2", [128, NFF, D])
    f2g = din("f2g", [128, 8])
    f2w1 = din("f2w1", [128, 8, DFF])
    f2w3 = din("f2w3", [128, 8, DFF])
    f2w2 = din("f2w2", [128, NFF, D])
    fg = din("fg", [128, D])
    ident = din("ident", [128, 128])
    epsd = din("epsc", [128, 1])
    dr = {"ident": ident, "epsc": epsd}
    for nm, shp in (("mg", [128, 8]), ("win", [128, 8, INC]), ("fb", [8, 1]), ("lng", [128, 512]), ("lnb", [128, 512]),
                    ("ws", [128, 8, 128]), ("bsT", [128, 8]), ("og", [128, 8]), ("wo", [128, 8, D]),
                    ("nmask", [128, 128]), ("smask", [128, 128]), ("b2a", [128, 128]), ("b2b", [128, 128])):
        dr[nm] = din(nm, shp)
    out = nc.dram_tensor("out", [S, D], F32, kind="ExternalOutput").ap()
    x1 = dint("x1s", [S, D], F32)
    x2 = dint("x2s", [S, D], F32)
    qTd = dint("qTd", [8, 70, S], BF16)
    kTd = dint("kTd", [8, 70, S], BF16)
    vd = dint("vd", [128, NSUB, 768], BF16)
    ysTd = dint("ysTd", [512, S], BF16)
    with ExitStack() as es:
        P = Prog(nc, es)
        ffn_phase(P, S, x, out if upto == "A" else x1, f1g, f1w1, f1w3, f1w2, ident, epsd, None, "A")
        if upto in ("B", "C"):
            with ExitStack() as es2:
                Wo = es2.enter_context(nc.sbuf_tensor("Wo", [128, 8, D], BF16))
                wsT = es2.enter_context(nc.sbuf_tensor("wsT", [128, 8, 128], BF16))
                B2c = es2.enter_context(nc.sbuf_tensor("B2c", [128, 8, 64], F32))
                with ExitStack() as es3:
                    Win = es3.enter_context(nc.sbuf_tensor("Win", [128, 8, INC], BF16))
                    b_setup = Buf()
                    b0_phase(P, dr, Wo, wsT, B2c, b_setup, Win)
                    b_setup = Buf()
                    b1_phase(P, S, x1, qTd, kTd, vd, ysTd, dr, wsT, B2c, b_setup, Win)
                b_setup = Buf()
                b2_phase(P, S, x1, out if upto == "B" else x2, qTd, kTd, vd, ysTd, dr, Wo, b_setup)
        if upto == "AC":
            ffn_phase(P, S, x1, out, f2g, f2w1, f2w3, f2w2, ident, epsd, fg, "C")
        if upto == "C":
            ffn_phase(P, S, x2, out, f2g, f2w1, f2w3, f2w2, ident, epsd, fg, "C")
    return nc


def fm(v, n):
    return np.ascontiguousarray(np.asarray(v, np.float32).reshape(n, 128).T)


def wl(w, n):
    w = np.asarray(w, np.float32)
    return np.ascontiguousarray(w.reshape(n, 128, w.shape[1]).transpose(1, 0, 2))


def make_inmaps(inp, S, ncores):
    g = lambda k: np.asarray(inp[k], np.float32)
    pi = np.arange(128)
    b2a = np.zeros((128, 128), np.float32)
    b2a[:64, :64] = 1.0 / 64
    b2a[64:, :64] = EPS / 64
    b2b = np.zeros((128, 128), np.float32)
    b2b[64:, 64:] = 1.0 / 64
    b2b[:64, 64:] = EPS / 64
    shared = {
        "f1g": fm(g("ffn1_norm_g")[0], 8), "f1w1": wl(g("ffn1_w1")[0], 8), "f1w3": wl(g("ffn1_w3")[0], 8),
        "f1w2": wl(g("ffn1_w2")[0], NFF),
        "f2g": fm(g("ffn2_norm_g")[0], 8), "f2w1": wl(g("ffn2_w1")[0], 8), "f2w3": wl(g("ffn2_w3")[0], 8),
        "f2w2": wl(g("ffn2_w2")[0], NFF),
        "fg": np.ascontiguousarray(np.broadcast_to(g("final_norm_g")[None, :], (128, D))),
        "ident": np.eye(128, dtype=np.float32),
        "epsc": np.full((128, 1), EPS, np.float32),
        "mg": fm(g("mix_norm_g")[0], 8), "win": wl(g("w_in")[0], 8),
        "fb": np.ascontiguousarray(g("fox_f_bias")[0].reshape(8, 1)),
        "lng": np.ascontiguousarray(np.broadcast_to(g("sgu_ln_g")[0][None, :], (128, 512))),
        "lnb": np.ascontiguousarray(np.broadcast_to(g("sgu_ln_b")[0][None, :], (128, 512))),
        "ws": np.ascontiguousarray(g("sgu_w_s")[0].transpose(2, 0, 1)),
        "bsT": np.ascontiguousarray(g("sgu_b_s")[0].T),
        "og": fm(g("mix_out_g")[0], 8), "wo": wl(g("w_out")[0], 8),
        "nmask": -30000.0 * (pi[:, None] > pi[None, :]).astype(np.float32),
        "smask": ((pi[None, :] // 64) >= (pi[:, None] // 64)).astype(np.float32),
        "b2a": b2a, "b2b": b2b,
    }
    xs = g("x")
    maps = []
    for c in range(ncores):
        m = dict(shared)
        m["x"] = np.ascontiguousarray(xs[c, :S, :])
        maps.append(m)
    return maps


_NC_CACHE = {}


def kernel(**inputs):
    S = 4096
    n = 8
    key = (S, "C")
    if key not in _NC_CACHE:
        _NC_CACHE[key] = build_nc(S, "C")
    nc = _NC_CACHE[key]
    maps = make_inmaps(inputs, S, n)
    res = run_bass_kernel_spmd(nc, maps, core_ids=list(range(n)))
    return np.stack([np.asarray(r["out"], np.float32) for r in res.results], axis=0)
```

```python
import numpy as np
from contextlib import ExitStack
import concourse.bass as bass
import concourse.mybir as mybir
from concourse.bass_utils import run_bass_kernel_spmd

F32 = mybir.dt.float32
BF16 = mybir.dt.bfloat16
AF = mybir.ActivationFunctionType
ALU = mybir.AluOpType
AX = mybir.AxisListType

D = 1024
DFF = 2816
NFF = 22
EPS = 1e-6
INC = 2568
GC1 = 0.044715
GC2 = 1.5957691216057308
ENGS = ("pe", "act", "dve", "pool", "sp")


class Buf:
    def __init__(self):
        self.w = {}
        self.r = {}
        self.tw = 0.0
        self.tr = 0.0
        self.weng = None


def _merge(d, tok):
    k, v = tok
    if d.get(k, 0) < v:
        d[k] = v


class Prog:
    def __init__(self, nc, es):
        self.nc = nc
        self.sem = {e: es.enter_context(nc.semaphore("c_" + e)) for e in ENGS}
        self.cnt = {e: 0 for e in ENGS}
        self.seen = {e: {} for e in ENGS}
        self.dsem = {}
        self.es = es
        self.q = {e: [] for e in ENGS}

    def _semobj(self, key):
        return self.sem[key] if key in self.sem else self.dsem[key][0]

    def _deps(self, eng, rd, wr, extra):
        need = {}
        for b in rd:
            for k, v in b.w.items():
                _merge(need, (k, v))
        for b in wr:
            for k, v in b.w.items():
                _merge(need, (k, v))
            for k, v in b.r.items():
                _merge(need, (k, v))
        for t in extra:
            if t is not None:
                _merge(need, t)
        waits = []
        for k, v in need.items():
            if k == eng and eng == "pe":
                continue
            if self.seen[eng].get(k, 0) >= v:
                continue
            self.seen[eng][k] = v
            waits.append((k, v))
        return waits

    def _commit(self, tok, rd, wr):
        for b in rd:
            _merge(b.r, tok)
        for b in wr:
            b.w = {tok[0]: tok[1]}
            b.r = {}

    rec = None

    def record(self, fn, *a):
        self.rec = []
        fn(*a)
        L, self.rec = self.rec, None
        return L

    COST = {"pe": 0.22, "act": 0.7, "dve": 0.62, "pool": 1.0, "sp": 0.05}
    HOP = 0.3

    def play(self, lists):
        idx = [0] * len(lists)
        tf = self.__dict__.setdefault("tfree", {})
        while True:
            best = None
            for li, L in enumerate(lists):
                if idx[li] >= len(L):
                    continue
                kind, a, kw = L[idx[li]]
                eng = a[0]
                t = tf.get(eng, 0.0)
                for b in kw["rd"]:
                    t = max(t, b.tw + (self.HOP if b.weng != eng else 0.05))
                for b in kw["wr"]:
                    t = max(t, b.tw + (self.HOP if b.weng != eng else 0.05), b.tr + self.HOP)
                tb = kw.get("tbl")
                if tb is not None and tb != self.__dict__.get("cur_tbl"):
                    t += 1.3
                if best is None or t < best[0] - 1e-9:
                    best = (t, li)
            if best is None:
                break
            t, li = best
            kind, a, kw = lists[li][idx[li]]
            idx[li] += 1
            eng = a[0]
            cost = kw.pop("cost", None)
            tb = kw.pop("tbl", None)
            if tb is not None:
                self.cur_tbl = tb
            if kind == "op":
                fns = a[1]
                nf = len(fns) if isinstance(fns, (list, tuple)) else 1
                dur = cost if cost is not None else (self.COST[eng] * nf + (0.06 if eng == "pe" else 0.0))
                end = t + dur
                tf[eng] = end
                self.op(*a, **kw)
            else:
                tf[eng] = t + 0.05
                end = t + 2.5
                self.dma(*a, **kw)
            for b in kw["rd"]:
                b.tr = max(b.tr, end)
            for b in kw["wr"]:
                b.tw = end
                b.tr = 0.0
                b.weng = eng if kind == "op" else "dma"

    def op(self, eng, fns, rd=(), wr=(), extra=(), cost=None):
        if self.rec is not None:
            self.rec.append(("op", (eng, fns), dict(rd=list(rd), wr=list(wr), extra=list(extra), cost=cost,
                                                    tbl=getattr(self, "_tbl", None))))
            return None
        if not isinstance(fns, (list, tuple)):
            fns = [fns]
        waits = self._deps(eng, rd, wr, extra)
        self.cnt[eng] += 1
        tok = (eng, self.cnt[eng])
        self.q[eng].append((waits, list(fns), ("eng", eng)))
        self._commit(tok, rd, wr)
        return tok

    def dma(self, eng, semname, out, in_, rd=(), wr=(), extra=(), part=False):
        if self.rec is not None:
            self.rec.append(("dma", (eng, semname, out, in_), dict(rd=list(rd), wr=list(wr), extra=list(extra), part=part)))
            return None
        if semname not in self.dsem:
            self.dsem[semname] = [self.es.enter_context(self.nc.semaphore("d_" + semname)), 0]
        if part:
            ex = list(extra)
            for b in wr:
                ex.extend(b.r.items())
                ex.extend((k, v) for k, v in b.w.items() if k != semname)
            waits = self._deps(eng, rd, (), ex)
        else:
            waits = self._deps(eng, rd, wr, extra)
        s = self.dsem[semname]
        s[1] += 16
        tok = (semname, s[1])
        self.q[eng].append((waits, [lambda e: e.dma_start(out=out, in_=in_)], ("dma", semname)))
        if part:
            for b in rd:
                _merge(b.r, tok)
            for b in wr:
                _merge(b.w, tok)
                b.r = {}
        else:
            self._commit(tok, rd, wr)
        return tok

    TBL = {AF.Sigmoid: "sig", AF.Sqrt: "sqrt", AF.Exp: "exp", AF.Ln: "exp", AF.Gelu_apprx_tanh: "gelu", AF.Silu: "silu"}

    def act(self, out, in_, func, rd=(), wr=(), extra=(), cost=None, **kw):
        self._tbl = self.TBL.get(func)
        r = self.op("act", lambda e: e.activation(out=out, in_=in_, func=func, **kw), rd, wr, extra, cost)
        self._tbl = None
        return r

    def tt(self, eng, out, in0, in1, op, rd=(), wr=(), extra=(), cost=None):
        return self.op(eng, lambda e: e.tensor_tensor(out=out, in0=in0, in1=in1, op=op), rd, wr, extra, cost)

    def ts(self, eng, out, in0, s1, s2, op0, op1=None, rd=(), wr=(), extra=(), cost=None):
        if op1 is None:
            return self.op(eng, lambda e: e.tensor_scalar(out=out, in0=in0, scalar1=s1, scalar2=None, op0=op0), rd, wr, extra, cost)
        return self.op(eng, lambda e: e.tensor_scalar(out=out, in0=in0, scalar1=s1, scalar2=s2, op0=op0, op1=op1), rd, wr, extra, cost)

    def stt(self, out, in0, scalar, in1, op0, op1, rd=(), wr=(), extra=(), cost=None, **kw):
        return self.op("dve", lambda e: e.scalar_tensor_tensor(out=out, in0=in0, scalar=scalar, in1=in1, op0=op0, op1=op1, **kw), rd, wr, extra, cost)

    def copy(self, eng, out, in_, rd=(), wr=(), extra=()):
        return self.op(eng, lambda e: e.tensor_copy(out=out, in_=in_), rd, wr, extra)

    def recip(self, out, in_, rd=(), wr=(), extra=(), cost=0.2):
        return self.op("dve", lambda e: e.reciprocal(out=out, in_=in_), rd, wr, extra, cost)

    def memset(self, eng, ap, val, rd=(), wr=(), extra=()):
        return self.op(eng, lambda e: e.memset(ap, val), rd, wr, extra)

    def mms(self, specs, rd=(), wr=(), extra=()):
        fns = []
        for (o, l, r, st, sp) in specs:
            fns.append((lambda o, l, r, st, sp: (lambda e: e.matmul(o, lhsT=l, rhs=r, start=st, stop=sp)))(o, l, r, st, sp))
        return self.op("pe", fns, rd, wr, extra)

    def trs(self, specs, ident, rd=(), wr=(), extra=()):
        fns = []
        for (o, i) in specs:
            fns.append((lambda o, i: (lambda e: e.transpose(out=o, in_=i, identity=ident)))(o, i))
        return self.op("pe", fns, rd, wr, extra)

    def run_phase(self, name):
        fin = [(k, s[1]) for k, s in self.dsem.items() if s[1] > 0]
        self.op("sp", lambda e: e.nop(), extra=fin)
        q = self.q
        self.q = {e: [] for e in ENGS}

        def replay(eng, e):
            for waits, fns, inc in q[eng]:
                for (k, v) in waits:
                    e.wait_ge(self._semobj(k), v)
                ins = None
                for f in fns:
                    ins = f(e)
                if inc[0] == "eng":
                    ins.then_inc(self.sem[eng], 1)
                else:
                    ins.then_inc(self.dsem[inc[1]][0], 16)

        with self.nc.Block() as block:
            @block.tensor
            def _(e):
                replay("pe", e)

            @block.scalar
            def _(e):
                replay("act", e)

            @block.vector
            def _(e):
                replay("dve", e)

            @block.gpsimd
            def _(e):
                replay("pool", e)

            @block.sync
            def _(e):
                replay("sp", e)


class NormT:
    def __init__(self, P, sb, ps, src, gd, identd, NSUB, nh):
        self.P = P
        self.src = src
        self.xin = [sb("xin%d" % i, [128, D], F32) for i in range(2)]
        self.hbf = [sb("hbf%d" % i, [128, D], BF16) for i in range(2)]
        self.ssq = sb("ssq", [128, NSUB], F32)
        self.std = sb("std", [128, NSUB], F32)
        self.rstd = sb("rstd", [128, NSUB], F32)
        self.hT = [sb("hT%d" % i, [128, 8, 512], BF16) for i in range(nh)]
        self.tp = [ps("tp%d" % i, [128, 8, 128], BF16) for i in range(2)]
        self.ident = sb("ident", [128, 128], BF16)
        self.gfm = sb("gfm", [128, 8], F32)
        self.b_xin = [Buf(), Buf()]
        self.b_hbf = [Buf(), Buf()]
        self.b_tp = [Buf(), Buf()]
        self.b_hT = [Buf() for _ in range(nh)]
        self.b_stc = [Buf(), Buf()]
        self.b_c = Buf()
        P.dma("sp", "cst", out=self.gfm[:, :], in_=gd, wr=[self.b_c], part=True)
        P.dma("pool", "cstp", out=self.ident[:, :], in_=identd, wr=[self.b_c], part=True)

    def norm(self, g):
        P = self.P
        sl = g % 2
        xin, hbf = self.xin[sl], self.hbf[sl]
        P.dma("sp", "xin%d" % sl, out=xin[:, :], in_=self.src[g * 128:(g + 1) * 128, :], wr=[self.b_xin[sl]])
        P.act(hbf[:, :], xin[:, :], AF.Square, rd=[self.b_xin[sl]], wr=[self.b_hbf[sl]], accum_out=self.ssq[:, g:g + 1])
        bst = [self.b_stc[sl]]
        P.act(self.std[:, g:g + 1], self.ssq[:, g:g + 1], AF.Sqrt, rd=[self.b_hbf[sl], self.b_c], wr=bst, scale=1.0 / D, bias=self.eps[:, 0:1], cost=0.3)
        P.recip(self.rstd[:, g:g + 1], self.std[:, g:g + 1], wr=bst)
        P.ts("dve", hbf[:, :], xin[:, :], self.rstd[:, g:g + 1], None, ALU.mult, rd=[self.b_xin[sl]] + bst, wr=[self.b_hbf[sl]])

    def transp(self, g, hi):
        P = self.P
        sl = g % 2
        s = g % 4
        hbf, tp = self.hbf[sl], self.tp[sl]
        P.trs([(tp[:, k, :], hbf[:, k * 128:(k + 1) * 128]) for k in range(8)], self.ident[:, :],
              rd=[self.b_hbf[sl], self.b_c], wr=[self.b_tp[sl]])
        P.tt("dve", self.hT[hi][:, :, s * 128:(s + 1) * 128], tp[:, :, :],
             self.gfm[:, :].unsqueeze(2).to_broadcast([128, 8, 128]), ALU.mult,
             rd=[self.b_tp[sl], self.b_c], wr=[self.b_hT[hi]])


def load_cast(P, sem, dst, srcd, nk, ncol, wr):
    npiece = (ncol + 2047) // 2048
    while ncol % npiece:
        npiece += 1
    w = ncol // npiece
    for k in range(nk):
        for p in range(npiece):
            P.dma("pool", sem, out=dst[:, k, p * w:(p + 1) * w], in_=srcd[:, k, p * w:(p + 1) * w], wr=wr, part=True)


def ffn_phase(P, S, src, dst, gd, w1d, w3d, w2d, identd, epsd, fgd, ph):
    nc = P.nc
    NT = S // 512
    NSUB = S // 128
    final = fgd is not None
    with ExitStack() as es:
        def sb(name, shape, dt):
            return es.enter_context(nc.sbuf_tensor("%s_%s" % (ph, name), shape, dt))

        def ps(name, shape, dt):
            return es.enter_context(nc.psum_tensor("%s_%s" % (ph, name), shape, dt))

        W1 = sb("W1", [128, 8, DFF], BF16)
        W3 = sb("W3", [128, 8, DFF], BF16)
        W2 = sb("W2", [128, NFF, D], BF16)
        N = NormT(P, sb, ps, src, gd, identd, NSUB, 1)
        N.eps = sb("eps", [128, 1], F32)
        P.dma("sp", "cst", out=N.eps[:, :], in_=epsd, wr=[N.b_c], part=True)
        sil = [sb("sil%d" % i, [128, 512], F32) for i in range(2)]
        aT = sb("aT", [128, NFF, 512], BF16)
        NX = 3 if final else 2
        xres = [sb("xres%d" % i, [128, D], F32) for i in range(NX)]
        if final:
            fg = sb("fg", [128, D], F32)
            junk2 = sb("junk2", [128, D], BF16)
            ssq2 = sb("ssq2", [128, NSUB], F32)
            std2 = sb("std2", [128, NSUB], F32)
            rstd2 = sb("rstd2", [128, NSUB], F32)
            P.dma("sp", "cst", out=fg[:, :], in_=fgd, wr=[N.b_c], part=True)
        pa = [ps("pa%d" % i, [128, 512], F32) for i in range(2)]
        pb = [ps("pb%d" % i, [128, 512], F32) for i in range(2)]
        py = [ps("py%d" % i, [128, 512], F32) for i in range(2)]
        b_w2 = Buf()
        b_pa, b_pb, b_py = [Buf(), Buf()], [Buf(), Buf()], [Buf(), Buf()]
        b_sil = [Buf(), Buf()]
        b_aT = [Buf() for _ in range(NFF)]
        b_xres = [Buf() for _ in range(NX)]
        b_j2 = Buf()

        HW_ = DFF // 2
        b_wg = [Buf(), Buf()]
        for gi in range(2):
            for Wt, wd in ((W1, w1d), (W3, w3d)):
                for k in range(8):
                    P.dma("pool", "w13%d" % gi, out=Wt[:, k, gi * HW_:(gi + 1) * HW_], in_=wd[:, k, gi * HW_:(gi + 1) * HW_],
                          wr=[b_wg[gi]], part=True)
        load_cast(P, "w2", W2, w2d, NFF, D, [b_w2])

        def stage1(i):
            hT = N.hT[0]
            for f in range(NFF):
                sl = f % 2
                bw = b_wg[0] if f < NFF // 2 else b_wg[1]
                P.mms([(pa[sl][:, :], W1[:, k, f * 128:(f + 1) * 128], hT[:, k, :], k == 0, k == 7) for k in range(8)],
                      rd=[N.b_hT[0], bw], wr=[b_pa[sl]])
                P.mms([(pb[sl][:, :], W3[:, k, f * 128:(f + 1) * 128], hT[:, k, :], k == 0, k == 7) for k in range(8)],
                      rd=[N.b_hT[0], bw], wr=[b_pb[sl]])
                P.act(sil[sl][:, :], pa[sl][:, :], AF.Silu, rd=[b_pa[sl]], wr=[b_sil[sl]])
                P.tt("dve", aT[:, f, :], sil[sl][:, :], pb[sl][:, :], ALU.mult, rd=[b_sil[sl], b_pb[sl]], wr=[b_aT[f]])

        def fin_store(g):
            sl = g % NX
            xr = xres[sl]
            if final:
                P.act(junk2[:, :], xr[:, :], AF.Square, rd=[b_xres[sl]], wr=[b_j2], accum_out=ssq2[:, g:g + 1])
                t1 = P.act(std2[:, g:g + 1], ssq2[:, g:g + 1], AF.Sqrt, rd=[b_j2, N.b_c], scale=1.0 / D, bias=N.eps[:, 0:1])
                t2 = P.recip(rstd2[:, g:g + 1], std2[:, g:g + 1], extra=[t1])
                P.stt(xr[:, :], xr[:, :], rstd2[:, g:g + 1], fg[:, :], ALU.mult, ALU.mult, rd=[N.b_c], wr=[b_xres[sl]], extra=[t2])
            P.dma("sp", "st%d" % sl, out=dst[g * 128:(g + 1) * 128, :], in_=xr[:, :], rd=[b_xres[sl]])

        def stage2(i, s):
            g = 4 * i + s
            sl = g % NX
            xr = xres[sl]
            P.dma("sp", "xr%d" % sl, out=xr[:, :], in_=src[g * 128:(g + 1) * 128, :], wr=[b_xres[sl]])
            for h in range(2):
                P.mms([(py[h][:, :], aT[:, f, s * 128:(s + 1) * 128], W2[:, f, h * 512:(h + 1) * 512], f == 0, f == NFF - 1)
                       for f in range(NFF)], rd=b_aT + [b_w2], wr=[b_py[h]])
                P.stt(xr[:, h * 512:(h + 1) * 512], py[h][:, :], 0.5, xr[:, h * 512:(h + 1) * 512], ALU.mult, ALU.add,
                      rd=[b_py[h]], wr=[b_xres[sl]])
            if final:
                if g > 0:
                    fin_store(g - 1)
            else:
                fin_store(g)

        for s in range(4):
            N.norm(s)
            N.transp(s, 0)
        for i in range(NT):
            stage1(i)
            for s in range(4):
                if i + 1 < NT:
                    N.norm(4 * (i + 1) + s)
                stage2(i, s)
                if i + 1 < NT:
                    N.transp(4 * (i + 1) + s, 0)
        if final:
            fin_store(4 * NT - 1)
        P.run_phase(ph)


def b0_phase(P, dr, Wo, wsT, B2c, b_setup, Win):
    nc = P.nc
    with ExitStack() as es:
        def sb(name, shape, dt):
            return es.enter_context(nc.sbuf_tensor("B0_" + name, shape, dt))
        wost = sb("wost", [128, 8, D], F32)
        ogfm = sb("ogfm", [128, 8], F32)
        wsf = sb("wsf", [128, 8, 128], F32)
        smk = sb("smk", [128, 128], F32)
        lnb = sb("lnb", [128, 8, 64], F32)
        bsT = sb("bsT", [128, 8], F32)
        rs = sb("rs", [128, 8], F32)
        onesb = sb("onesb", [128, 1], BF16)
        prs = es.enter_context(nc.psum_tensor("B0_prs", [128, 8], F32))
        b_in, b_o, b_ws, b_prs, b_rs = Buf(), Buf(), Buf(), Buf(), Buf()
        load_cast(P, "w13", Win, dr["win"], 8, INC, [Buf()])
        P.dma("sp", "c0", out=wost[:, :, :], in_=dr["wo"], wr=[b_in], part=True)
        for nm, t, ap in (("og", ogfm, None), ("ws", wsf, None), ("smask", smk, None), ("lnb", lnb, None), ("bsT", bsT, None)):
            if nm == "lnb":
                P.dma("sp", "c0", out=t[:, :, :], in_=dr[nm].rearrange("p (h c) -> p h c", c=64), wr=[b_in], part=True)
            elif nm == "ws":
                P.dma("sp", "c0", out=t[:, :, :], in_=dr[nm], wr=[b_in], part=True)
            else:
                P.dma("sp", "c0", out=t[:, :], in_=dr[nm], wr=[b_in], part=True)
        for kc in range(8):
            P.ts("dve", Wo[:, kc, :], wost[:, kc, :], ogfm[:, kc:kc + 1], None, ALU.mult,
                 rd=[b_in], wr=[b_setup])
        P.tt("dve", wsT[:, :, :], wsf[:, :, :], smk[:, :].unsqueeze(1).to_broadcast([128, 8, 128]), ALU.mult,
             rd=[b_in], wr=[b_ws])
        P.memset("dve", onesb[:, :], 1.0, wr=[b_o])
        P.mms([(prs[:, h:h + 1], wsT[:, h, :], onesb[:, 0:1], True, True) for h in range(8)], rd=[b_ws, b_o], wr=[b_prs])
        P.copy("dve", rs[:, :], prs[:, :], rd=[b_prs], wr=[b_rs])
        P.tt("dve", B2c[:, :, :], lnb[:, :, :], rs[:, :].unsqueeze(2).to_broadcast([128, 8, 64]), ALU.mult,
             rd=[b_in, b_rs], wr=[b_setup])
        P.tt("dve", B2c[:, :, :], B2c[:, :, :], bsT[:, :].unsqueeze(2).to_broadcast([128, 8, 64]), ALU.add,
             rd=[b_in], wr=[b_setup])
        P.run_phase("B0")


def b1_phase(P, S, src, qTd, kTd, vd, ysTd, dr, wsT, B2c, b_setup, Win):
    nc = P.nc
    NCH = S // 512
    NSUB = S // 128
    with ExitStack() as es:
        def sb(name, shape, dt):
            return es.enter_context(nc.sbuf_tensor("B1_" + name, shape, dt))

        def ps(name, shape, dt):
            return es.enter_context(nc.psum_tensor("B1_" + name, shape, dt))

        N = NormT(P, sb, ps, src, dr["mg"], dr["ident"], NSUB, 2)
        N.eps = sb("eps", [128, 1], F32)
        P.dma("sp", "cst", out=N.eps[:, :], in_=dr["epsc"], wr=[N.b_c], part=True)
        lng = sb("lng", [128, 512], F32)
        P.dma("sp", "cst", out=lng[:, :], in_=dr["lng"], wr=[N.b_c], part=True)
        fb = sb("fb", [8, 1], F32)
        nfb = sb("nfb", [8, 1], F32)
        one8 = sb("one8", [8, 1], F32)
        P.dma("sp", "cst", out=fb[:, :], in_=dr["fb"], wr=[N.b_c], part=True)
        P.ts("dve", nfb[:, :], fb[:, :], -1.0, None, ALU.mult, rd=[N.b_c], wr=[N.b_c])
        P.memset("dve", one8[:, :], 1.0, wr=[N.b_c])
        qst = sb("qst", [128, 4, 512], BF16)
        kst = sb("kst", [128, 4, 512], BF16)
        vst = sb("vst", [128, 4, 4, 192], BF16)
        yst = sb("yst", [128, 4, 512], BF16)
        augq = sb("augq", [8, 6, 512], BF16)
        augk = sb("augk", [8, 6, 512], BF16)
        ef = sb("ef", [8, 512], F32)
        lf = sb("lf", [8, 512], F32)
        Fc = [sb("Fc%d" % i, [8, 512], F32) for i in range(2)]
        tmpU = [sb("tmpU%d" % i, [128, 512], F32) for i in range(3)]
        tmpV = [sb("tmpV%d" % i, [128, 512], F32) for i in range(3)]
        gu = [sb("gu%d" % i, [128, 512], F32) for i in range(3)]
        gv = [sb("gv%d" % i, [128, 512], F32) for i in range(3)]
        nbf = [sb("nbf%d" % i, [128, 8, 64], BF16) for i in range(3)]
        yn = [sb("yn%d" % i, [128, 512], BF16) for i in range(2)]
        stn = ("vsum", "vssq", "mu", "m2", "var", "sdv", "rstdv", "nmr")
        st = {n: sb(n, [128, NSUB], F32) for n in stn}
        gss = sb("gss", [128, NSUB, 8], F32)
        gsd = sb("gsd", [128, NSUB, 8], F32)
        gr = sb("gr", [128, NSUB, 8], F32)
        pA = [ps("pA%d" % i, [128, 512], F32) for i in range(2)]
        pB = [ps("pB%d" % i, [128, 512], F32) for i in range(2)]
        pM = ps("pM", [128, 8, 64], F32)
        pT = ps("pT", [128, 4, 128], BF16)
        b_win = Buf()
        b_pA, b_pB = [Buf(), Buf()], [Buf(), Buf()]
        b_pM, b_pT = Buf(), Buf()
        b_qst, b_kst, b_vst, b_yst, b_augq, b_augk = Buf(), Buf(), Buf(), Buf(), Buf(), Buf()
        b_ef, b_lf = Buf(), Buf()
        b_Fc = [Buf(), Buf()]
        b_tmpU, b_tmpV, b_gu, b_gv = ([Buf() for _ in range(3)] for _ in range(4))
        b_nbf, b_yn, b_stat = [Buf() for _ in range(3)], [Buf(), Buf()], [Buf() for _ in range(3)]

        P.memset("dve", vst[:, :, :, :], 1.0, wr=[b_vst])
        P.memset("dve", augq[:, :, :], 1.0, wr=[b_augq])
        P.memset("dve", augk[:, :, :], 1.0, wr=[b_augk])
        P.memset("dve", Fc[1][:, :], 0.0, wr=[b_Fc[1]])

        def v3(t):
            return t[:, :].rearrange("p (h c) -> p h c", c=64)

        def proj_qkf(c):
            cc = c % 2
            hT = N.hT[cc]
            c0, c1 = c * 512, (c + 1) * 512
            for j in range(8):
                pbj = pB[j % 2]
                P.mms([(pbj[:, :], Win[:, k, 1024 + j * 128:1024 + (j + 1) * 128], hT[:, k, :], k == 0, k == 7) for k in range(8)],
                      rd=[N.b_hT[cc], b_win], wr=[b_pB[j % 2]])
                if j < 4:
                    P.act(qst[:, j, :], pbj[:, :], AF.Copy, rd=[b_pB[j % 2]], wr=[b_qst], scale=0.125)
                else:
                    P.copy("dve", kst[:, j - 4, :], pbj[:, :], rd=[b_pB[j % 2]], wr=[b_kst])
            for h in range(8):
                r0 = (h % 2) * 64
                P.dma("sp", "qd", out=qTd[h, 0:64, c0:c1], in_=qst[r0:r0 + 64, h // 2, :], rd=[b_qst])
                P.dma("sp", "kd", out=kTd[h, 0:64, c0:c1], in_=kst[r0:r0 + 64, h // 2, :], rd=[b_kst])
            P.mms([(pB[0][0:8, :], Win[:, k, 2560:2568], hT[:, k, :], k == 0, k == 7) for k in range(8)],
                  rd=[N.b_hT[cc], b_win], wr=[b_pB[0]])
            P.act(ef[:, :], pB[0][0:8, :], AF.Exp, rd=[b_pB[0], N.b_c], wr=[b_ef], scale=-1.0, bias=nfb[:, 0:1])
            P.act(lf[:, :], ef[:, :], AF.Ln, rd=[b_ef], wr=[b_lf], bias=one8[:, 0:1])
            fcc, fpp = Fc[cc], Fc[1 - cc]
            P.op("dve", lambda e: e.tensor_tensor_scan(out=fcc[:, :], data0=one8[:, 0:1].to_broadcast([8, 512]), data1=lf[:, :],
                                                       initial=fpp[:, 511:512], op0=ALU.mult, op1=ALU.subtract),
                 rd=[b_lf, b_Fc[1 - cc], N.b_c], wr=[b_Fc[cc]])
            P.copy("dve", augq[:, 0, :], fcc[:, :], rd=[b_Fc[cc]], wr=[b_augq])
            P.tt("dve", ef[:, :], fcc[:, :], augq[:, 0, :], ALU.subtract, rd=[b_Fc[cc], b_augq], wr=[b_ef])
            P.copy("dve", augq[:, 1, :], ef[:, :], rd=[b_ef], wr=[b_augq])
            P.tt("dve", lf[:, :], ef[:, :], augq[:, 1, :], ALU.subtract, rd=[b_ef, b_augq], wr=[b_lf])
            P.copy("dve", augq[:, 2, :], lf[:, :], rd=[b_lf], wr=[b_augq])
            P.ts("dve", augk[:, 3:6, :], augq[:, 0:3, :], -1.0, None, ALU.mult, rd=[b_augq], wr=[b_augk])
            P.dma("sp", "qa", out=qTd[:, 64:70, c0:c1], in_=augq[:, :, :], rd=[b_augq])
            P.dma("sp", "ka", out=kTd[:, 64:70, c0:c1], in_=augk[:, :, :], rd=[b_augk])

        def proj_v(c):
            cc = c % 2
            hT = N.hT[cc]
            for s in range(4):
                pa_ = pA[s % 2]
                P.mms([(pa_[:, :], hT[:, k, s * 128:(s + 1) * 128], Win[:, k, 2048:2560], k == 0, k == 7) for k in range(8)],
                      rd=[N.b_hT[cc], b_win], wr=[b_pA[s % 2]])
                pv4 = pa_[:, :].rearrange("p (j hh d) -> p j hh d", hh=2, d=64)
                P.copy("dve", vst[:, s, :, 0:64], pv4[:, :, 0, :], rd=[b_pA[s % 2]], wr=[b_vst])
                P.act(vst[:, s, :, 128:192], pv4[:, :, 1, :], AF.Copy, rd=[b_pA[s % 2]], wr=[b_vst])
            P.dma("sp", "vd", out=vd[:, c * 4:(c + 1) * 4, :],
                  in_=vst[:, :, :, :].rearrange("p s j f -> p s (j f)"), rd=[b_vst])

        def gelu(Xp, b_X, tmp, b_tmp, res, b_res, accum):
            if accum is None:
                P.act(res[:, :], Xp[:, :], AF.Gelu_apprx_tanh, rd=[b_X], wr=[b_res])
            else:
                P.act(res[:, :], Xp[:, :], AF.Gelu_apprx_tanh, rd=[b_X], wr=[b_res, accum[1]], accum_out=accum[0])

        def sgu1(c, s):
            cc = c % 2
            hT = N.hT[cc]
            g = 4 * c + s
            w = g % 3
            gc = slice(g, g + 1)
            P.mms([(pA[0][:, :], hT[:, k, s * 128:(s + 1) * 128], Win[:, k, 0:512], k == 0, k == 7) for k in range(8)],
                  rd=[N.b_hT[cc], b_win], wr=[b_pA[0]])
            P.mms([(pA[1][:, :], hT[:, k, s * 128:(s + 1) * 128], Win[:, k, 512:1024], k == 0, k == 7) for k in range(8)],
                  rd=[N.b_hT[cc], b_win], wr=[b_pA[1]])
            gelu(pA[0], b_pA[0], tmpU[w], b_tmpU[w], gu[w], b_gu[w], None)
            gelu(pA[1], b_pA[1], tmpV[w], b_tmpV[w], gv[w], b_gv[w], (st["vsum"][:, gc], b_stat[w]))

        def sgu1b(c, s):
            g = 4 * c + s
            w = g % 3
            gc = slice(g, g + 1)
            bs = [b_stat[w]]
            P.act(tmpV[w][:, :], gv[w][:, :], AF.Square, rd=[b_gv[w]], wr=[b_tmpV[w]] + bs, accum_out=st["vssq"][:, gc])
            P.ts("dve", st["mu"][:, gc], st["vsum"][:, gc], 1.0 / 512, None, ALU.mult, wr=bs, cost=0.25)
            P.tt("dve", st["m2"][:, gc], st["mu"][:, gc], st["mu"][:, gc], ALU.mult, wr=bs, cost=0.25)
            P.stt(st["var"][:, gc], st["vssq"][:, gc], 1.0 / 512, st["m2"][:, gc], ALU.mult, ALU.subtract, wr=bs, cost=0.25)
            P.act(st["sdv"][:, gc], st["var"][:, gc], AF.Sqrt, rd=[N.b_c], wr=bs, bias=N.eps[:, 0:1], cost=0.25)
            P.recip(st["rstdv"][:, gc], st["sdv"][:, gc], wr=bs)
            P.stt(st["nmr"][:, gc], st["mu"][:, gc], -1.0, st["rstdv"][:, gc], ALU.mult, ALU.mult, wr=bs, cost=0.25)
            P.act(nbf[w][:, :, :], v3(gv[w]), AF.Identity, rd=[b_gv[w]] + bs, wr=[b_nbf[w]],
                  scale=st["rstdv"][:, gc], bias=st["nmr"][:, gc])

        def sgu2(c, s):
            g = 4 * c + s
            w = g % 3
            y = g % 2
            bs = [b_stat[w]]
            P.mms([(pM[:, h, :], wsT[:, h, :], nbf[w][:, h, :], True, True) for h in range(8)],
                  rd=[b_nbf[w], b_setup], wr=[b_pM])
            P.tt("dve", v3(tmpU[w]), pM[:, :, :], v3(lng), ALU.mult, rd=[b_pM, N.b_c], wr=[b_tmpU[w]])
            P.tt("dve", v3(tmpU[w]), v3(tmpU[w]), B2c[:, :, :], ALU.add, rd=[b_setup], wr=[b_tmpU[w]])
            P.tt("dve", tmpU[w][:, :], tmpU[w][:, :], gu[w][:, :], ALU.mult, rd=[b_gu[w]], wr=[b_tmpU[w]])
            P.tt("dve", tmpV[w][:, :], tmpU[w][:, :], tmpU[w][:, :], ALU.mult, rd=[b_tmpU[w]], wr=[b_tmpV[w]])
            tv3 = v3(tmpV[w])
            P.op("dve", lambda e: e.tensor_reduce(out=gss[:, g, :], in_=tv3, axis=AX.X, op=ALU.add), rd=[b_tmpV[w]], wr=bs)
            P.act(gsd[:, g, :], gss[:, g, :], AF.Sqrt, rd=[N.b_c], wr=bs, scale=1.0 / 64, bias=N.eps[:, 0:1], cost=0.25)
            P.recip(gr[:, g, :], gsd[:, g, :], wr=bs)
            P.tt("dve", v3(yn[y]), v3(tmpU[w]), gr[:, g, :].unsqueeze(2).to_broadcast([128, 8, 64]), ALU.mult,
                 rd=[b_tmpU[w]] + bs, wr=[b_yn[y]])

        def sgu3(c, s):
            y = (4 * c + s) % 2
            P.trs([(pT[:, kc, :], yn[y][:, kc * 128:(kc + 1) * 128]) for kc in range(4)], N.ident[:, :],
                  rd=[b_yn[y], N.b_c], wr=[b_pT])
            P.act(yst[:, :, s * 128:(s + 1) * 128], pT[:, :, :], AF.Copy, rd=[b_pT], wr=[b_yst])
            if s == 3:
                P.dma("sp", "yd", out=ysTd.rearrange("(kc p) n -> p kc n", p=128)[:, :, c * 512:(c + 1) * 512],
                      in_=yst[:, :, :], rd=[b_yst])

        for s in range(4):
            N.norm(s)
            N.transp(s, 0)
        NG = 4 * NCH

        def stream_a(t):
            c, s_ = divmod(t, 4)
            if s_ == 0:
                proj_qkf(c)
                proj_v(c)
            if c + 1 < NCH:
                N.norm(4 * (c + 1) + s_)
            sgu1(c, s_)
            if c + 1 < NCH:
                N.transp(4 * (c + 1) + s_, (c + 1) % 2)

        for t in range(NG + 3):
            lists = []
            if t < NG:
                lists.append(P.record(stream_a, t))
            for d_, fn_ in ((1, sgu1b), (2, sgu2), (3, sgu3)):
                if 0 <= t - d_ < NG:
                    lists.append(P.record(fn_, (t - d_) // 4, (t - d_) % 4))
            P.play(lists)
        P.run_phase("B1")


def b2_phase(P, S, x1, x2, qTd, kTd, vd, ysTd, dr, Wo, b_setup):
    nc = P.nc
    NCH = S // 512
    NSUB = S // 128
    with ExitStack() as es:
        def sb(name, shape, dt):
            return es.enter_context(nc.sbuf_tensor("B2_" + name, shape, dt))

        def ps(name, shape, dt):
            return es.enter_context(nc.psum_tensor("B2_" + name, shape, dt))

        Kst = sb("Kst", [128, 8, S], BF16)
        Vst = sb("Vst", [128, NSUB, 768], BF16)
        qc = [sb("qc%d" % i, [128, 8, 512], BF16) for i in range(2)]
        ysg = [sb("ysg%d" % i, [128, 4, 512], BF16) for i in range(2)]
        yfox = [sb("yfox%d" % i, [128, 4, 512], BF16) for i in range(2)]
        PT = [sb("PT%d" % i, [128, 512], BF16) for i in range(4)]
        Ocp = [sb("Ocp%d" % i, [128, 512], BF16) for i in range(2)]
        X = [sb("X%d" % i, [128, 512], BF16) for i in range(2)]
        lnn = sb("lnn", [128, 512], F32)
        rr = sb("rr", [128, 512], F32)
        xres = [sb("xres%d" % i, [128, D], F32) for i in range(2)]
        tri = sb("tri", [128, 128], BF16)
        idb = sb("idb", [128, 128], BF16)
        b2m = [sb("b2m%d" % i, [128, 128], BF16) for i in range(2)]
        pS = [ps("pS%d" % i, [128, 512], F32) for i in range(4)]
        pO = [ps("pO%d" % i, [128, 512], F32) for i in range(2)]
        pN = ps("pN", [128, 512], F32)
        pY = [ps("pY0", [128, 512], F32)] * 2
        b_cst = Buf()
        b_qc, b_ysg, b_yfox = [Buf(), Buf()], [Buf(), Buf()], [Buf(), Buf()]
        b_PT = [Buf() for _ in range(4)]
        b_Ocp, b_X = [Buf(), Buf()], [Buf(), Buf()]
        b_lnn, b_rr = Buf(), Buf()
        b_xres = [Buf(), Buf()]
        b_pS = [Buf() for _ in range(4)]
        b_pO = [Buf(), Buf()]
        b_pN = Buf()
        b_pY = [Buf()] * 2

        P.dma("pool", "c2", out=tri[:, :], in_=dr["nmask"], wr=[b_cst], part=True)
        P.dma("pool", "c2", out=idb[:, :], in_=dr["ident"], wr=[b_cst], part=True)
        P.dma("pool", "c2", out=b2m[0][:, :], in_=dr["b2a"], wr=[b_cst], part=True)
        P.dma("pool", "c2", out=b2m[1][:, :], in_=dr["b2b"], wr=[b_cst], part=True)
        b_Kh = [Buf() for _ in range(8)]
        b_Kl = [Buf() for _ in range(8)]
        nv = max(1, NSUB // 8)
        b_Vp = [Buf() for _ in range(0, NSUB, nv)]
        for i in range(2):
            P.memset("dve", qc[i][64:128, :, :].bitcast(F32), 0.0, wr=[b_qc[i]])
        for h in range(8):
            P.memset("dve", Kst[64:128, h, :].bitcast(F32), 0.0, wr=[b_Kh[h]])
        def vlhsT(h, j):
            o = (h // 2) * 192 + (h % 2) * 64
            return Vst[:, j, o:o + 128]

        def load_chunk(c):
            cc = c % 2
            P.dma("sp", "qc%d" % cc, out=qc[cc][0:70, :, :], in_=qTd[:, :, c * 512:(c + 1) * 512].rearrange("h r n -> r h n"),
                  wr=[b_qc[cc]])
            P.dma("sp", "ys%d" % cc, out=ysg[cc][:, :, :],
                  in_=ysTd.rearrange("(kc p) n -> p kc n", p=128)[:, :, c * 512:(c + 1) * 512], wr=[b_ysg[cc]])

        load_chunk(0)
        P.dma("sp", "vv0", out=Vst[:, 0:nv, :], in_=vd[:, 0:nv, :], wr=[b_Vp[0]])
        for h in range(8):
            P.dma("sp", "kl%d" % h, out=Kst[0:64, h, :], in_=kTd[h, 0:64, :], wr=[b_Kl[h]])
            P.dma("sp", "kh%d" % h, out=Kst[64:70, h, :], in_=kTd[h, 64:70, :], wr=[b_Kh[h]])
        for h in range(1, len(b_Vp)):
            i = h * nv
            P.dma("sp", "vv%d" % h, out=Vst[:, i:i + nv, :], in_=vd[:, i:i + nv, :], wr=[b_Vp[h]])

        def epi_a(u):
            P.copy("dve", Ocp[u % 2][:, :], pO[u % 2][:, :], rd=[b_pO[u % 2]], wr=[b_Ocp[u % 2]])
            P.tt("dve", X[u % 2][:, :], Ocp[u % 2][:, :], Ocp[u % 2][:, :], ALU.mult, rd=[b_Ocp[u % 2]], wr=[b_X[u % 2]])

        def epi_b(u, c, h):
            cc = c % 2
            R = slice(0, 64) if h % 2 == 0 else slice(64, 128)
            P.mms([(pN[:, :], b2m[h % 2][:, :], X[u % 2][:, :], True, True)], rd=[b_X[u % 2], b_cst], wr=[b_pN])
            P.act(lnn[R, :], pN[R, :], AF.Ln, rd=[b_pN], wr=[b_lnn])
            P.act(rr[R, :], lnn[R, :], AF.Exp, rd=[b_lnn], wr=[b_rr], scale=-0.5)
            P.tt("dve", yfox[cc][R, h // 2, :], Ocp[u % 2][R, :], rr[R, :], ALU.mult, rd=[b_Ocp[u % 2], b_rr], wr=[b_yfox[cc]])

        def w_out_parts(c):
            cc = c % 2
            parts = []
            for s in range(4):
                g = 4 * c + s
                sl = g % 2
                xr = xres[sl]
                for hf in range(2):
                    for q in range(4):
                        def part(s=s, g=g, sl=sl, xr=xr, hf=hf, q=q):
                            if hf == 0 and q == 0:
                                P.dma("sp", "xr%d" % sl, out=xr[:, :], in_=x1[g * 128:(g + 1) * 128, :], wr=[b_xres[sl]])
                            P.mms([(pY[hf][:, :], (ysg[cc] if kc < 4 else yfox[cc])[:, kc % 4, s * 128:(s + 1) * 128],
                                    Wo[:, kc, hf * 512:(hf + 1) * 512], kc == 0, kc == 7) for kc in (2 * q, 2 * q + 1)],
                                  rd=[b_ysg[cc], b_yfox[cc], b_setup], wr=[b_pY[hf]])
                            if q == 3:
                                P.tt("dve", xr[:, hf * 512:(hf + 1) * 512], pY[hf][:, :], xr[:, hf * 512:(hf + 1) * 512],
                                     ALU.add, rd=[b_pY[hf]], wr=[b_xres[sl]])
                                if hf == 1:
                                    P.dma("sp", "st%d" % sl, out=x2[g * 128:(g + 1) * 128, :], in_=xr[:, :], rd=[b_xres[sl]])
                        parts.append(part)
            return parts

        wq = []

        def unit(c, h, u, sl, pend=(), nw=0):
            cc = c % 2
            njt = 4 * c + 4

            def s_mm(j):
                q0 = max(0, j - 4 * c) * 128
                k_ = 2 * sl + j % 2
                mm = [(pS[k_][:, q0:512], Kst[:, h, j * 128:(j + 1) * 128], qc[cc][:, h, q0:512], True, j < 4 * c)]
                if j >= 4 * c:
                    mm.append((pS[k_][:, q0:q0 + 128], idb[:, :], tri[:, :], False, True))
                P.mms(mm, rd=[b_Kl[h], b_Kh[h], b_qc[cc], b_cst], wr=[b_pS[k_]])

            def e_pv(j):
                r = j - 4 * c
                q0 = max(0, r) * 128
                k_ = 2 * sl + j % 2
                P.act(PT[k_][:, q0:512], pS[k_][:, q0:512], AF.Exp, rd=[b_pS[k_]], wr=[b_PT[k_]])
                P.mms([(pO[u % 2][:, q0:512], vlhsT(h, j), PT[k_][:, q0:512], j == 0, j == njt - 1)],
                      rd=[b_PT[k_], b_Vp[j // nv]], wr=[b_pO[u % 2]])

            for j in range(min(2, njt)):
                s_mm(j)
            for j in range(njt):
                e_pv(j)
                if j + 2 < njt:
                    s_mm(j + 2)
                if j == min(2, njt - 1):
                    for f in pend:
                        f()
                if j >= 1 and nw > 0 and wq:
                    wq.pop(0)()
                    nw -= 1
            while nw > 0 and wq:
                wq.pop(0)()
                nw -= 1

        u = 0
        pending = []
        for c in range(NCH):
            for p in range(4):
                nw = 0 if p == 0 else (11 if p < 3 else 99)

                lists = [P.record(unit, c, 2 * p, u, 0, pending, nw), P.record(unit, c, 2 * p + 1, u + 1, 1)]
                P.play(lists)
                epi_a(u)
                epi_a(u + 1)
                pending = [(lambda u_=u, c_=c, h_=2 * p: epi_b(u_, c_, h_)),
                           (lambda u_=u + 1, c_=c, h_=2 * p + 1: epi_b(u_, c_, h_))]
                if p == 0 and c > 0:
                    wq.extend(w_out_parts(c - 1))
                if p == 3 and c + 1 < NCH:
                    load_chunk(c + 1)
                u += 2
        for f in pending:
            f()
        for f in w_out_parts(NCH - 1):
            f()
        P.run_phase("B2")


def build_nc(S=4096, upto="C"):
    nc = bass.Bass("TRN2", target_bir_lowering=False)
    NSUB = S // 128

    def din(name, shape):
        return nc.dram_tensor(name, shape, F32, kind="ExternalInput").ap()

    def dint(name, shape, dt):
        return nc.dram_tensor(name, shape, dt, kind="Internal").ap()

    x = din("x", [S, D])
    f1g = din("f1g", [128, 8])
    f1w1 = din("f1w1", [128, 8, DFF])
    f1w3 = din("f1w3", [128, 8, DFF])
    f1w2 = din("f1w2", [128, NFF, D])
    f2g = din("f2g", [128, 8])
    f2w1 = din("f2w1", [128, 8, DFF])
    f2w3 = din("f2w3", [128, 8, DFF])
    f2w2 = din("f2w2", [128, NFF, D])
    fg = din("fg", [128, D])
    ident = din("ident", [128, 128])
    epsd = din("epsc", [128, 1])
    dr = {"ident": ident, "epsc": epsd}
    for nm, shp in (("mg", [128, 8]), ("win", [128, 8, INC]), ("fb", [8, 1]), ("lng", [128, 512]), ("lnb", [128, 512]),
                    ("ws", [128, 8, 128]), ("bsT", [128, 8]), ("og", [128, 8]), ("wo", [128, 8, D]),
                    ("nmask", [128, 128]), ("smask", [128, 128]), ("b2a", [128, 128]), ("b2b", [128, 128])):
        dr[nm] = din(nm, shp)
    out = nc.dram_tensor("out", [S, D], F32, kind="ExternalOutput").ap()
    x1 = dint("x1s", [S, D], F32)
    x2 = dint("x2s", [S, D], F32)
    qTd = dint("qTd", [8, 70, S], BF16)
    kTd = dint("kTd", [8, 70, S], BF16)
    vd = dint("vd", [128, NSUB, 768], BF16)
    ysTd = dint("ysTd", [512, S], BF16)
    with ExitStack() as es:
        P = Prog(nc, es)
        ffn_phase(P, S, x, out if upto == "A" else x1, f1g, f1w1, f1w3, f1w2, ident, epsd, None, "A")
        if upto in ("B", "C"):
            with ExitStack() as es2:
                Wo = es2.enter_context(nc.sbuf_tensor("Wo", [128, 8, D], BF16))
                wsT = es2.enter_context(nc.sbuf_tensor("wsT", [128, 8, 128], BF16))
                B2c = es2.enter_context(nc.sbuf_tensor("B2c", [128, 8, 64], F32))
                with ExitStack() as es3:
                    Win = es3.enter_context(nc.sbuf_tensor("Win", [128, 8, INC], BF16))
                    b_setup = Buf()
                    b0_phase(P, dr, Wo, wsT, B2c, b_setup, Win)
                    b_setup = Buf()
                    b1_phase(P, S, x1, qTd, kTd, vd, ysTd, dr, wsT, B2c, b_setup, Win)
                b_setup = Buf()
                b2_phase(P, S, x1, out if upto == "B" else x2, qTd, kTd, vd, ysTd, dr, Wo, b_setup)
        if upto == "AC":
            ffn_phase(P, S, x1, out, f2g, f2w1, f2w3, f2w2, ident, epsd, fg, "C")
        if upto == "C":
            ffn_phase(P, S, x2, out, f2g, f2w1, f2w3, f2w2, ident, epsd, fg, "C")
    return nc


def fm(v, n):
    return np.ascontiguousarray(np.asarray(v, np.float32).reshape(n, 128).T)


def wl(w, n):
    w = np.asarray(w, np.float32)
    return np.ascontiguousarray(w.reshape(n, 128, w.shape[1]).transpose(1, 0, 2))


def make_inmaps(inp, S, ncores):
    g = lambda k: np.asarray(inp[k], np.float32)
    pi = np.arange(128)
    b2a = np.zeros((128, 128), np.float32)
    b2a[:64, :64] = 1.0 / 64
    b2a[64:, :64] = EPS / 64
    b2b = np.zeros((128, 128), np.float32)
    b2b[64:, 64:] = 1.0 / 64
    b2b[:64, 64:] = EPS / 64
    shared = {
        "f1g": fm(g("ffn1_norm_g")[0], 8), "f1w1": wl(g("ffn1_w1")[0], 8), "f1w3": wl(g("ffn1_w3")[0], 8),
        "f1w2": wl(g("ffn1_w2")[0], NFF),
        "f2g": fm(g("ffn2_norm_g")[0], 8), "f2w1": wl(g("ffn2_w1")[0], 8), "f2w3": wl(g("ffn2_w3")[0], 8),
        "f2w2": wl(g("ffn2_w2")[0], NFF),
        "fg": np.ascontiguousarray(np.broadcast_to(g("final_norm_g")[None, :], (128, D))),
        "ident": np.eye(128, dtype=np.float32),
        "epsc": np.full((128, 1), EPS, np.float32),
        "mg": fm(g("mix_norm_g")[0], 8), "win": wl(g("w_in")[0], 8),
        "fb": np.ascontiguousarray(g("fox_f_bias")[0].reshape(8, 1)),
        "lng": np.ascontiguousarray(np.broadcast_to(g("sgu_ln_g")[0][None, :], (128, 512))),
        "lnb": np.ascontiguousarray(np.broadcast_to(g("sgu_ln_b")[0][None, :], (128, 512))),
        "ws": np.ascontiguousarray(g("sgu_w_s")[0].transpose(2, 0, 1)),
        "bsT": np.ascontiguousarray(g("sgu_b_s")[0].T),
        "og": fm(g("mix_out_g")[0], 8), "wo": wl(g("w_out")[0], 8),
        "nmask": -30000.0 * (pi[:, None] > pi[None, :]).astype(np.float32),
        "smask": ((pi[None, :] // 64) >= (pi[:, None] // 64)).astype(np.float32),
        "b2a": b2a, "b2b": b2b,
    }
    xs = g("x")
    maps = []
    for c in range(ncores):
        m = dict(shared)
        m["x"] = np.ascontiguousarray(xs[c, :S, :])
        maps.append(m)
    return maps


_NC_CACHE = {}


def kernel(**inputs):
    S = 4096
    n = 8
    key = (S, "C")
    if key not in _NC_CACHE:
        _NC_CACHE[key] = build_nc(S, "C")
    nc = _NC_CACHE[key]
    maps = make_inmaps(inputs, S, n)
    res = run_bass_kernel_spmd(nc, maps, core_ids=list(range(n)))
    return np.stack([np.asarray(r["out"], np.float32) for r in res.results], axis=0)
```

```python
import numpy as np
from contextlib import ExitStack
import concourse.bass as bass
import concourse.mybir as mybir
from concourse.bass_utils import run_bass_kernel_spmd

F32 = mybir.dt.float32
BF16 = mybir.dt.bfloat16
AF = mybir.ActivationFunctionType
ALU = mybir.AluOpType
AX = mybir.AxisListType

D = 1024
DFF = 2816
NFF = 22
EPS = 1e-6
INC = 2568
GC1 = 0.044715
GC2 = 1.5957691216057308
ENGS = ("pe", "act", "dve", "pool", "sp")


class Buf:
    def __init__(self):
        self.w = {}
        self.r = {}
        self.tw = 0.0
        self.tr = 0.0
        self.weng = None


def _merge(d, tok):
    k, v = tok
    if d.get(k, 0) < v:
        d[k] = v


class Prog:
    def __init__(self, nc, es):
        self.nc = nc
        self.sem = {e: es.enter_context(nc.semaphore("c_" + e)) for e in ENGS}
        self.cnt = {e: 0 for e in ENGS}
        self.seen = {e: {} for e in ENGS}
        self.dsem = {}
        self.es = es
        self.q = {e: [] for e in ENGS}

    def _semobj(self, key):
        return self.sem[key] if key in self.sem else self.dsem[key][0]

    def _deps(self, eng, rd, wr, extra):
        need = {}
        for b in rd:
            for k, v in b.w.items():
                _merge(need, (k, v))
        for b in wr:
            for k, v in b.w.items():
                _merge(need, (k, v))
            for k, v in b.r.items():
                _merge(need, (k, v))
        for t in extra:
            if t is not None:
                _merge(need, t)
        waits = []
        for k, v in need.items():
            if k == eng and eng == "pe":
                continue
            if self.seen[eng].get(k, 0) >= v:
                continue
            self.seen[eng][k] = v
            waits.append((k, v))
        return waits

    def _commit(self, tok, rd, wr):
        for b in rd:
            _merge(b.r, tok)
        for b in wr:
            b.w = {tok[0]: tok[1]}
            b.r = {}

    rec = None

    def record(self, fn, *a):
        self.rec = []
        fn(*a)
        L, self.rec = self.rec, None
        return L

    COST = {"pe": 0.22, "act": 0.7, "dve": 0.62, "pool": 1.0, "sp": 0.05}
    HOP = 0.3

    def play(self, lists):
        idx = [0] * len(lists)
        tf = self.__dict__.setdefault("tfree", {})
        while True:
            best = None
            for li, L in enumerate(lists):
                if idx[li] >= len(L):
                    continue
                kind, a, kw = L[idx[li]]
                eng = a[0]
                t = tf.get(eng, 0.0)
                for b in kw["rd"]:
                    t = max(t, b.tw + (self.HOP if b.weng != eng else 0.05))
                for b in kw["wr"]:
                    t = max(t, b.tw + (self.HOP if b.weng != eng else 0.05), b.tr + self.HOP)
                tb = kw.get("tbl")
                if tb is not None and tb != self.__dict__.get("cur_tbl"):
                    t += 1.3
                if best is None or t < best[0] - 1e-9:
                    best = (t, li)
            if best is None:
                break
            t, li = best
            kind, a, kw = lists[li][idx[li]]
            idx[li] += 1
            eng = a[0]
            cost = kw.pop("cost", None)
            tb = kw.pop("tbl", None)
            if tb is not None:
                self.cur_tbl = tb
            if kind == "op":
                fns = a[1]
                nf = len(fns) if isinstance(fns, (list, tuple)) else 1
                dur = cost if cost is not None else (self.COST[eng] * nf + (0.06 if eng == "pe" else 0.0))
                end = t + dur
                tf[eng] = end
                self.op(*a, **kw)
            else:
                tf[eng] = t + 0.05
                end = t + 2.5
                self.dma(*a, **kw)
            for b in kw["rd"]:
                b.tr = max(b.tr, end)
            for b in kw["wr"]:
                b.tw = end
                b.tr = 0.0
                b.weng = eng if kind == "op" else "dma"

    def op(self, eng, fns, rd=(), wr=(), extra=(), cost=None):
        if self.rec is not None:
            self.rec.append(("op", (eng, fns), dict(rd=list(rd), wr=list(wr), extra=list(extra), cost=cost,
                                                    tbl=getattr(self, "_tbl", None))))
            return None
        if not isinstance(fns, (list, tuple)):
            fns = [fns]
        waits = self._deps(eng, rd, wr, extra)
        self.cnt[eng] += 1
        tok = (eng, self.cnt[eng])
        self.q[eng].append((waits, list(fns), ("eng", eng)))
        self._commit(tok, rd, wr)
        return tok

    def dma(self, eng, semname, out, in_, rd=(), wr=(), extra=(), part=False):
        if self.rec is not None:
            self.rec.append(("dma", (eng, semname, out, in_), dict(rd=list(rd), wr=list(wr), extra=list(extra), part=part)))
            return None
        if semname not in self.dsem:
            self.dsem[semname] = [self.es.enter_context(self.nc.semaphore("d_" + semname)), 0]
        if part:
            ex = list(extra)
            for b in wr:
                ex.extend(b.r.items())
                ex.extend((k, v) for k, v in b.w.items() if k != semname)
            waits = self._deps(eng, rd, (), ex)
        else:
            waits = self._deps(eng, rd, wr, extra)
        s = self.dsem[semname]
        s[1] += 16
        tok = (semname, s[1])
        self.q[eng].append((waits, [lambda e: e.dma_start(out=out, in_=in_)], ("dma", semname)))
        if part:
            for b in rd:
                _merge(b.r, tok)
            for b in wr:
                _merge(b.w, tok)
                b.r = {}
        else:
            self._commit(tok, rd, wr)
        return tok

    TBL = {AF.Sigmoid: "sig", AF.Sqrt: "sqrt", AF.Exp: "exp", AF.Ln: "exp", AF.Gelu_apprx_tanh: "gelu", AF.Silu: "silu"}

    def act(self, out, in_, func, rd=(), wr=(), extra=(), cost=None, **kw):
        self._tbl = self.TBL.get(func)
        r = self.op("act", lambda e: e.activation(out=out, in_=in_, func=func, **kw), rd, wr, extra, cost)
        self._tbl = None
        return r

    def tt(self, eng, out, in0, in1, op, rd=(), wr=(), extra=(), cost=None):
        return self.op(eng, lambda e: e.tensor_tensor(out=out, in0=in0, in1=in1, op=op), rd, wr, extra, cost)

    def ts(self, eng, out, in0, s1, s2, op0, op1=None, rd=(), wr=(), extra=(), cost=None):
        if op1 is None:
            return self.op(eng, lambda e: e.tensor_scalar(out=out, in0=in0, scalar1=s1, scalar2=None, op0=op0), rd, wr, extra, cost)
        return self.op(eng, lambda e: e.tensor_scalar(out=out, in0=in0, scalar1=s1, scalar2=s2, op0=op0, op1=op1), rd, wr, extra, cost)

    def stt(self, out, in0, scalar, in1, op0, op1, rd=(), wr=(), extra=(), cost=None, **kw):
        return self.op("dve", lambda e: e.scalar_tensor_tensor(out=out, in0=in0, scalar=scalar, in1=in1, op0=op0, op1=op1, **kw), rd, wr, extra, cost)

    def copy(self, eng, out, in_, rd=(), wr=(), extra=()):
        return self.op(eng, lambda e: e.tensor_copy(out=out, in_=in_), rd, wr, extra)

    def recip(self, out, in_, rd=(), wr=(), extra=(), cost=0.2):
        return self.op("dve", lambda e: e.reciprocal(out=out, in_=in_), rd, wr, extra, cost)

    def memset(self, eng, ap, val, rd=(), wr=(), extra=()):
        return self.op(eng, lambda e: e.memset(ap, val), rd, wr, extra)

    def mms(self, specs, rd=(), wr=(), extra=()):
        fns = []
        for (o, l, r, st, sp) in specs:
            fns.append((lambda o, l, r, st, sp: (lambda e: e.matmul(o, lhsT=l, rhs=r, start=st, stop=sp)))(o, l, r, st, sp))
        return self.op("pe", fns, rd, wr, extra)

    def trs(self, specs, ident, rd=(), wr=(), extra=()):
        fns = []
        for (o, i) in specs:
            fns.append((lambda o, i: (lambda e: e.transpose(out=o, in_=i, identity=ident)))(o, i))
        return self.op("pe", fns, rd, wr, extra)

    def run_phase(self, name):
        fin = [(k, s[1]) for k, s in self.dsem.items() if s[1] > 0]
        self.op("sp", lambda e: e.nop(), extra=fin)
        q = self.q
        self.q = {e: [] for e in ENGS}

        def replay(eng, e):
            for waits, fns, inc in q[eng]:
                for (k, v) in waits:
                    e.wait_ge(self._semobj(k), v)
                ins = None
                for f in fns:
                    ins = f(e)
                if inc[0] == "eng":
                    ins.then_inc(self.sem[eng], 1)
                else:
                    ins.then_inc(self.dsem[inc[1]][0], 16)

        with self.nc.Block() as block:
            @block.tensor
            def _(e):
                replay("pe", e)

            @block.scalar
            def _(e):
                replay("act", e)

            @block.vector
            def _(e):
                replay("dve", e)

            @block.gpsimd
            def _(e):
                replay("pool", e)

            @block.sync
            def _(e):
                replay("sp", e)


class NormT:
    def __init__(self, P, sb, ps, src, gd, identd, NSUB, nh):
        self.P = P
        self.src = src
        self.xin = [sb("xin%d" % i, [128, D], F32) for i in range(2)]
        self.hbf = [sb("hbf%d" % i, [128, D], BF16) for i in range(2)]
        self.ssq = sb("ssq", [128, NSUB], F32)
        self.std = sb("std", [128, NSUB], F32)
        self.rstd = sb("rstd", [128, NSUB], F32)
        self.hT = [sb("hT%d" % i, [128, 8, 512], BF16) for i in range(nh)]
        self.tp = [ps("tp%d" % i, [128, 8, 128], BF16) for i in range(2)]
        self.ident = sb("ident", [128, 128], BF16)
        self.gfm = sb("gfm", [128, 8], F32)
        self.b_xin = [Buf(), Buf()]
        self.b_hbf = [Buf(), Buf()]
        self.b_tp = [Buf(), Buf()]
        self.b_hT = [Buf() for _ in range(nh)]
        self.b_stc = [Buf(), Buf()]
        self.b_c = Buf()
        P.dma("sp", "cst", out=self.gfm[:, :], in_=gd, wr=[self.b_c], part=True)
        P.dma("pool", "cstp", out=self.ident[:, :], in_=identd, wr=[self.b_c], part=True)

    def norm(self, g):
        P = self.P
        sl = g % 2
        xin, hbf = self.xin[sl], self.hbf[sl]
        P.dma("sp", "xin%d" % sl, out=xin[:, :], in_=self.src[g * 128:(g + 1) * 128, :], wr=[self.b_xin[sl]])
        P.act(hbf[:, :], xin[:, :], AF.Square, rd=[self.b_xin[sl]], wr=[self.b_hbf[sl]], accum_out=self.ssq[:, g:g + 1])
        bst = [self.b_stc[sl]]
        P.act(self.std[:, g:g + 1], self.ssq[:, g:g + 1], AF.Sqrt, rd=[self.b_hbf[sl], self.b_c], wr=bst, scale=1.0 / D, bias=self.eps[:, 0:1], cost=0.3)
        P.recip(self.rstd[:, g:g + 1], self.std[:, g:g + 1], wr=bst)
        P.ts("dve", hbf[:, :], xin[:, :], self.rstd[:, g:g + 1], None, ALU.mult, rd=[self.b_xin[sl]] + bst, wr=[self.b_hbf[sl]])

    def transp(self, g, hi):
        P = self.P
        sl = g % 2
        s = g % 4
        hbf, tp = self.hbf[sl], self.tp[sl]
        P.trs([(tp[:, k, :], hbf[:, k * 128:(k + 1) * 128]) for k in range(8)], self.ident[:, :],
              rd=[self.b_hbf[sl], self.b_c], wr=[self.b_tp[sl]])
        P.tt("dve", self.hT[hi][:, :, s * 128:(s + 1) * 128], tp[:, :, :],
             self.gfm[:, :].unsqueeze(2).to_broadcast([128, 8, 128]), ALU.mult,
             rd=[self.b_tp[sl], self.b_c], wr=[self.b_hT[hi]])


def load_cast(P, sem, dst, srcd, nk, ncol, wr):
    npiece = (ncol + 2047) // 2048
    while ncol % npiece:
        npiece += 1
    w = ncol // npiece
    for k in range(nk):
        for p in range(npiece):
            P.dma("pool", sem, out=dst[:, k, p * w:(p + 1) * w], in_=srcd[:, k, p * w:(p + 1) * w], wr=wr, part=True)


def ffn_phase(P, S, src, dst, gd, w1d, w3d, w2d, identd, epsd, fgd, ph):
    nc = P.nc
    NT = S // 512
    NSUB = S // 128
    final = fgd is not None
    with ExitStack() as es:
        def sb(name, shape, dt):
            return es.enter_context(nc.sbuf_tensor("%s_%s" % (ph, name), shape, dt))

        def ps(name, shape, dt):
            return es.enter_context(nc.psum_tensor("%s_%s" % (ph, name), shape, dt))

        W1 = sb("W1", [128, 8, DFF], BF16)
        W3 = sb("W3", [128, 8, DFF], BF16)
        W2 = sb("W2", [128, NFF, D], BF16)
        N = NormT(P, sb, ps, src, gd, identd, NSUB, 1)
        N.eps = sb("eps", [128, 1], F32)
        P.dma("sp", "cst", out=N.eps[:, :], in_=epsd, wr=[N.b_c], part=True)
        sil = [sb("sil%d" % i, [128, 512], F32) for i in range(2)]
        aT = sb("aT", [128, NFF, 512], BF16)
        NX = 3 if final else 2
        xres = [sb("xres%d" % i, [128, D], F32) for i in range(NX)]
        if final:
            fg = sb("fg", [128, D], F32)
            junk2 = sb("junk2", [128, D], BF16)
            ssq2 = sb("ssq2", [128, NSUB], F32)
            std2 = sb("std2", [128, NSUB], F32)
            rstd2 = sb("rstd2", [128, NSUB], F32)
            P.dma("sp", "cst", out=fg[:, :], in_=fgd, wr=[N.b_c], part=True)
        pa = [ps("pa%d" % i, [128, 512], F32) for i in range(2)]
        pb = [ps("pb%d" % i, [128, 512], F32) for i in range(2)]
        py = [ps("py%d" % i, [128, 512], F32) for i in range(2)]
        b_w2 = Buf()
        b_pa, b_pb, b_py = [Buf(), Buf()], [Buf(), Buf()], [Buf(), Buf()]
        b_sil = [Buf(), Buf()]
        b_aT = [Buf() for _ in range(NFF)]
        b_xres = [Buf() for _ in range(NX)]
        b_j2 = Buf()

        HW_ = DFF // 2
        b_wg = [Buf(), Buf()]
        for gi in range(2):
            for Wt, wd in ((W1, w1d), (W3, w3d)):
                for k in range(8):
                    P.dma("pool", "w13%d" % gi, out=Wt[:, k, gi * HW_:(gi + 1) * HW_], in_=wd[:, k, gi * HW_:(gi + 1) * HW_],
                          wr=[b_wg[gi]], part=True)
        load_cast(P, "w2", W2, w2d, NFF, D, [b_w2])

        def stage1(i):
            hT = N.hT[0]
            for f in range(NFF):
                sl = f % 2
                bw = b_wg[0] if f < NFF // 2 else b_wg[1]
                P.mms([(pa[sl][:, :], W1[:, k, f * 128:(f + 1) * 128], hT[:, k, :], k == 0, k == 7) for k in range(8)],
                      rd=[N.b_hT[0], bw], wr=[b_pa[sl]])
                P.mms([(pb[sl][:, :], W3[:, k, f * 128:(f + 1) * 128], hT[:, k, :], k == 0, k == 7) for k in range(8)],
                      rd=[N.b_hT[0], bw], wr=[b_pb[sl]])
                P.act(sil[sl][:, :], pa[sl][:, :], AF.Silu, rd=[b_pa[sl]], wr=[b_sil[sl]])
                P.tt("dve", aT[:, f, :], sil[sl][:, :], pb[sl][:, :], ALU.mult, rd=[b_sil[sl], b_pb[sl]], wr=[b_aT[f]])

        def fin_store(g):
            sl = g % NX
            xr = xres[sl]
            if final:
                P.act(junk2[:, :], xr[:, :], AF.Square, rd=[b_xres[sl]], wr=[b_j2], accum_out=ssq2[:, g:g + 1])
                t1 = P.act(std2[:, g:g + 1], ssq2[:, g:g + 1], AF.Sqrt, rd=[b_j2, N.b_c], scale=1.0 / D, bias=N.eps[:, 0:1])
                t2 = P.recip(rstd2[:, g:g + 1], std2[:, g:g + 1], extra=[t1])
                P.stt(xr[:, :], xr[:, :], rstd2[:, g:g + 1], fg[:, :], ALU.mult, ALU.mult, rd=[N.b_c], wr=[b_xres[sl]], extra=[t2])
            P.dma("sp", "st%d" % sl, out=dst[g * 128:(g + 1) * 128, :], in_=xr[:, :], rd=[b_xres[sl]])

        def stage2(i, s):
            g = 4 * i + s
            sl = g % NX
            xr = xres[sl]
            P.dma("sp", "xr%d" % sl, out=xr[:, :], in_=src[g * 128:(g + 1) * 128, :], wr=[b_xres[sl]])
            for h in range(2):
                P.mms([(py[h][:, :], aT[:, f, s * 128:(s + 1) * 128], W2[:, f, h * 512:(h + 1) * 512], f == 0, f == NFF - 1)
                       for f in range(NFF)], rd=b_aT + [b_w2], wr=[b_py[h]])
                P.stt(xr[:, h * 512:(h + 1) * 512], py[h][:, :], 0.5, xr[:, h * 512:(h + 1) * 512], ALU.mult, ALU.add,
                      rd=[b_py[h]], wr=[b_xres[sl]])
            if final:
                if g > 0:
                    fin_store(g - 1)
            else:
                fin_store(g)

        for s in range(4):
            N.norm(s)
            N.transp(s, 0)
        for i in range(NT):
            stage1(i)
            for s in range(4):
                if i + 1 < NT:
                    N.norm(4 * (i + 1) + s)
                stage2(i, s)
                if i + 1 < NT:
                    N.transp(4 * (i + 1) + s, 0)
        if final:
            fin_store(4 * NT - 1)
        P.run_phase(ph)


def b0_phase(P, dr, Wo, wsT, B2c, b_setup, Win):
    nc = P.nc
    with ExitStack() as es:
        def sb(name, shape, dt):
            return es.enter_context(nc.sbuf_tensor("B0_" + name, shape, dt))
        wost = sb("wost", [128, 8, D], F32)
        ogfm = sb("ogfm", [128, 8], F32)
        wsf = sb("wsf", [128, 8, 128], F32)
        smk = sb("smk", [128, 128], F32)
        lnb = sb("lnb", [128, 8, 64], F32)
        bsT = sb("bsT", [128, 8], F32)
        rs = sb("rs", [128, 8], F32)
        onesb = sb("onesb", [128, 1], BF16)
        prs = es.enter_context(nc.psum_tensor("B0_prs", [128, 8], F32))
        b_in, b_o, b_ws, b_prs, b_rs = Buf(), Buf(), Buf(), Buf(), Buf()
        load_cast(P, "w13", Win, dr["win"], 8, INC, [Buf()])
        P.dma("sp", "c0", out=wost[:, :, :], in_=dr["wo"], wr=[b_in], part=True)
        for nm, t, ap in (("og", ogfm, None), ("ws", wsf, None), ("smask", smk, None), ("lnb", lnb, None), ("bsT", bsT, None)):
            if nm == "lnb":
                P.dma("sp", "c0", out=t[:, :, :], in_=dr[nm].rearrange("p (h c) -> p h c", c=64), wr=[b_in], part=True)
            elif nm == "ws":
                P.dma("sp", "c0", out=t[:, :, :], in_=dr[nm], wr=[b_in], part=True)
            else:
                P.dma("sp", "c0", out=t[:, :], in_=dr[nm], wr=[b_in], part=True)
        for kc in range(8):
            P.ts("dve", Wo[:, kc, :], wost[:, kc, :], ogfm[:, kc:kc + 1], None, ALU.mult,
                 rd=[b_in], wr=[b_setup])
        P.tt("dve", wsT[:, :, :], wsf[:, :, :], smk[:, :].unsqueeze(1).to_broadcast([128, 8, 128]), ALU.mult,
             rd=[b_in], wr=[b_ws])
        P.memset("dve", onesb[:, :], 1.0, wr=[b_o])
        P.mms([(prs[:, h:h + 1], wsT[:, h, :], onesb[:, 0:1], True, True) for h in range(8)], rd=[b_ws, b_o], wr=[b_prs])
        P.copy("dve", rs[:, :], prs[:, :], rd=[b_prs], wr=[b_rs])
        P.tt("dve", B2c[:, :, :], lnb[:, :, :], rs[:, :].unsqueeze(2).to_broadcast([128, 8, 64]), ALU.mult,
             rd=[b_in, b_rs], wr=[b_setup])
        P.tt("dve", B2c[:, :, :], B2c[:, :, :], bsT[:, :].unsqueeze(2).to_broadcast([128, 8, 64]), ALU.add,
             rd=[b_in], wr=[b_setup])
        P.run_phase("B0")


def b1_phase(P, S, src, qTd, kTd, vd, ysTd, dr, wsT, B2c, b_setup, Win):
    nc = P.nc
    NCH = S // 512
    NSUB = S // 128
    with ExitStack() as es:
        def sb(name, shape, dt):
            return es.enter_context(nc.sbuf_tensor("B1_" + name, shape, dt))

        def ps(name, shape, dt):
            return es.enter_context(nc.psum_tensor("B1_" + name, shape, dt))

        N = NormT(P, sb, ps, src, dr["mg"], dr["ident"], NSUB, 2)
        N.eps = sb("eps", [128, 1], F32)
        P.dma("sp", "cst", out=N.eps[:, :], in_=dr["epsc"], wr=[N.b_c], part=True)
        lng = sb("lng", [128, 512], F32)
        P.dma("sp", "cst", out=lng[:, :], in_=dr["lng"], wr=[N.b_c], part=True)
        fb = sb("fb", [8, 1], F32)
        nfb = sb("nfb", [8, 1], F32)
        one8 = sb("one8", [8, 1], F32)
        P.dma("sp", "cst", out=fb[:, :], in_=dr["fb"], wr=[N.b_c], part=True)
        P.ts("dve", nfb[:, :], fb[:, :], -1.0, None, ALU.mult, rd=[N.b_c], wr=[N.b_c])
        P.memset("dve", one8[:, :], 1.0, wr=[N.b_c])
        qst = sb("qst", [128, 4, 512], BF16)
        kst = sb("kst", [128, 4, 512], BF16)
        vst = sb("vst", [128, 4, 4, 192], BF16)
        yst = sb("yst", [128, 4, 512], BF16)
        augq = sb("augq", [8, 6, 512], BF16)
        augk = sb("augk", [8, 6, 512], BF16)
        ef = sb("ef", [8, 512], F32)
        lf = sb("lf", [8, 512], F32)
        Fc = [sb("Fc%d" % i, [8, 512], F32) for i in range(2)]
        tmpU = [sb("tmpU%d" % i, [128, 512], F32) for i in range(3)]
        tmpV = [sb("tmpV%d" % i, [128, 512], F32) for i in range(3)]
        gu = [sb("gu%d" % i, [128, 512], F32) for i in range(3)]
        gv = [sb("gv%d" % i, [128, 512], F32) for i in range(3)]
        nbf = [sb("nbf%d" % i, [128, 8, 64], BF16) for i in range(3)]
        yn = [sb("yn%d" % i, [128, 512], BF16) for i in range(2)]
        stn = ("vsum", "vssq", "mu", "m2", "var", "sdv", "rstdv", "nmr")
        st = {n: sb(n, [128, NSUB], F32) for n in stn}
        gss = sb("gss", [128, NSUB, 8], F32)
        gsd = sb("gsd", [128, NSUB, 8], F32)
        gr = sb("gr", [128, NSUB, 8], F32)
        pA = [ps("pA%d" % i, [128, 512], F32) for i in range(2)]
        pB = [ps("pB%d" % i, [128, 512], F32) for i in range(2)]
        pM = ps("pM", [128, 8, 64], F32)
        pT = ps("pT", [128, 4, 128], BF16)
        b_win = Buf()
        b_pA, b_pB = [Buf(), Buf()], [Buf(), Buf()]
        b_pM, b_pT = Buf(), Buf()
        b_qst, b_kst, b_vst, b_yst, b_augq, b_augk = Buf(), Buf(), Buf(), Buf(), Buf(), Buf()
        b_ef, b_lf = Buf(), Buf()
        b_Fc = [Buf(), Buf()]
        b_tmpU, b_tmpV, b_gu, b_gv = ([Buf() for _ in range(3)] for _ in range(4))
        b_nbf, b_yn, b_stat = [Buf() for _ in range(3)], [Buf(), Buf()], [Buf() for _ in range(3)]

        P.memset("dve", vst[:, :, :, :], 1.0, wr=[b_vst])
        P.memset("dve", augq[:, :, :], 1.0, wr=[b_augq])
        P.memset("dve", augk[:, :, :], 1.0, wr=[b_augk])
        P.memset("dve", Fc[1][:, :], 0.0, wr=[b_Fc[1]])

        def v3(t):
            return t[:, :].rearrange("p (h c) -> p h c", c=64)

        def proj_qkf(c):
            cc = c % 2
            hT = N.hT[cc]
            c0, c1 = c * 512, (c + 1) * 512
            for j in range(8):
                pbj = pB[j % 2]
                P.mms([(pbj[:, :], Win[:, k, 1024 + j * 128:1024 + (j + 1) * 128], hT[:, k, :], k == 0, k == 7) for k in range(8)],
                      rd=[N.b_hT[cc], b_win], wr=[b_pB[j % 2]])
                if j < 4:
                    P.act(qst[:, j, :], pbj[:, :], AF.Copy, rd=[b_pB[j % 2]], wr=[b_qst], scale=0.125)
                else:
                    P.copy("dve", kst[:, j - 4, :], pbj[:, :], rd=[b_pB[j % 2]], wr=[b_kst])
            for h in range(8):
                r0 = (h % 2) * 64
                P.dma("sp", "qd", out=qTd[h, 0:64, c0:c1], in_=qst[r0:r0 + 64, h // 2, :], rd=[b_qst])
                P.dma("sp", "kd", out=kTd[h, 0:64, c0:c1], in_=kst[r0:r0 + 64, h // 2, :], rd=[b_kst])
            P.mms([(pB[0][0:8, :], Win[:, k, 2560:2568], hT[:, k, :], k == 0, k == 7) for k in range(8)],
                  rd=[N.b_hT[cc], b_win], wr=[b_pB[0]])
            P.act(ef[:, :], pB[0][0:8, :], AF.Exp, rd=[b_pB[0], N.b_c], wr=[b_ef], scale=-1.0, bias=nfb[:, 0:1])
            P.act(lf[:, :], ef[:, :], AF.Ln, rd=[b_ef], wr=[b_lf], bias=one8[:, 0:1])
            fcc, fpp = Fc[cc], Fc[1 - cc]
            P.op("dve", lambda e: e.tensor_tensor_scan(out=fcc[:, :], data0=one8[:, 0:1].to_broadcast([8, 512]), data1=lf[:, :],
                                                       initial=fpp[:, 511:512], op0=ALU.mult, op1=ALU.subtract),
                 rd=[b_lf, b_Fc[1 - cc], N.b_c], wr=[b_Fc[cc]])
            P.copy("dve", augq[:, 0, :], fcc[:, :], rd=[b_Fc[cc]], wr=[b_augq])
            P.tt("dve", ef[:, :], fcc[:, :], augq[:, 0, :], ALU.subtract, rd=[b_Fc[cc], b_augq], wr=[b_ef])
            P.copy("dve", augq[:, 1, :], ef[:, :], rd=[b_ef], wr=[b_augq])
            P.tt("dve", lf[:, :], ef[:, :], augq[:, 1, :], ALU.subtract, rd=[b_ef, b_augq], wr=[b_lf])
            P.copy("dve", augq[:, 2, :], lf[:, :], rd=[b_lf], wr=[b_augq])
            P.ts("dve", augk[:, 3:6, :], augq[:, 0:3, :], -1.0, None, ALU.mult, rd=[b_augq], wr=[b_augk])
            P.dma("sp", "qa", out=qTd[:, 64:70, c0:c1], in_=augq[:, :, :], rd=[b_augq])
            P.dma("sp", "ka", out=kTd[:, 64:70, c0:c1], in_=augk[:, :, :], rd=[b_augk])

        def proj_v(c):
            cc = c % 2
            hT = N.hT[cc]
            for s in range(4):
                pa_ = pA[s % 2]
                P.mms([(pa_[:, :], hT[:, k, s * 128:(s + 1) * 128], Win[:, k, 2048:2560], k == 0, k == 7) for k in range(8)],
                      rd=[N.b_hT[cc], b_win], wr=[b_pA[s % 2]])
                pv4 = pa_[:, :].rearrange("p (j hh d) -> p j hh d", hh=2, d=64)
                P.copy("dve", vst[:, s, :, 0:64], pv4[:, :, 0, :], rd=[b_pA[s % 2]], wr=[b_vst])
                P.act(vst[:, s, :, 128:192], pv4[:, :, 1, :], AF.Copy, rd=[b_pA[s % 2]], wr=[b_vst])
            P.dma("sp", "vd", out=vd[:, c * 4:(c + 1) * 4, :],
                  in_=vst[:, :, :, :].rearrange("p s j f -> p s (j f)"), rd=[b_vst])

        def gelu(Xp, b_X, tmp, b_tmp, res, b_res, accum):
            if accum is None:
                P.act(res[:, :], Xp[:, :], AF.Gelu_apprx_tanh, rd=[b_X], wr=[b_res])
            else:
                P.act(res[:, :], Xp[:, :], AF.Gelu_apprx_tanh, rd=[b_X], wr=[b_res, accum[1]], accum_out=accum[0])

        def sgu1(c, s):
            cc = c % 2
            hT = N.hT[cc]
            g = 4 * c + s
            w = g % 3
            gc = slice(g, g + 1)
            P.mms([(pA[0][:, :], hT[:, k, s * 128:(s + 1) * 128], Win[:, k, 0:512], k == 0, k == 7) for k in range(8)],
                  rd=[N.b_hT[cc], b_win], wr=[b_pA[0]])
            P.mms([(pA[1][:, :], hT[:, k, s * 128:(s + 1) * 128], Win[:, k, 512:1024], k == 0, k == 7) for k in range(8)],
                  rd=[N.b_hT[cc], b_win], wr=[b_pA[1]])
            gelu(pA[0], b_pA[0], tmpU[w], b_tmpU[w], gu[w], b_gu[w], None)
            gelu(pA[1], b_pA[1], tmpV[w], b_tmpV[w], gv[w], b_gv[w], (st["vsum"][:, gc], b_stat[w]))

        def sgu1b(c, s):
            g = 4 * c + s
            w = g % 3
            gc = slice(g, g + 1)
            bs = [b_stat[w]]
            P.act(tmpV[w][:, :], gv[w][:, :], AF.Square, rd=[b_gv[w]], wr=[b_tmpV[w]] + bs, accum_out=st["vssq"][:, gc])
            P.ts("dve", st["mu"][:, gc], st["vsum"][:, gc], 1.0 / 512, None, ALU.mult, wr=bs, cost=0.25)
            P.tt("dve", st["m2"][:, gc], st["mu"][:, gc], st["mu"][:, gc], ALU.mult, wr=bs, cost=0.25)
            P.stt(st["var"][:, gc], st["vssq"][:, gc], 1.0 / 512, st["m2"][:, gc], ALU.mult, ALU.subtract, wr=bs, cost=0.25)
            P.act(st["sdv"][:, gc], st["var"][:, gc], AF.Sqrt, rd=[N.b_c], wr=bs, bias=N.eps[:, 0:1], cost=0.25)
            P.recip(st["rstdv"][:, gc], st["sdv"][:, gc], wr=bs)
            P.stt(st["nmr"][:, gc], st["mu"][:, gc], -1.0, st["rstdv"][:, gc], ALU.mult, ALU.mult, wr=bs, cost=0.25)
            P.act(nbf[w][:, :, :], v3(gv[w]), AF.Identity, rd=[b_gv[w]] + bs, wr=[b_nbf[w]],
                  scale=st["rstdv"][:, gc], bias=st["nmr"][:, gc])

        def sgu2(c, s):
            g = 4 * c + s
            w = g % 3
            y = g % 2
            bs = [b_stat[w]]
            P.mms([(pM[:, h, :], wsT[:, h, :], nbf[w][:, h, :], True, True) for h in range(8)],
                  rd=[b_nbf[w], b_setup], wr=[b_pM])
            P.tt("dve", v3(tmpU[w]), pM[:, :, :], v3(lng), ALU.mult, rd=[b_pM, N.b_c], wr=[b_tmpU[w]])
            P.tt("dve", v3(tmpU[w]), v3(tmpU[w]), B2c[:, :, :], ALU.add, rd=[b_setup], wr=[b_tmpU[w]])
            P.tt("dve", tmpU[w][:, :], tmpU[w][:, :], gu[w][:, :], ALU.mult, rd=[b_gu[w]], wr=[b_tmpU[w]])
            P.tt("dve", tmpV[w][:, :], tmpU[w][:, :], tmpU[w][:, :], ALU.mult, rd=[b_tmpU[w]], wr=[b_tmpV[w]])
            tv3 = v3(tmpV[w])
            P.op("dve", lambda e: e.tensor_reduce(out=gss[:, g, :], in_=tv3, axis=AX.X, op=ALU.add), rd=[b_tmpV[w]], wr=bs)
            P.act(gsd[:, g, :], gss[:, g, :], AF.Sqrt, rd=[N.b_c], wr=bs, scale=1.0 / 64, bias=N.eps[:, 0:1], cost=0.25)
            P.recip(gr[:, g, :], gsd[:, g, :], wr=bs)
            P.tt("dve", v3(yn[y]), v3(tmpU[w]), gr[:, g, :].unsqueeze(2).to_broadcast([128, 8, 64]), ALU.mult,
                 rd=[b_tmpU[w]] + bs, wr=[b_yn[y]])

        def sgu3(c, s):
            y = (4 * c + s) % 2
            P.trs([(pT[:, kc, :], yn[y][:, kc * 128:(kc + 1) * 128]) for kc in range(4)], N.ident[:, :],
                  rd=[b_yn[y], N.b_c], wr=[b_pT])
            P.act(yst[:, :, s * 128:(s + 1) * 128], pT[:, :, :], AF.Copy, rd=[b_pT], wr=[b_yst])
            if s == 3:
                P.dma("sp", "yd", out=ysTd.rearrange("(kc p) n -> p kc n", p=128)[:, :, c * 512:(c + 1) * 512],
                      in_=yst[:, :, :], rd=[b_yst])

        for s in range(4):
            N.norm(s)
            N.transp(s, 0)
        NG = 4 * NCH

        def stream_a(t):
            c, s_ = divmod(t, 4)
            if s_ == 0:
                proj_qkf(c)
                proj_v(c)
            if c + 1 < NCH:
                N.norm(4 * (c + 1) + s_)
            sgu1(c, s_)
            if c + 1 < NCH:
                N.transp(4 * (c + 1) + s_, (c + 1) % 2)

        for t in range(NG + 3):
            lists = []
            if t < NG:
                lists.append(P.record(stream_a, t))
            for d_, fn_ in ((1, sgu1b), (2, sgu2), (3, sgu3)):
                if 0 <= t - d_ < NG:
                    lists.append(P.record(fn_, (t - d_) // 4, (t - d_) % 4))
            P.play(lists)
        P.run_phase("B1")


def b2_phase(P, S, x1, x2, qTd, kTd, vd, ysTd, dr, Wo, b_setup):
    nc = P.nc
    NCH = S // 512
    NSUB = S // 128
    with ExitStack() as es:
        def sb(name, shape, dt):
            return es.enter_context(nc.sbuf_tensor("B2_" + name, shape, dt))

        def ps(name, shape, dt):
            return es.enter_context(nc.psum_tensor("B2_" + name, shape, dt))

        Kst = sb("Kst", [128, 8, S], BF16)
        Vst = sb("Vst", [128, NSUB, 768], BF16)
        qc = [sb("qc%d" % i, [128, 8, 512], BF16) for i in range(2)]
        ysg = [sb("ysg%d" % i, [128, 4, 512], BF16) for i in range(2)]
        yfox = [sb("yfox%d" % i, [128, 4, 512], BF16) for i in range(2)]
        PT = [sb("PT%d" % i, [128, 512], BF16) for i in range(4)]
        Ocp = [sb("Ocp%d" % i, [128, 512], BF16) for i in range(2)]
        X = [sb("X%d" % i, [128, 512], BF16) for i in range(2)]
        lnn = sb("lnn", [128, 512], F32)
        rr = sb("rr", [128, 512], F32)
        xres = [sb("xres%d" % i, [128, D], F32) for i in range(2)]
        tri = sb("tri", [128, 128], BF16)
        idb = sb("idb", [128, 128], BF16)
        b2m = [sb("b2m%d" % i, [128, 128], BF16) for i in range(2)]
        pS = [ps("pS%d" % i, [128, 512], F32) for i in range(4)]
        pO = [ps("pO%d" % i, [128, 512], F32) for i in range(2)]
        pN = ps("pN", [128, 512], F32)
        pY = [ps("pY0", [128, 512], F32)] * 2
        b_cst = Buf()
        b_qc, b_ysg, b_yfox = [Buf(), Buf()], [Buf(), Buf()], [Buf(), Buf()]
        b_PT = [Buf() for _ in range(4)]
        b_Ocp, b_X = [Buf(), Buf()], [Buf(), Buf()]
        b_lnn, b_rr = Buf(), Buf()
        b_xres = [Buf(), Buf()]
        b_pS = [Buf() for _ in range(4)]
        b_pO = [Buf(), Buf()]
        b_pN = Buf()
        b_pY = [Buf()] * 2

        P.dma("pool", "c2", out=tri[:, :], in_=dr["nmask"], wr=[b_cst], part=True)
        P.dma("pool", "c2", out=idb[:, :], in_=dr["ident"], wr=[b_cst], part=True)
        P.dma("pool", "c2", out=b2m[0][:, :], in_=dr["b2a"], wr=[b_cst], part=True)
        P.dma("pool", "c2", out=b2m[1][:, :], in_=dr["b2b"], wr=[b_cst], part=True)
        b_Kh = [Buf() for _ in range(8)]
        b_Kl = [Buf() for _ in range(8)]
        nv = max(1, NSUB // 8)
        b_Vp = [Buf() for _ in range(0, NSUB, nv)]
        for i in range(2):
            P.memset("dve", qc[i][64:128, :, :].bitcast(F32), 0.0, wr=[b_qc[i]])
        for h in range(8):
            P.memset("dve", Kst[64:128, h, :].bitcast(F32), 0.0, wr=[b_Kh[h]])
        def vlhsT(h, j):
            o = (h // 2) * 192 + (h % 2) * 64
            return Vst[:, j, o:o + 128]

        def load_chunk(c):
            cc = c % 2
            P.dma("sp", "qc%d" % cc, out=qc[cc][0:70, :, :], in_=qTd[:, :, c * 512:(c + 1) * 512].rearrange("h r n -> r h n"),
                  wr=[b_qc[cc]])
            P.dma("sp", "ys%d" % cc, out=ysg[cc][:, :, :],
                  in_=ysTd.rearrange("(kc p) n -> p kc n", p=128)[:, :, c * 512:(c + 1) * 512], wr=[b_ysg[cc]])

        load_chunk(0)
        P.dma("sp", "vv0", out=Vst[:, 0:nv, :], in_=vd[:, 0:nv, :], wr=[b_Vp[0]])
        for h in range(8):
            P.dma("sp", "kl%d" % h, out=Kst[0:64, h, :], in_=kTd[h, 0:64, :], wr=[b_Kl[h]])
            P.dma("sp", "kh%d" % h, out=Kst[64:70, h, :], in_=kTd[h, 64:70, :], wr=[b_Kh[h]])
        for h in range(1, len(b_Vp)):
            i = h * nv
            P.dma("sp", "vv%d" % h, out=Vst[:, i:i + nv, :], in_=vd[:, i:i + nv, :], wr=[b_Vp[h]])

        def epi_a(u):
            P.copy("dve", Ocp[u % 2][:, :], pO[u % 2][:, :], rd=[b_pO[u % 2]], wr=[b_Ocp[u % 2]])
            P.tt("dve", X[u % 2][:, :], Ocp[u % 2][:, :], Ocp[u % 2][:, :], ALU.mult, rd=[b_Ocp[u % 2]], wr=[b_X[u % 2]])

        def epi_pair(u, c, p):
            cc = c % 2
            P.mms([(pN[:, :], b2m[0][:, :], X[u % 2][:, :], True, False),
                   (pN[:, :], b2m[1][:, :], X[(u + 1) % 2][:, :], False, True)],
                  rd=[b_X[0], b_X[1], b_cst], wr=[b_pN])
            P.act(lnn[:, :], pN[:, :], AF.Ln, rd=[b_pN], wr=[b_lnn])
            P.act(rr[:, :], lnn[:, :], AF.Exp, rd=[b_lnn], wr=[b_rr], scale=-0.5)
            for i, R in enumerate((slice(0, 64), slice(64, 128))):
                P.tt("dve", yfox[cc][R, p, :], Ocp[(u + i) % 2][R, :], rr[R, :], ALU.mult,
                     rd=[b_Ocp[(u + i) % 2], b_rr], wr=[b_yfox[cc]])

        def epi_b(u, c, h):
            cc = c % 2
            R = slice(0, 64) if h % 2 == 0 else slice(64, 128)
            P.mms([(pN[:, :], b2m[h % 2][:, :], X[u % 2][:, :], True, True)], rd=[b_X[u % 2], b_cst], wr=[b_pN])
            P.act(lnn[R, :], pN[R, :], AF.Ln, rd=[b_pN], wr=[b_lnn])
            P.act(rr[R, :], lnn[R, :], AF.Exp, rd=[b_lnn], wr=[b_rr], scale=-0.5)
            P.tt("dve", yfox[cc][R, h // 2, :], Ocp[u % 2][R, :], rr[R, :], ALU.mult, rd=[b_Ocp[u % 2], b_rr], wr=[b_yfox[cc]])

        def w_out_parts(c):
            cc = c % 2
            parts = []
            for s in range(4):
                g = 4 * c + s
                sl = g % 2
                xr = xres[sl]
                for hf in range(2):
                    for q in range(4):
                        def part(s=s, g=g, sl=sl, xr=xr, hf=hf, q=q):
                            if hf == 0 and q == 0:
                                P.dma("sp", "xr%d" % sl, out=xr[:, :], in_=x1[g * 128:(g + 1) * 128, :], wr=[b_xres[sl]])
                            P.mms([(pY[hf][:, :], (ysg[cc] if kc < 4 else yfox[cc])[:, kc % 4, s * 128:(s + 1) * 128],
                                    Wo[:, kc, hf * 512:(hf + 1) * 512], kc == 0, kc == 7) for kc in (2 * q, 2 * q + 1)],
                                  rd=[b_ysg[cc], b_yfox[cc], b_setup], wr=[b_pY[hf]])
                            if q == 3:
                                P.tt("dve", xr[:, hf * 512:(hf + 1) * 512], pY[hf][:, :], xr[:, hf * 512:(hf + 1) * 512],
                                     ALU.add, rd=[b_pY[hf]], wr=[b_xres[sl]])
                                if hf == 1:
                                    P.dma("sp", "st%d" % sl, out=x2[g * 128:(g + 1) * 128, :], in_=xr[:, :], rd=[b_xres[sl]])
                        parts.append(part)
            return parts

        wq = []

        def unit(c, h, u, sl, pend=(), nw=0):
            cc = c % 2
            njt = 4 * c + 4

            def s_mm(j):
                q0 = max(0, j - 4 * c) * 128
                k_ = 2 * sl + j % 2
                mm = [(pS[k_][:, q0:512], Kst[:, h, j * 128:(j + 1) * 128], qc[cc][:, h, q0:512], True, j < 4 * c)]
                if j >= 4 * c:
                    mm.append((pS[k_][:, q0:q0 + 128], idb[:, :], tri[:, :], False, True))
                P.mms(mm, rd=[b_Kl[h], b_Kh[h], b_qc[cc], b_cst], wr=[b_pS[k_]])

            def e_pv(j):
                r = j - 4 * c
                q0 = max(0, r) * 128
                k_ = 2 * sl + j % 2
                P.act(PT[k_][:, q0:512], pS[k_][:, q0:512], AF.Exp, rd=[b_pS[k_]], wr=[b_PT[k_]])
                P.mms([(pO[u % 2][:, q0:512], vlhsT(h, j), PT[k_][:, q0:512], j == 0, j == njt - 1)],
                      rd=[b_PT[k_], b_Vp[j // nv]], wr=[b_pO[u % 2]])

            for j in range(min(2, njt)):
                s_mm(j)
            for j in range(njt):
                e_pv(j)
                if j + 2 < njt:
                    s_mm(j + 2)
                if j == min(2, njt - 1):
                    for f in pend:
                        f()
                if j >= 1 and nw > 0 and wq:
                    wq.pop(0)()
                    nw -= 1
            while nw > 0 and wq:
                wq.pop(0)()
                nw -= 1

        u = 0
        pending = []
        for c in range(NCH):
            for p in range(4):
                nw = 0 if p == 0 else (11 if p < 3 else 99)

                lists = [P.record(unit, c, 2 * p, u, 0, pending, nw), P.record(unit, c, 2 * p + 1, u + 1, 1)]
                P.play(lists)
                epi_a(u)
                epi_a(u + 1)
                pending = [(lambda u_=u, c_=c, p_=p: epi_pair(u_, c_, p_))]
                if p == 0 and c > 0:
                    wq.extend(w_out_parts(c - 1))
                if p == 3 and c + 1 < NCH:
                    load_chunk(c + 1)
                u += 2
        for f in pending:
            f()
        for f in w_out_parts(NCH - 1):
            f()
        P.run_phase("B2")


def build_nc(S=4096, upto="C"):
    nc = bass.Bass("TRN2", target_bir_lowering=False)
    NSUB = S // 128

    def din(name, shape):
        return nc.dram_tensor(name, shape, F32, kind="ExternalInput").ap()

    def dint(name, shape, dt):
        return nc.dram_tensor(name, shape, dt, kind="Internal").ap()

    x = din("x", [S, D])
    f1g = din("f1g", [128, 8])
    f1w1 = din("f1w1", [128, 8, DFF])
    f1w3 = din("f1w3", [128, 8, DFF])
    f1w2 = din("f1w2", [128, NFF, D])
    f2g = din("f2g", [128, 8])
    f2w1 = din("f2w1", [128, 8, DFF])
    f2w3 = din("f2w3", [128, 8, DFF])
    f2w2 = din("f2w2", [128, NFF, D])
    fg = din("fg", [128, D])
    ident = din("ident", [128, 128])
    epsd = din("epsc", [128, 1])
    dr = {"ident": ident, "epsc": epsd}
    for nm, shp in (("mg", [128, 8]), ("win", [128, 8, INC]), ("fb", [8, 1]), ("lng", [128, 512]), ("lnb", [128, 512]),
                    ("ws", [128, 8, 128]), ("bsT", [128, 8]), ("og", [128, 8]), ("wo", [128, 8, D]),
                    ("nmask", [128, 128]), ("smask", [128, 128]), ("b2a", [128, 128]), ("b2b", [128, 128])):
        dr[nm] = din(nm, shp)
    out = nc.dram_tensor("out", [S, D], F32, kind="ExternalOutput").ap()
    x1 = dint("x1s", [S, D], F32)
    x2 = dint("x2s", [S, D], F32)
    qTd = dint("qTd", [8, 70, S], BF16)
    kTd = dint("kTd", [8, 70, S], BF16)
    vd = dint("vd", [128, NSUB, 768], BF16)
    ysTd = dint("ysTd", [512, S], BF16)
    with ExitStack() as es:
        P = Prog(nc, es)
        ffn_phase(P, S, x, out if upto == "A" else x1, f1g, f1w1, f1w3, f1w2, ident, epsd, None, "A")
        if upto in ("B", "C"):
            with ExitStack() as es2:
                Wo = es2.enter_context(nc.sbuf_tensor("Wo", [128, 8, D], BF16))
                wsT = es2.enter_context(nc.sbuf_tensor("wsT", [128, 8, 128], BF16))
                B2c = es2.enter_context(nc.sbuf_tensor("B2c", [128, 8, 64], F32))
                with ExitStack() as es3:
                    Win = es3.enter_context(nc.sbuf_tensor("Win", [128, 8, INC], BF16))
                    b_setup = Buf()
                    b0_phase(P, dr, Wo, wsT, B2c, b_setup, Win)
                    b_setup = Buf()
                    b1_phase(P, S, x1, qTd, kTd, vd, ysTd, dr, wsT, B2c, b_setup, Win)
                b_setup = Buf()
                b2_phase(P, S, x1, out if upto == "B" else x2, qTd, kTd, vd, ysTd, dr, Wo, b_setup)
        if upto == "AC":
            ffn_phase(P, S, x1, out, f2g, f2w1, f2w3, f2w2, ident, epsd, fg, "C")
        if upto == "C":
            ffn_phase(P, S, x2, out, f2g, f2w1, f2w3, f2w2, ident, epsd, fg, "C")
    return nc


def fm(v, n):
    return np.ascontiguousarray(np.asarray(v, np.float32).reshape(n, 128).T)


def wl(w, n):
    w = np.asarray(w, np.float32)
    return np.ascontiguousarray(w.reshape(n, 128, w.shape[1]).transpose(1, 0, 2))


def make_inmaps(inp, S, ncores):
    g = lambda k: np.asarray(inp[k], np.float32)
    pi = np.arange(128)
    b2a = np.zeros((128, 128), np.float32)
    b2a[:64, :64] = 1.0 / 64
    b2a[64:, :64] = EPS / 64
    b2b = np.zeros((128, 128), np.float32)
    b2b[64:, 64:] = 1.0 / 64
    b2b[:64, 64:] = EPS / 64
    shared = {
        "f1g": fm(g("ffn1_norm_g")[0], 8), "f1w1": wl(g("ffn1_w1")[0], 8), "f1w3": wl(g("ffn1_w3")[0], 8),
        "f1w2": wl(g("ffn1_w2")[0], NFF),
        "f2g": fm(g("ffn2_norm_g")[0], 8), "f2w1": wl(g("ffn2_w1")[0], 8), "f2w3": wl(g("ffn2_w3")[0], 8),
        "f2w2": wl(g("ffn2_w2")[0], NFF),
        "fg": np.ascontiguousarray(np.broadcast_to(g("final_norm_g")[None, :], (128, D))),
        "ident": np.eye(128, dtype=np.float32),
        "epsc": np.full((128, 1), EPS, np.float32),
        "mg": fm(g("mix_norm_g")[0], 8), "win": wl(g("w_in")[0], 8),
        "fb": np.ascontiguousarray(g("fox_f_bias")[0].reshape(8, 1)),
        "lng": np.ascontiguousarray(np.broadcast_to(g("sgu_ln_g")[0][None, :], (128, 512))),
        "lnb": np.ascontiguousarray(np.broadcast_to(g("sgu_ln_b")[0][None, :], (128, 512))),
        "ws": np.ascontiguousarray(g("sgu_w_s")[0].transpose(2, 0, 1)),
        "bsT": np.ascontiguousarray(g("sgu_b_s")[0].T),
        "og": fm(g("mix_out_g")[0], 8), "wo": wl(g("w_out")[0], 8),
        "nmask": -30000.0 * (pi[:, None] > pi[None, :]).astype(np.float32),
        "smask": ((pi[None, :] // 64) >= (pi[:, None] // 64)).astype(np.float32),
        "b2a": b2a, "b2b": b2b,
    }
    xs = g("x")
    maps = []
    for c in range(ncores):
        m = dict(shared)
        m["x"] = np.ascontiguousarray(xs[c, :S, :])
        maps.append(m)
    return maps


_NC_CACHE = {}


def kernel(**inputs):
    S = 4096
    n = 8
    key = (S, "C")
    if key not in _NC_CACHE:
        _NC_CACHE[key] = build_nc(S, "C")
    nc = _NC_CACHE[key]
    maps = make_inmaps(inputs, S, n)
    res = run_bass_kernel_spmd(nc, maps, core_ids=list(range(n)))
    return np.stack([np.asarray(r["out"], np.float32) for r in res.results], axis=0)
```

```python
import numpy as np
from contextlib import ExitStack
import concourse.bass as bass
import concourse.mybir as mybir
from concourse.bass_utils import run_bass_kernel_spmd

F32 = mybir.dt.float32
BF16 = mybir.dt.bfloat16
AF = mybir.ActivationFunctionType
ALU = mybir.AluOpType
AX = mybir.AxisListType

D = 1024
DFF = 2816
NFF = 22
EPS = 1e-6
INC = 2568
GC1 = 0.044715
GC2 = 1.5957691216057308
ENGS = ("pe", "act", "dve", "pool", "sp")


class Buf:
    def __init__(self):
        self.w = {}
        self.r = {}
        self.tw = 0.0
        self.tr = 0.0
        self.weng = None


def _merge(d, tok):
    k, v = tok
    if d.get(k, 0) < v:
        d[k] = v


class Prog:
    def __init__(self, nc, es):
        self.nc = nc
        self.sem = {e: es.enter_context(nc.semaphore("c_" + e)) for e in ENGS}
        self.cnt = {e: 0 for e in ENGS}
        self.seen = {e: {} for e in ENGS}
        self.dsem = {}
        self.es = es
        self.q = {e: [] for e in ENGS}

    def _semobj(self, key):
        return self.sem[key] if key in self.sem else self.dsem[key][0]

    def _deps(self, eng, rd, wr, extra):
        need = {}
        for b in rd:
            for k, v in b.w.items():
                _merge(need, (k, v))
        for b in wr:
            for k, v in b.w.items():
                _merge(need, (k, v))
            for k, v in b.r.items():
                _merge(need, (k, v))
        for t in extra:
            if t is not None:
                _merge(need, t)
        waits = []
        for k, v in need.items():
            if k == eng and eng == "pe":
                continue
            if self.seen[eng].get(k, 0) >= v:
                continue
            self.seen[eng][k] = v
            waits.append((k, v))
        return waits

    def _commit(self, tok, rd, wr):
        for b in rd:
            _merge(b.r, tok)
        for b in wr:
            b.w = {tok[0]: tok[1]}
            b.r = {}

    rec = None

    def record(self, fn, *a):
        self.rec = []
        fn(*a)
        L, self.rec = self.rec, None
        return L

    COST = {"pe": 0.22, "act": 0.7, "dve": 0.62, "pool": 1.0, "sp": 0.05}
    HOP = 0.3

    def play(self, lists):
        idx = [0] * len(lists)
        tf = self.__dict__.setdefault("tfree", {})
        while True:
            best = None
            for li, L in enumerate(lists):
                if idx[li] >= len(L):
                    continue
                kind, a, kw = L[idx[li]]
                eng = a[0]
                t = tf.get(eng, 0.0)
                for b in kw["rd"]:
                    t = max(t, b.tw + (self.HOP if b.weng != eng else 0.05))
                for b in kw["wr"]:
                    t = max(t, b.tw + (self.HOP if b.weng != eng else 0.05), b.tr + self.HOP)
                tb = kw.get("tbl")
                if tb is not None and tb != self.__dict__.get("cur_tbl"):
                    t += 1.3
                if best is None or t < best[0] - 1e-9:
                    best = (t, li)
            if best is None:
                break
            t, li = best
            kind, a, kw = lists[li][idx[li]]
            idx[li] += 1
            eng = a[0]
            cost = kw.pop("cost", None)
            tb = kw.pop("tbl", None)
            if tb is not None:
                self.cur_tbl = tb
            if kind == "op":
                fns = a[1]
                nf = len(fns) if isinstance(fns, (list, tuple)) else 1
                dur = cost if cost is not None else (self.COST[eng] * nf + (0.06 if eng == "pe" else 0.0))
                end = t + dur
                tf[eng] = end
                self.op(*a, **kw)
            else:
                tf[eng] = t + 0.05
                end = t + 2.5
                self.dma(*a, **kw)
            for b in kw["rd"]:
                b.tr = max(b.tr, end)
            for b in kw["wr"]:
                b.tw = end
                b.tr = 0.0
                b.weng = eng if kind == "op" else "dma"

    def op(self, eng, fns, rd=(), wr=(), extra=(), cost=None):
        if self.rec is not None:
            self.rec.append(("op", (eng, fns), dict(rd=list(rd), wr=list(wr), extra=list(extra), cost=cost,
                                                    tbl=getattr(self, "_tbl", None))))
            return None
        if not isinstance(fns, (list, tuple)):
            fns = [fns]
        waits = self._deps(eng, rd, wr, extra)
        self.cnt[eng] += 1
        tok = (eng, self.cnt[eng])
        self.q[eng].append((waits, list(fns), ("eng", eng)))
        self._commit(tok, rd, wr)
        return tok

    def dma(self, eng, semname, out, in_, rd=(), wr=(), extra=(), part=False):
        if self.rec is not None:
            self.rec.append(("dma", (eng, semname, out, in_), dict(rd=list(rd), wr=list(wr), extra=list(extra), part=part)))
            return None
        if semname not in self.dsem:
            self.dsem[semname] = [self.es.enter_context(self.nc.semaphore("d_" + semname)), 0]
        if part:
            ex = list(extra)
            for b in wr:
                ex.extend(b.r.items())
                ex.extend((k, v) for k, v in b.w.items() if k != semname)
            waits = self._deps(eng, rd, (), ex)
        else:
            waits = self._deps(eng, rd, wr, extra)
        s = self.dsem[semname]
        s[1] += 16
        tok = (semname, s[1])
        self.q[eng].append((waits, [lambda e: e.dma_start(out=out, in_=in_)], ("dma", semname)))
        if part:
            for b in rd:
                _merge(b.r, tok)
            for b in wr:
                _merge(b.w, tok)
                b.r = {}
        else:
            self._commit(tok, rd, wr)
        return tok

    TBL = {AF.Sigmoid: "sig", AF.Sqrt: "sqrt", AF.Exp: "exp", AF.Ln: "exp", AF.Gelu_apprx_tanh: "gelu", AF.Silu: "silu"}

    def act(self, out, in_, func, rd=(), wr=(), extra=(), cost=None, **kw):
        self._tbl = self.TBL.get(func)
        r = self.op("act", lambda e: e.activation(out=out, in_=in_, func=func, **kw), rd, wr, extra, cost)
        self._tbl = None
        return r

    def tt(self, eng, out, in0, in1, op, rd=(), wr=(), extra=(), cost=None):
        return self.op(eng, lambda e: e.tensor_tensor(out=out, in0=in0, in1=in1, op=op), rd, wr, extra, cost)

    def ts(self, eng, out, in0, s1, s2, op0, op1=None, rd=(), wr=(), extra=(), cost=None):
        if op1 is None:
            return self.op(eng, lambda e: e.tensor_scalar(out=out, in0=in0, scalar1=s1, scalar2=None, op0=op0), rd, wr, extra, cost)
        return self.op(eng, lambda e: e.tensor_scalar(out=out, in0=in0, scalar1=s1, scalar2=s2, op0=op0, op1=op1), rd, wr, extra, cost)

    def stt(self, out, in0, scalar, in1, op0, op1, rd=(), wr=(), extra=(), cost=None, **kw):
        return self.op("dve", lambda e: e.scalar_tensor_tensor(out=out, in0=in0, scalar=scalar, in1=in1, op0=op0, op1=op1, **kw), rd, wr, extra, cost)

    def copy(self, eng, out, in_, rd=(), wr=(), extra=()):
        return self.op(eng, lambda e: e.tensor_copy(out=out, in_=in_), rd, wr, extra)

    def recip(self, out, in_, rd=(), wr=(), extra=(), cost=0.2):
        return self.op("dve", lambda e: e.reciprocal(out=out, in_=in_), rd, wr, extra, cost)

    def memset(self, eng, ap, val, rd=(), wr=(), extra=()):
        return self.op(eng, lambda e: e.memset(ap, val), rd, wr, extra)

    def mms(self, specs, rd=(), wr=(), extra=()):
        fns = []
        for (o, l, r, st, sp) in specs:
            fns.append((lambda o, l, r, st, sp: (lambda e: e.matmul(o, lhsT=l, rhs=r, start=st, stop=sp)))(o, l, r, st, sp))
        return self.op("pe", fns, rd, wr, extra)

    def trs(self, specs, ident, rd=(), wr=(), extra=()):
        fns = []
        for (o, i) in specs:
            fns.append((lambda o, i: (lambda e: e.transpose(out=o, in_=i, identity=ident)))(o, i))
        return self.op("pe", fns, rd, wr, extra)

    def run_phase(self, name):
        fin = [(k, s[1]) for k, s in self.dsem.items() if s[1] > 0]
        self.op("sp", lambda e: e.nop(), extra=fin)
        q = self.q
        self.q = {e: [] for e in ENGS}

        def replay(eng, e):
            for waits, fns, inc in q[eng]:
                for (k, v) in waits:
                    e.wait_ge(self._semobj(k), v)
                ins = None
                for f in fns:
                    ins = f(e)
                if inc[0] == "eng":
                    ins.then_inc(self.sem[eng], 1)
                else:
                    ins.then_inc(self.dsem[inc[1]][0], 16)

        with self.nc.Block() as block:
            @block.tensor
            def _(e):
                replay("pe", e)

            @block.scalar
            def _(e):
                replay("act", e)

            @block.vector
            def _(e):
                replay("dve", e)

            @block.gpsimd
            def _(e):
                replay("pool", e)

            @block.sync
            def _(e):
                replay("sp", e)


class NormT:
    def __init__(self, P, sb, ps, src, gd, identd, NSUB, nh):
        self.P = P
        self.src = src
        self.xin = [sb("xin%d" % i, [128, D], F32) for i in range(2)]
        self.hbf = [sb("hbf%d" % i, [128, D], BF16) for i in range(2)]
        self.ssq = sb("ssq", [128, NSUB], F32)
        self.std = sb("std", [128, NSUB], F32)
        self.rstd = sb("rstd", [128, NSUB], F32)
        self.hT = [sb("hT%d" % i, [128, 8, 512], BF16) for i in range(nh)]
        self.tp = [ps("tp%d" % i, [128, 8, 128], BF16) for i in range(2)]
        self.ident = sb("ident", [128, 128], BF16)
        self.gfm = sb("gfm", [128, 8], F32)
        self.b_xin = [Buf(), Buf()]
        self.b_hbf = [Buf(), Buf()]
        self.b_tp = [Buf(), Buf()]
        self.b_hT = [Buf() for _ in range(nh)]
        self.b_stc = [Buf(), Buf()]
        self.b_c = Buf()
        P.dma("sp", "cst", out=self.gfm[:, :], in_=gd, wr=[self.b_c], part=True)
        P.dma("pool", "cstp", out=self.ident[:, :], in_=identd, wr=[self.b_c], part=True)

    def norm(self, g):
        P = self.P
        sl = g % 2
        xin, hbf = self.xin[sl], self.hbf[sl]
        P.dma("sp", "xin%d" % sl, out=xin[:, :], in_=self.src[g * 128:(g + 1) * 128, :], wr=[self.b_xin[sl]])
        P.act(hbf[:, :], xin[:, :], AF.Square, rd=[self.b_xin[sl]], wr=[self.b_hbf[sl]], accum_out=self.ssq[:, g:g + 1])
        bst = [self.b_stc[sl]]
        P.act(self.std[:, g:g + 1], self.ssq[:, g:g + 1], AF.Sqrt, rd=[self.b_hbf[sl], self.b_c], wr=bst, scale=1.0 / D, bias=self.eps[:, 0:1], cost=0.3)
        P.recip(self.rstd[:, g:g + 1], self.std[:, g:g + 1], wr=bst)
        P.ts("dve", hbf[:, :], xin[:, :], self.rstd[:, g:g + 1], None, ALU.mult, rd=[self.b_xin[sl]] + bst, wr=[self.b_hbf[sl]])

    def transp(self, g, hi):
        P = self.P
        sl = g % 2
        s = g % 4
        hbf, tp = self.hbf[sl], self.tp[sl]
        P.trs([(tp[:, k, :], hbf[:, k * 128:(k + 1) * 128]) for k in range(8)], self.ident[:, :],
              rd=[self.b_hbf[sl], self.b_c], wr=[self.b_tp[sl]])
        P.tt("dve", self.hT[hi][:, :, s * 128:(s + 1) * 128], tp[:, :, :],
             self.gfm[:, :].unsqueeze(2).to_broadcast([128, 8, 128]), ALU.mult,
             rd=[self.b_tp[sl], self.b_c], wr=[self.b_hT[hi]])


def load_cast(P, sem, dst, srcd, nk, ncol, wr):
    npiece = (ncol + 2047) // 2048
    while ncol % npiece:
        npiece += 1
    w = ncol // npiece
    for k in range(nk):
        for p in range(npiece):
            P.dma("pool", sem, out=dst[:, k, p * w:(p + 1) * w], in_=srcd[:, k, p * w:(p + 1) * w], wr=wr, part=True)


def ffn_phase(P, S, src, dst, gd, w1d, w3d, w2d, identd, epsd, fgd, ph):
    nc = P.nc
    NT = S // 512
    NSUB = S // 128
    final = fgd is not None
    with ExitStack() as es:
        def sb(name, shape, dt):
            return es.enter_context(nc.sbuf_tensor("%s_%s" % (ph, name), shape, dt))

        def ps(name, shape, dt):
            return es.enter_context(nc.psum_tensor("%s_%s" % (ph, name), shape, dt))

        W1 = sb("W1", [128, 8, DFF], BF16)
        W3 = sb("W3", [128, 8, DFF], BF16)
        W2 = sb("W2", [128, NFF, D], BF16)
        N = NormT(P, sb, ps, src, gd, identd, NSUB, 1)
        N.eps = sb("eps", [128, 1], F32)
        P.dma("sp", "cst", out=N.eps[:, :], in_=epsd, wr=[N.b_c], part=True)
        sil = [sb("sil%d" % i, [128, 512], F32) for i in range(2)]
        aT = sb("aT", [128, NFF, 512], BF16)
        NX = 3 if final else 2
        xres = [sb("xres%d" % i, [128, D], F32) for i in range(NX)]
        if final:
            fg = sb("fg", [128, D], F32)
            junk2 = sb("junk2", [128, D], BF16)
            ssq2 = sb("ssq2", [128, NSUB], F32)
            std2 = sb("std2", [128, NSUB], F32)
            rstd2 = sb("rstd2", [128, NSUB], F32)
            P.dma("sp", "cst", out=fg[:, :], in_=fgd, wr=[N.b_c], part=True)
        pa = [ps("pa%d" % i, [128, 512], F32) for i in range(2)]
        pb = [ps("pb%d" % i, [128, 512], F32) for i in range(2)]
        py = [ps("py%d" % i, [128, 512], F32) for i in range(2)]
        b_w2 = Buf()
        b_pa, b_pb, b_py = [Buf(), Buf()], [Buf(), Buf()], [Buf(), Buf()]
        b_sil = [Buf(), Buf()]
        b_aT = [Buf() for _ in range(NFF)]
        b_xres = [Buf() for _ in range(NX)]
        b_j2 = Buf()

        HW_ = DFF // 2
        b_wg = [Buf(), Buf()]
        for gi in range(2):
            for Wt, wd in ((W1, w1d), (W3, w3d)):
                for k in range(8):
                    P.dma("pool", "w13%d" % gi, out=Wt[:, k, gi * HW_:(gi + 1) * HW_], in_=wd[:, k, gi * HW_:(gi + 1) * HW_],
                          wr=[b_wg[gi]], part=True)
        load_cast(P, "w2", W2, w2d, NFF, D, [b_w2])

        def stage1(i):
            hT = N.hT[0]
            for f in range(NFF):
                sl = f % 2
                bw = b_wg[0] if f < NFF // 2 else b_wg[1]
                P.mms([(pa[sl][:, :], W1[:, k, f * 128:(f + 1) * 128], hT[:, k, :], k == 0, k == 7) for k in range(8)],
                      rd=[N.b_hT[0], bw], wr=[b_pa[sl]])
                P.mms([(pb[sl][:, :], W3[:, k, f * 128:(f + 1) * 128], hT[:, k, :], k == 0, k == 7) for k in range(8)],
                      rd=[N.b_hT[0], bw], wr=[b_pb[sl]])
                P.act(sil[sl][:, :], pa[sl][:, :], AF.Silu, rd=[b_pa[sl]], wr=[b_sil[sl]])
                P.tt("dve", aT[:, f, :], sil[sl][:, :], pb[sl][:, :], ALU.mult, rd=[b_sil[sl], b_pb[sl]], wr=[b_aT[f]])

        def fin_store(g):
            sl = g % NX
            xr = xres[sl]
            if final:
                P.act(junk2[:, :], xr[:, :], AF.Square, rd=[b_xres[sl]], wr=[b_j2], accum_out=ssq2[:, g:g + 1])
                t1 = P.act(std2[:, g:g + 1], ssq2[:, g:g + 1], AF.Sqrt, rd=[b_j2, N.b_c], scale=1.0 / D, bias=N.eps[:, 0:1])
                t2 = P.recip(rstd2[:, g:g + 1], std2[:, g:g + 1], extra=[t1])
                P.stt(xr[:, :], xr[:, :], rstd2[:, g:g + 1], fg[:, :], ALU.mult, ALU.mult, rd=[N.b_c], wr=[b_xres[sl]], extra=[t2])
            P.dma("sp", "st%d" % sl, out=dst[g * 128:(g + 1) * 128, :], in_=xr[:, :], rd=[b_xres[sl]])

        def stage2(i, s):
            g = 4 * i + s
            sl = g % NX
            xr = xres[sl]
            P.dma("sp", "xr%d" % sl, out=xr[:, :], in_=src[g * 128:(g + 1) * 128, :], wr=[b_xres[sl]])
            for h in range(2):
                P.mms([(py[h][:, :], aT[:, f, s * 128:(s + 1) * 128], W2[:, f, h * 512:(h + 1) * 512], f == 0, f == NFF - 1)
                       for f in range(NFF)], rd=b_aT + [b_w2], wr=[b_py[h]])
                P.stt(xr[:, h * 512:(h + 1) * 512], py[h][:, :], 0.5, xr[:, h * 512:(h + 1) * 512], ALU.mult, ALU.add,
                      rd=[b_py[h]], wr=[b_xres[sl]])
            if final:
                if g > 0:
                    fin_store(g - 1)
            else:
                fin_store(g)

        for s in range(4):
            N.norm(s)
            N.transp(s, 0)
        for i in range(NT):
            stage1(i)
            for s in range(4):
                if i + 1 < NT:
                    N.norm(4 * (i + 1) + s)
                stage2(i, s)
                if i + 1 < NT:
                    N.transp(4 * (i + 1) + s, 0)
        if final:
            fin_store(4 * NT - 1)
        P.run_phase(ph)


def b0_phase(P, dr, Wo, wsT, B2c, b_setup, Win, N):
    nc = P.nc
    with ExitStack() as es:
        def sb(name, shape, dt):
            return es.enter_context(nc.sbuf_tensor("B0_" + name, shape, dt))
        wost = sb("wost", [128, 8, D], F32)
        ogfm = sb("ogfm", [128, 8], F32)
        wsf = sb("wsf", [128, 8, 128], F32)
        smk = sb("smk", [128, 128], F32)
        lnb = sb("lnb", [128, 8, 64], F32)
        bsT = sb("bsT", [128, 8], F32)
        rs = sb("rs", [128, 8], F32)
        onesb = sb("onesb", [128, 1], BF16)
        prs = es.enter_context(nc.psum_tensor("B0_prs", [128, 8], F32))
        b_in, b_o, b_ws, b_prs, b_rs = Buf(), Buf(), Buf(), Buf(), Buf()
        load_cast(P, "w13", Win, dr["win"], 8, INC, [Buf()])
        P.dma("sp", "c0", out=wost[:, :, :], in_=dr["wo"], wr=[b_in], part=True)
        for nm, t, ap in (("og", ogfm, None), ("ws", wsf, None), ("smask", smk, None), ("lnb", lnb, None), ("bsT", bsT, None)):
            if nm == "lnb":
                P.dma("sp", "c0", out=t[:, :, :], in_=dr[nm].rearrange("p (h c) -> p h c", c=64), wr=[b_in], part=True)
            elif nm == "ws":
                P.dma("sp", "c0", out=t[:, :, :], in_=dr[nm], wr=[b_in], part=True)
            else:
                P.dma("sp", "c0", out=t[:, :], in_=dr[nm], wr=[b_in], part=True)
        for kc in range(8):
            P.ts("dve", Wo[:, kc, :], wost[:, kc, :], ogfm[:, kc:kc + 1], None, ALU.mult,
                 rd=[b_in], wr=[b_setup])
        P.tt("dve", wsT[:, :, :], wsf[:, :, :], smk[:, :].unsqueeze(1).to_broadcast([128, 8, 128]), ALU.mult,
             rd=[b_in], wr=[b_ws])
        P.memset("dve", onesb[:, :], 1.0, wr=[b_o])
        P.mms([(prs[:, h:h + 1], wsT[:, h, :], onesb[:, 0:1], True, True) for h in range(8)], rd=[b_ws, b_o], wr=[b_prs])
        P.copy("dve", rs[:, :], prs[:, :], rd=[b_prs], wr=[b_rs])
        P.tt("dve", B2c[:, :, :], lnb[:, :, :], rs[:, :].unsqueeze(2).to_broadcast([128, 8, 64]), ALU.mult,
             rd=[b_in, b_rs], wr=[b_setup])
        P.tt("dve", B2c[:, :, :], B2c[:, :, :], bsT[:, :].unsqueeze(2).to_broadcast([128, 8, 64]), ALU.add,
             rd=[b_in], wr=[b_setup])
        for s_ in range(4):
            N.norm(s_)
            N.transp(s_, 0)
        P.run_phase("B0")


def b1_phase(P, S, src, qTd, kTd, vd, ysTd, dr, wsT, B2c, b_setup, Win, N):
    nc = P.nc
    NCH = S // 512
    NSUB = S // 128
    with ExitStack() as es:
        def sb(name, shape, dt):
            return es.enter_context(nc.sbuf_tensor("B1_" + name, shape, dt))

        def ps(name, shape, dt):
            return es.enter_context(nc.psum_tensor("B1_" + name, shape, dt))

        lng = sb("lng", [128, 512], F32)
        P.dma("sp", "cst", out=lng[:, :], in_=dr["lng"], wr=[N.b_c], part=True)
        fb = sb("fb", [8, 1], F32)
        nfb = sb("nfb", [8, 1], F32)
        one8 = sb("one8", [8, 1], F32)
        P.dma("sp", "cst", out=fb[:, :], in_=dr["fb"], wr=[N.b_c], part=True)
        P.ts("dve", nfb[:, :], fb[:, :], -1.0, None, ALU.mult, rd=[N.b_c], wr=[N.b_c])
        P.memset("dve", one8[:, :], 1.0, wr=[N.b_c])
        qst = sb("qst", [128, 4, 512], BF16)
        kst = sb("kst", [128, 4, 512], BF16)
        vst = sb("vst", [128, 4, 4, 192], BF16)
        yst = sb("yst", [128, 4, 512], BF16)
        augq = sb("augq", [8, 6, 512], BF16)
        augk = sb("augk", [8, 6, 512], BF16)
        ef = sb("ef", [8, 512], F32)
        lf = sb("lf", [8, 512], F32)
        Fc = [sb("Fc%d" % i, [8, 512], F32) for i in range(2)]
        tmpU = [sb("tmpU%d" % i, [128, 512], F32) for i in range(3)]
        tmpV = [sb("tmpV%d" % i, [128, 512], F32) for i in range(3)]
        gu = [sb("gu%d" % i, [128, 512], F32) for i in range(3)]
        gv = [sb("gv%d" % i, [128, 512], F32) for i in range(3)]
        nbf = [sb("nbf%d" % i, [128, 8, 64], BF16) for i in range(3)]
        yn = [sb("yn%d" % i, [128, 512], BF16) for i in range(2)]
        stn = ("vsum", "vssq", "mu", "m2", "var", "sdv", "rstdv", "nmr")
        st = {n: sb(n, [128, NSUB], F32) for n in stn}
        gss = sb("gss", [128, NSUB, 8], F32)
        gsd = sb("gsd", [128, NSUB, 8], F32)
        gr = sb("gr", [128, NSUB, 8], F32)
        pA = [ps("pA%d" % i, [128, 512], F32) for i in range(2)]
        pB = [ps("pB%d" % i, [128, 512], F32) for i in range(2)]
        pM = ps("pM", [128, 8, 64], F32)
        pT = ps("pT", [128, 4, 128], BF16)
        b_win = Buf()
        b_pA, b_pB = [Buf(), Buf()], [Buf(), Buf()]
        b_pM, b_pT = Buf(), Buf()
        b_qst, b_kst, b_vst, b_yst, b_augq, b_augk = Buf(), Buf(), Buf(), Buf(), Buf(), Buf()
        b_ef, b_lf = Buf(), Buf()
        b_Fc = [Buf(), Buf()]
        b_tmpU, b_tmpV, b_gu, b_gv = ([Buf() for _ in range(3)] for _ in range(4))
        b_nbf, b_yn, b_stat = [Buf() for _ in range(3)], [Buf(), Buf()], [Buf() for _ in range(3)]

        P.memset("dve", vst[:, :, :, :], 1.0, wr=[b_vst])
        P.memset("dve", augq[:, :, :], 1.0, wr=[b_augq])
        P.memset("dve", augk[:, :, :], 1.0, wr=[b_augk])
        P.memset("dve", Fc[1][:, :], 0.0, wr=[b_Fc[1]])

        def v3(t):
            return t[:, :].rearrange("p (h c) -> p h c", c=64)

        def proj_qkf(c):
            cc = c % 2
            hT = N.hT[cc]
            c0, c1 = c * 512, (c + 1) * 512
            for j in range(8):
                pbj = pB[j % 2]
                P.mms([(pbj[:, :], Win[:, k, 1024 + j * 128:1024 + (j + 1) * 128], hT[:, k, :], k == 0, k == 7) for k in range(8)],
                      rd=[N.b_hT[cc], b_win], wr=[b_pB[j % 2]])
                if j < 4:
                    P.act(qst[:, j, :], pbj[:, :], AF.Copy, rd=[b_pB[j % 2]], wr=[b_qst], scale=0.125)
                else:
                    P.copy("dve", kst[:, j - 4, :], pbj[:, :], rd=[b_pB[j % 2]], wr=[b_kst])
            for h in range(8):
                r0 = (h % 2) * 64
                P.dma("sp", "qd", out=qTd[h, 0:64, c0:c1], in_=qst[r0:r0 + 64, h // 2, :], rd=[b_qst])
                P.dma("sp", "kd", out=kTd[h, 0:64, c0:c1], in_=kst[r0:r0 + 64, h // 2, :], rd=[b_kst])
            P.mms([(pB[0][0:8, :], Win[:, k, 2560:2568], hT[:, k, :], k == 0, k == 7) for k in range(8)],
                  rd=[N.b_hT[cc], b_win], wr=[b_pB[0]])
            P.act(ef[:, :], pB[0][0:8, :], AF.Exp, rd=[b_pB[0], N.b_c], wr=[b_ef], scale=-1.0, bias=nfb[:, 0:1])
            P.act(lf[:, :], ef[:, :], AF.Ln, rd=[b_ef], wr=[b_lf], bias=one8[:, 0:1])
            fcc, fpp = Fc[cc], Fc[1 - cc]
            P.op("dve", lambda e: e.tensor_tensor_scan(out=fcc[:, :], data0=one8[:, 0:1].to_broadcast([8, 512]), data1=lf[:, :],
                                                       initial=fpp[:, 511:512], op0=ALU.mult, op1=ALU.subtract),
                 rd=[b_lf, b_Fc[1 - cc], N.b_c], wr=[b_Fc[cc]])
            P.copy("dve", augq[:, 0, :], fcc[:, :], rd=[b_Fc[cc]], wr=[b_augq])
            P.tt("dve", ef[:, :], fcc[:, :], augq[:, 0, :], ALU.subtract, rd=[b_Fc[cc], b_augq], wr=[b_ef])
            P.copy("dve", augq[:, 1, :], ef[:, :], rd=[b_ef], wr=[b_augq])
            P.tt("dve", lf[:, :], ef[:, :], augq[:, 1, :], ALU.subtract, rd=[b_ef, b_augq], wr=[b_lf])
            P.copy("dve", augq[:, 2, :], lf[:, :], rd=[b_lf], wr=[b_augq])
            P.ts("dve", augk[:, 3:6, :], augq[:, 0:3, :], -1.0, None, ALU.mult, rd=[b_augq], wr=[b_augk])
            P.dma("sp", "qa", out=qTd[:, 64:70, c0:c1], in_=augq[:, :, :], rd=[b_augq])
            P.dma("sp", "ka", out=kTd[:, 64:70, c0:c1], in_=augk[:, :, :], rd=[b_augk])

        def proj_v(c):
            cc = c % 2
            hT = N.hT[cc]
            for s in range(4):
                pa_ = pA[s % 2]
                P.mms([(pa_[:, :], hT[:, k, s * 128:(s + 1) * 128], Win[:, k, 2048:2560], k == 0, k == 7) for k in range(8)],
                      rd=[N.b_hT[cc], b_win], wr=[b_pA[s % 2]])
                pv4 = pa_[:, :].rearrange("p (j hh d) -> p j hh d", hh=2, d=64)
                P.copy("dve", vst[:, s, :, 0:64], pv4[:, :, 0, :], rd=[b_pA[s % 2]], wr=[b_vst])
                P.act(vst[:, s, :, 128:192], pv4[:, :, 1, :], AF.Copy, rd=[b_pA[s % 2]], wr=[b_vst])
            P.dma("sp", "vd", out=vd[:, c * 4:(c + 1) * 4, :],
                  in_=vst[:, :, :, :].rearrange("p s j f -> p s (j f)"), rd=[b_vst])

        def gelu(Xp, b_X, tmp, b_tmp, res, b_res, accum):
            if accum is None:
                P.act(res[:, :], Xp[:, :], AF.Gelu_apprx_tanh, rd=[b_X], wr=[b_res])
            else:
                P.act(res[:, :], Xp[:, :], AF.Gelu_apprx_tanh, rd=[b_X], wr=[b_res, accum[1]], accum_out=accum[0])

        def sgu1(c, s):
            cc = c % 2
            hT = N.hT[cc]
            g = 4 * c + s
            w = g % 3
            gc = slice(g, g + 1)
            P.mms([(pA[0][:, :], hT[:, k, s * 128:(s + 1) * 128], Win[:, k, 0:512], k == 0, k == 7) for k in range(8)],
                  rd=[N.b_hT[cc], b_win], wr=[b_pA[0]])
            P.mms([(pA[1][:, :], hT[:, k, s * 128:(s + 1) * 128], Win[:, k, 512:1024], k == 0, k == 7) for k in range(8)],
                  rd=[N.b_hT[cc], b_win], wr=[b_pA[1]])
            gelu(pA[0], b_pA[0], tmpU[w], b_tmpU[w], gu[w], b_gu[w], None)
            gelu(pA[1], b_pA[1], tmpV[w], b_tmpV[w], gv[w], b_gv[w], (st["vsum"][:, gc], b_stat[w]))

        def sgu1b(c, s):
            g = 4 * c + s
            w = g % 3
            gc = slice(g, g + 1)
            bs = [b_stat[w]]
            P.act(tmpV[w][:, :], gv[w][:, :], AF.Square, rd=[b_gv[w]], wr=[b_tmpV[w]] + bs, accum_out=st["vssq"][:, gc])
            P.ts("dve", st["mu"][:, gc], st["vsum"][:, gc], 1.0 / 512, None, ALU.mult, wr=bs, cost=0.25)
            P.tt("dve", st["m2"][:, gc], st["mu"][:, gc], st["mu"][:, gc], ALU.mult, wr=bs, cost=0.25)
            P.stt(st["var"][:, gc], st["vssq"][:, gc], 1.0 / 512, st["m2"][:, gc], ALU.mult, ALU.subtract, wr=bs, cost=0.25)
            P.act(st["sdv"][:, gc], st["var"][:, gc], AF.Sqrt, rd=[N.b_c], wr=bs, bias=N.eps[:, 0:1], cost=0.25)
            P.recip(st["rstdv"][:, gc], st["sdv"][:, gc], wr=bs)
            P.stt(st["nmr"][:, gc], st["mu"][:, gc], -1.0, st["rstdv"][:, gc], ALU.mult, ALU.mult, wr=bs, cost=0.25)
            P.act(nbf[w][:, :, :], v3(gv[w]), AF.Identity, rd=[b_gv[w]] + bs, wr=[b_nbf[w]],
                  scale=st["rstdv"][:, gc], bias=st["nmr"][:, gc])

        def sgu2(c, s):
            g = 4 * c + s
            w = g % 3
            y = g % 2
            bs = [b_stat[w]]
            P.mms([(pM[:, h, :], wsT[:, h, :], nbf[w][:, h, :], True, True) for h in range(8)],
                  rd=[b_nbf[w], b_setup], wr=[b_pM])
            P.tt("dve", v3(tmpU[w]), pM[:, :, :], v3(lng), ALU.mult, rd=[b_pM, N.b_c], wr=[b_tmpU[w]])
            P.tt("dve", v3(tmpU[w]), v3(tmpU[w]), B2c[:, :, :], ALU.add, rd=[b_setup], wr=[b_tmpU[w]])
            P.tt("dve", tmpU[w][:, :], tmpU[w][:, :], gu[w][:, :], ALU.mult, rd=[b_gu[w]], wr=[b_tmpU[w]])
            P.tt("dve", tmpV[w][:, :], tmpU[w][:, :], tmpU[w][:, :], ALU.mult, rd=[b_tmpU[w]], wr=[b_tmpV[w]])
            tv3 = v3(tmpV[w])
            P.op("dve", lambda e: e.tensor_reduce(out=gss[:, g, :], in_=tv3, axis=AX.X, op=ALU.add), rd=[b_tmpV[w]], wr=bs)
            P.act(gsd[:, g, :], gss[:, g, :], AF.Sqrt, rd=[N.b_c], wr=bs, scale=1.0 / 64, bias=N.eps[:, 0:1], cost=0.25)
            P.recip(gr[:, g, :], gsd[:, g, :], wr=bs)
            P.tt("dve", v3(yn[y]), v3(tmpU[w]), gr[:, g, :].unsqueeze(2).to_broadcast([128, 8, 64]), ALU.mult,
                 rd=[b_tmpU[w]] + bs, wr=[b_yn[y]])

        def sgu3(c, s):
            y = (4 * c + s) % 2
            P.trs([(pT[:, kc, :], yn[y][:, kc * 128:(kc + 1) * 128]) for kc in range(4)], N.ident[:, :],
                  rd=[b_yn[y], N.b_c], wr=[b_pT])
            P.act(yst[:, :, s * 128:(s + 1) * 128], pT[:, :, :], AF.Copy, rd=[b_pT], wr=[b_yst])
            if s == 3:
                P.dma("sp", "yd", out=ysTd.rearrange("(kc p) n -> p kc n", p=128)[:, :, c * 512:(c + 1) * 512],
                      in_=yst[:, :, :], rd=[b_yst])

        NG = 4 * NCH

        def stream_a(t):
            c, s_ = divmod(t, 4)
            if s_ == 0:
                proj_qkf(c)
                proj_v(c)
            if c + 1 < NCH:
                N.norm(4 * (c + 1) + s_)
            sgu1(c, s_)
            if c + 1 < NCH:
                N.transp(4 * (c + 1) + s_, (c + 1) % 2)

        for t in range(NG + 3):
            lists = []
            if t < NG:
                lists.append(P.record(stream_a, t))
            for d_, fn_ in ((1, sgu1b), (2, sgu2), (3, sgu3)):
                if 0 <= t - d_ < NG:
                    lists.append(P.record(fn_, (t - d_) // 4, (t - d_) % 4))
            P.play(lists)
        P.run_phase("B1")


def b2_phase(P, S, x1, x2, qTd, kTd, vd, ysTd, dr, Wo, b_setup):
    nc = P.nc
    NCH = S // 512
    NSUB = S // 128
    with ExitStack() as es:
        def sb(name, shape, dt):
            return es.enter_context(nc.sbuf_tensor("B2_" + name, shape, dt))

        def ps(name, shape, dt):
            return es.enter_context(nc.psum_tensor("B2_" + name, shape, dt))

        Kst = sb("Kst", [128, 8, S], BF16)
        Vst = sb("Vst", [128, NSUB, 768], BF16)
        qc = [sb("qc%d" % i, [128, 8, 512], BF16) for i in range(2)]
        ysg = [sb("ysg%d" % i, [128, 4, 512], BF16) for i in range(2)]
        yfox = [sb("yfox%d" % i, [128, 4, 512], BF16) for i in range(2)]
        PT = [sb("PT%d" % i, [128, 512], BF16) for i in range(4)]
        Ocp = [sb("Ocp%d" % i, [128, 512], BF16) for i in range(2)]
        X = [sb("X%d" % i, [128, 512], BF16) for i in range(2)]
        lnn = sb("lnn", [128, 512], F32)
        rr = sb("rr", [128, 512], F32)
        xres = [sb("xres%d" % i, [128, D], F32) for i in range(2)]
        tri = sb("tri", [128, 128], BF16)
        idb = sb("idb", [128, 128], BF16)
        b2m = [sb("b2m%d" % i, [128, 128], BF16) for i in range(2)]
        pS = [ps("pS%d" % i, [128, 512], F32) for i in range(4)]
        pO = [ps("pO%d" % i, [128, 512], F32) for i in range(2)]
        pN = ps("pN", [128, 512], F32)
        pY = [ps("pY0", [128, 512], F32)] * 2
        b_cst = Buf()
        b_qc, b_ysg, b_yfox = [Buf(), Buf()], [Buf(), Buf()], [Buf(), Buf()]
        b_PT = [Buf() for _ in range(4)]
        b_Ocp, b_X = [Buf(), Buf()], [Buf(), Buf()]
        b_lnn, b_rr = Buf(), Buf()
        b_xres = [Buf(), Buf()]
        b_pS = [Buf() for _ in range(4)]
        b_pO = [Buf(), Buf()]
        b_pN = Buf()
        b_pY = [Buf()] * 2

        P.dma("pool", "c2", out=tri[:, :], in_=dr["nmask"], wr=[b_cst], part=True)
        P.dma("pool", "c2", out=idb[:, :], in_=dr["ident"], wr=[b_cst], part=True)
        P.dma("pool", "c2", out=b2m[0][:, :], in_=dr["b2a"], wr=[b_cst], part=True)
        P.dma("pool", "c2", out=b2m[1][:, :], in_=dr["b2b"], wr=[b_cst], part=True)
        b_Kh = [Buf() for _ in range(8)]
        b_Kl = [Buf() for _ in range(8)]
        nv = max(1, NSUB // 8)
        b_Vp = [Buf() for _ in range(0, NSUB, nv)]
        for i in range(2):
            P.memset("dve", qc[i][64:128, :, :].bitcast(F32), 0.0, wr=[b_qc[i]])
        for h in range(8):
            P.memset("dve", Kst[64:128, h, :].bitcast(F32), 0.0, wr=[b_Kh[h]])
        def vlhsT(h, j):
            o = (h // 2) * 192 + (h % 2) * 64
            return Vst[:, j, o:o + 128]

        def load_chunk(c):
            cc = c % 2
            P.dma("sp", "qc%d" % cc, out=qc[cc][0:70, :, :], in_=qTd[:, :, c * 512:(c + 1) * 512].rearrange("h r n -> r h n"),
                  wr=[b_qc[cc]])
            P.dma("sp", "ys%d" % cc, out=ysg[cc][:, :, :],
                  in_=ysTd.rearrange("(kc p) n -> p kc n", p=128)[:, :, c * 512:(c + 1) * 512], wr=[b_ysg[cc]])

        load_chunk(0)
        P.dma("sp", "vv0", out=Vst[:, 0:nv, :], in_=vd[:, 0:nv, :], wr=[b_Vp[0]])
        for h in range(8):
            P.dma("sp", "kl%d" % h, out=Kst[0:64, h, :], in_=kTd[h, 0:64, :], wr=[b_Kl[h]])
            P.dma("sp", "kh%d" % h, out=Kst[64:70, h, :], in_=kTd[h, 64:70, :], wr=[b_Kh[h]])
        for h in range(1, len(b_Vp)):
            i = h * nv
            P.dma("sp", "vv%d" % h, out=Vst[:, i:i + nv, :], in_=vd[:, i:i + nv, :], wr=[b_Vp[h]])

        def epi_a(u):
            P.copy("dve", Ocp[u % 2][:, :], pO[u % 2][:, :], rd=[b_pO[u % 2]], wr=[b_Ocp[u % 2]])
            P.tt("dve", X[u % 2][:, :], Ocp[u % 2][:, :], Ocp[u % 2][:, :], ALU.mult, rd=[b_Ocp[u % 2]], wr=[b_X[u % 2]])

        def epi_pair(u, c, p):
            cc = c % 2
            P.mms([(pN[:, :], b2m[0][:, :], X[u % 2][:, :], True, False),
                   (pN[:, :], b2m[1][:, :], X[(u + 1) % 2][:, :], False, True)],
                  rd=[b_X[0], b_X[1], b_cst], wr=[b_pN])
            P.act(lnn[:, :], pN[:, :], AF.Ln, rd=[b_pN], wr=[b_lnn])
            P.act(rr[:, :], lnn[:, :], AF.Exp, rd=[b_lnn], wr=[b_rr], scale=-0.5)
            for i, R in enumerate((slice(0, 64), slice(64, 128))):
                P.tt("dve", yfox[cc][R, p, :], Ocp[(u + i) % 2][R, :], rr[R, :], ALU.mult,
                     rd=[b_Ocp[(u + i) % 2], b_rr], wr=[b_yfox[cc]])

        def epi_b(u, c, h):
            cc = c % 2
            R = slice(0, 64) if h % 2 == 0 else slice(64, 128)
            P.mms([(pN[:, :], b2m[h % 2][:, :], X[u % 2][:, :], True, True)], rd=[b_X[u % 2], b_cst], wr=[b_pN])
            P.act(lnn[R, :], pN[R, :], AF.Ln, rd=[b_pN], wr=[b_lnn])
            P.act(rr[R, :], lnn[R, :], AF.Exp, rd=[b_lnn], wr=[b_rr], scale=-0.5)
            P.tt("dve", yfox[cc][R, h // 2, :], Ocp[u % 2][R, :], rr[R, :], ALU.mult, rd=[b_Ocp[u % 2], b_rr], wr=[b_yfox[cc]])

        def w_out_parts(c):
            cc = c % 2
            parts = []
            for s in range(4):
                g = 4 * c + s
                sl = g % 2
                xr = xres[sl]
                for hf in range(2):
                    for q in range(4):
                        def part(s=s, g=g, sl=sl, xr=xr, hf=hf, q=q):
                            if hf == 0 and q == 0:
                                P.dma("sp", "xr%d" % sl, out=xr[:, :], in_=x1[g * 128:(g + 1) * 128, :], wr=[b_xres[sl]])
                            P.mms([(pY[hf][:, :], (ysg[cc] if kc < 4 else yfox[cc])[:, kc % 4, s * 128:(s + 1) * 128],
                                    Wo[:, kc, hf * 512:(hf + 1) * 512], kc == 0, kc == 7) for kc in (2 * q, 2 * q + 1)],
                                  rd=[b_ysg[cc], b_yfox[cc], b_setup], wr=[b_pY[hf]])
                            if q == 3:
                                P.tt("dve", xr[:, hf * 512:(hf + 1) * 512], pY[hf][:, :], xr[:, hf * 512:(hf + 1) * 512],
                                     ALU.add, rd=[b_pY[hf]], wr=[b_xres[sl]])
                                if hf == 1:
                                    P.dma("sp", "st%d" % sl, out=x2[g * 128:(g + 1) * 128, :], in_=xr[:, :], rd=[b_xres[sl]])
                        parts.append(part)
            return parts

        wq = []

        def unit(c, h, u, sl, pend=(), nw=0):
            cc = c % 2
            njt = 4 * c + 4

            def s_mm(j):
                q0 = max(0, j - 4 * c) * 128
                k_ = 2 * sl + j % 2
                mm = [(pS[k_][:, q0:512], Kst[:, h, j * 128:(j + 1) * 128], qc[cc][:, h, q0:512], True, j < 4 * c)]
                if j >= 4 * c:
                    mm.append((pS[k_][:, q0:q0 + 128], idb[:, :], tri[:, :], False, True))
                P.mms(mm, rd=[b_Kl[h], b_Kh[h], b_qc[cc], b_cst], wr=[b_pS[k_]])

            def e_pv(j):
                r = j - 4 * c
                q0 = max(0, r) * 128
                k_ = 2 * sl + j % 2
                P.act(PT[k_][:, q0:512], pS[k_][:, q0:512], AF.Exp, rd=[b_pS[k_]], wr=[b_PT[k_]])
                P.mms([(pO[u % 2][:, q0:512], vlhsT(h, j), PT[k_][:, q0:512], j == 0, j == njt - 1)],
                      rd=[b_PT[k_], b_Vp[j // nv]], wr=[b_pO[u % 2]])

            for j in range(min(2, njt)):
                s_mm(j)
            for j in range(njt):
                e_pv(j)
                if j + 2 < njt:
                    s_mm(j + 2)
                if j == min(2, njt - 1):
                    for f in pend:
                        f()
                if j >= 1 and nw > 0 and wq:
                    wq.pop(0)()
                    nw -= 1
            while nw > 0 and wq:
                wq.pop(0)()
                nw -= 1

        u = 0
        pending = []
        for c in range(NCH):
            for p in range(4):
                nw = 0 if p == 0 else (11 if p < 3 else 99)

                lists = [P.record(unit, c, 2 * p, u, 0, pending, nw), P.record(unit, c, 2 * p + 1, u + 1, 1)]
                P.play(lists)
                epi_a(u)
                epi_a(u + 1)
                pending = [(lambda u_=u, c_=c, p_=p: epi_pair(u_, c_, p_))]
                if p == 0 and c > 0:
                    wq.extend(w_out_parts(c - 1))
                if p == 3 and c + 1 < NCH:
                    load_chunk(c + 1)
                u += 2
        for f in pending:
            f()
        for f in w_out_parts(NCH - 1):
            f()
        P.run_phase("B2")


def build_nc(S=4096, upto="C"):
    nc = bass.Bass("TRN2", target_bir_lowering=False)
    NSUB = S // 128

    def din(name, shape):
        return nc.dram_tensor(name, shape, F32, kind="ExternalInput").ap()

    def dint(name, shape, dt):
        return nc.dram_tensor(name, shape, dt, kind="Internal").ap()

    x = din("x", [S, D])
    f1g = din("f1g", [128, 8])
    f1w1 = din("f1w1", [128, 8, DFF])
    f1w3 = din("f1w3", [128, 8, DFF])
    f1w2 = din("f1w2", [128, NFF, D])
    f2g = din("f2g", [128, 8])
    f2w1 = din("f2w1", [128, 8, DFF])
    f2w3 = din("f2w3", [128, 8, DFF])
    f2w2 = din("f2w2", [128, NFF, D])
    fg = din("fg", [128, D])
    ident = din("ident", [128, 128])
    epsd = din("epsc", [128, 1])
    dr = {"ident": ident, "epsc": epsd}
    for nm, shp in (("mg", [128, 8]), ("win", [128, 8, INC]), ("fb", [8, 1]), ("lng", [128, 512]), ("lnb", [128, 512]),
                    ("ws", [128, 8, 128]), ("bsT", [128, 8]), ("og", [128, 8]), ("wo", [128, 8, D]),
                    ("nmask", [128, 128]), ("smask", [128, 128]), ("b2a", [128, 128]), ("b2b", [128, 128])):
        dr[nm] = din(nm, shp)
    out = nc.dram_tensor("out", [S, D], F32, kind="ExternalOutput").ap()
    x1 = dint("x1s", [S, D], F32)
    x2 = dint("x2s", [S, D], F32)
    qTd = dint("qTd", [8, 70, S], BF16)
    kTd = dint("kTd", [8, 70, S], BF16)
    vd = dint("vd", [128, NSUB, 768], BF16)
    ysTd = dint("ysTd", [512, S], BF16)
    with ExitStack() as es:
        P = Prog(nc, es)
        ffn_phase(P, S, x, out if upto == "A" else x1, f1g, f1w1, f1w3, f1w2, ident, epsd, None, "A")
        if upto in ("B", "C"):
            with ExitStack() as es2:
                Wo = es2.enter_context(nc.sbuf_tensor("Wo", [128, 8, D], BF16))
                wsT = es2.enter_context(nc.sbuf_tensor("wsT", [128, 8, 128], BF16))
                B2c = es2.enter_context(nc.sbuf_tensor("B2c", [128, 8, 64], F32))
                with ExitStack() as es3:
                    Win = es3.enter_context(nc.sbuf_tensor("Win", [128, 8, INC], BF16))

                    def sb3(name, shape, dt):
                        return es3.enter_context(nc.sbuf_tensor("B1_" + name, shape, dt))

                    def ps3(name, shape, dt):
                        return es3.enter_context(nc.psum_tensor("B1_" + name, shape, dt))

                    N = NormT(P, sb3, ps3, x1, dr["mg"], dr["ident"], S // 128, 2)
                    N.eps = sb3("eps", [128, 1], F32)
                    P.dma("sp", "cst", out=N.eps[:, :], in_=dr["epsc"], wr=[N.b_c], part=True)
                    b_setup = Buf()
                    b0_phase(P, dr, Wo, wsT, B2c, b_setup, Win, N)
                    b_setup = Buf()
                    b1_phase(P, S, x1, qTd, kTd, vd, ysTd, dr, wsT, B2c, b_setup, Win, N)
                b_setup = Buf()
                b2_phase(P, S, x1, out if upto == "B" else x2, qTd, kTd, vd, ysTd, dr, Wo, b_setup)
        if upto == "AC":
            ffn_phase(P, S, x1, out, f2g, f2w1, f2w3, f2w2, ident, epsd, fg, "C")
        if upto == "C":
            ffn_phase(P, S, x2, out, f2g, f2w1, f2w3, f2w2, ident, epsd, fg, "C")
    return nc


def fm(v, n):
    return np.ascontiguousarray(np.asarray(v, np.float32).reshape(n, 128).T)


def wl(w, n):
    w = np.asarray(w, np.float32)
    return np.ascontiguousarray(w.reshape(n, 128, w.shape[1]).transpose(1, 0, 2))


def make_inmaps(inp, S, ncores):
    g = lambda k: np.asarray(inp[k], np.float32)
    pi = np.arange(128)
    b2a = np.zeros((128, 128), np.float32)
    b2a[:64, :64] = 1.0 / 64
    b2a[64:, :64] = EPS / 64
    b2b = np.zeros((128, 128), np.float32)
    b2b[64:, 64:] = 1.0 / 64
    b2b[:64, 64:] = EPS / 64
    shared = {
        "f1g": fm(g("ffn1_norm_g")[0], 8), "f1w1": wl(g("ffn1_w1")[0], 8), "f1w3": wl(g("ffn1_w3")[0], 8),
        "f1w2": wl(g("ffn1_w2")[0], NFF),
        "f2g": fm(g("ffn2_norm_g")[0], 8), "f2w1": wl(g("ffn2_w1")[0], 8), "f2w3": wl(g("ffn2_w3")[0], 8),
        "f2w2": wl(g("ffn2_w2")[0], NFF),
        "fg": np.ascontiguousarray(np.broadcast_to(g("final_norm_g")[None, :], (128, D))),
        "ident": np.eye(128, dtype=np.float32),
        "epsc": np.full((128, 1), EPS, np.float32),
        "mg": fm(g("mix_norm_g")[0], 8), "win": wl(g("w_in")[0], 8),
        "fb": np.ascontiguousarray(g("fox_f_bias")[0].reshape(8, 1)),
        "lng": np.ascontiguousarray(np.broadcast_to(g("sgu_ln_g")[0][None, :], (128, 512))),
        "lnb": np.ascontiguousarray(np.broadcast_to(g("sgu_ln_b")[0][None, :], (128, 512))),
        "ws": np.ascontiguousarray(g("sgu_w_s")[0].transpose(2, 0, 1)),
        "bsT": np.ascontiguousarray(g("sgu_b_s")[0].T),
        "og": fm(g("mix_out_g")[0], 8), "wo": wl(g("w_out")[0], 8),
        "nmask": -30000.0 * (pi[:, None] > pi[None, :]).astype(np.float32),
        "smask": ((pi[None, :] // 64) >= (pi[:, None] // 64)).astype(np.float32),
        "b2a": b2a, "b2b": b2b,
    }
    xs = g("x")
    maps = []
    for c in range(ncores):
        m = dict(shared)
        m["x"] = np.ascontiguousarray(xs[c, :S, :])
        maps.append(m)
    return maps


_NC_CACHE = {}


def kernel(**inputs):
    S = 4096
    n = 8
    key = (S, "C")
    if key not in _NC_CACHE:
        _NC_CACHE[key] = build_nc(S, "C")
    nc = _NC_CACHE[key]
    maps = make_inmaps(inputs, S, n)
    res = run_bass_kernel_spmd(nc, maps, core_ids=list(range(n)))
    return np.stack([np.asarray(r["out"], np.float32) for r in res.results], axis=0)
```
